# Optimizing a Trainium2 kernel written in Bass

```python
import jax, jax.numpy as jnp
from jax import lax
import numpy as np

D_MODEL = 1024
BATCH = 8
SEQ = 2048
DEPTH = 2
DEC_BATCH = 128
DEC_SEQ = 8
PAST_LEN = 16384
PAGE_SIZE = 128

D_MIX = D_MODEL
N_MIXERS = 4
GROUP = D_MIX // N_MIXERS
HEADS_PER_MIXER = 4
HEAD_DIM = GROUP // HEADS_PER_MIXER
N_IN_SLICES = 12
CONV_A_WIDTH = 31
CONV_B_WIDTH = 3
CHUNK = 128
POOL_WINDOWS = (2, 4, 8, 16)
POOL_BUF = max(POOL_WINDOWS) - 1
EPS = 1e-6

kernel_name = "hybrid_conv_gmlp_pool_decoder_step"


def _rmsnorm(x, g):
    xf = x.astype(jnp.float32)
    y = xf * lax.rsqrt(jnp.mean(xf * xf, axis=-1, keepdims=True) + EPS)
    return (y * g.astype(jnp.float32)).astype(x.dtype)


def _layernorm(x, g, b):
    xf = x.astype(jnp.float32)
    mu = jnp.mean(xf, axis=-1, keepdims=True)
    var = jnp.mean(jnp.square(xf - mu), axis=-1, keepdims=True)
    y = (xf - mu) * lax.rsqrt(var + EPS)
    return (y * g.astype(jnp.float32) + b.astype(jnp.float32)).astype(x.dtype)


def _depthwise_valid(ext, w):
    return lax.conv_general_dilated(ext, w[:, None, :].astype(ext.dtype), window_strides=(1,), padding='VALID',
                                    dimension_numbers=('NWC', 'WIO', 'NWC'), feature_group_count=ext.shape[-1])


def _chunk_spatial(v, w_s, b_s):
    bsz, L, _ = v.shape
    n_chunks = -(-L // CHUNK)
    vp = jnp.pad(v, ((0, 0), (0, n_chunks * CHUNK - L), (0, 0)))
    vr = vp.reshape(bsz, n_chunks, CHUNK, HEADS_PER_MIXER, HEAD_DIM)
    mask = jnp.tril(jnp.ones((CHUNK, CHUNK), dtype=w_s.dtype))
    s = jnp.einsum('hts,bnshc->bnthc', w_s * mask, vr) + b_s.T[None, None, :, :, None]
    return s.reshape(bsz, n_chunks * CHUNK, GROUP)[:, :L]


def _pool_mixer(xd, buf, pos0, w_p, scale):
    bsz, L, _ = xd.shape
    ext = jnp.concatenate([buf, xd], axis=1)
    ext32 = ext.astype(jnp.float32)
    cs = jnp.pad(jnp.cumsum(ext32, axis=1), ((0, 0), (1, 0), (0, 0)))
    pos = pos0 + jnp.arange(L)
    pooled = []
    for g, w in enumerate(POOL_WINDOWS):
        sl = slice(g * HEAD_DIM, (g + 1) * HEAD_DIM)
        end = cs[:, POOL_BUF + 1:POOL_BUF + 1 + L, sl]
        begin = cs[:, POOL_BUF + 1 - w:POOL_BUF + 1 - w + L, sl]
        cnt = jnp.minimum(pos + 1, w).astype(jnp.float32)[None, :, None]
        pooled.append((end - begin) / cnt)
    pooled = jnp.concatenate(pooled, axis=-1) - ext32[:, POOL_BUF:]
    pr = pooled.reshape(bsz, L, HEADS_PER_MIXER, HEAD_DIM)
    out = jnp.einsum('blgc,gcd->blgd', pr, w_p.astype(jnp.float32)).reshape(bsz, L, GROUP)
    out = out * scale.astype(jnp.float32)
    return out.astype(xd.dtype), ext[:, -POOL_BUF:]


def _layer(x, buf_a, buf_b, buf_d, pos0, pre_g, w_in, ca_w, ca_b, lna_g, lna_b, cb_w,
           lnc_g, lnc_b, sp_w, sp_b, pl_w, pl_s, w_out, post_g):
    h = _rmsnorm(x, pre_g)
    proj = jnp.einsum('bld,de->ble', h, w_in)
    (a_val, a_gate, z_a, b_b, b_c, b_x, z_b, c_u, c_v, z_c, d_x, z_d) = jnp.split(proj, N_IN_SLICES, axis=-1)
    glu = a_val * jax.nn.sigmoid(a_gate)
    ext_a = jnp.concatenate([buf_a, glu], axis=1)
    ya = _depthwise_valid(ext_a, ca_w) + ca_b
    ya = jax.nn.silu(_layernorm(ya, lna_g, lna_b)) * jax.nn.silu(z_a)
    hb = b_c * b_x
    ext_b = jnp.concatenate([buf_b, hb], axis=1)
    yb = b_b * _depthwise_valid(ext_b, cb_w) * jax.nn.silu(z_b)
    vn = _layernorm(c_v, lnc_g, lnc_b)
    yc = c_u * _chunk_spatial(vn, sp_w, sp_b) * jax.nn.silu(z_c)
    yd, new_buf_d = _pool_mixer(d_x, buf_d, pos0, pl_w, pl_s)
    yd = yd * jax.nn.silu(z_d)
    mix = jnp.concatenate([ya, yb, yc, yd], axis=-1)
    out = jnp.einsum('ble,ed->bld', mix, w_out)
    y = x + _rmsnorm(out, post_g)
    return y, ext_a[:, -(CONV_A_WIDTH - 1):], ext_b[:, -(CONV_B_WIDTH - 1):], new_buf_d, vn


def setup_inputs(seed: int = 0) -> dict:
    key = jax.random.key(seed)
    ks = jax.random.split(key, 24)
    nrm = lambda k, s: jax.random.normal(k, s, dtype=jnp.float32)
    return {
        "x_prompt": nrm(ks[0], (BATCH, SEQ, D_MODEL)),
        "x_sample": nrm(ks[1], (DEC_BATCH, DEC_SEQ, D_MODEL)),
        "state_conv_a": 0.5 * nrm(ks[2], (DEPTH, DEC_BATCH, CONV_A_WIDTH - 1, GROUP)),
        "state_conv_b": 0.5 * nrm(ks[3], (DEPTH, DEC_BATCH, CONV_B_WIDTH - 1, GROUP)),
        "state_pool": nrm(ks[4], (DEPTH, DEC_BATCH, POOL_BUF, GROUP)),
        "pre_norm_g": 1.0 + 0.02 * nrm(ks[5], (DEPTH, D_MODEL)),
        "w_in": nrm(ks[6], (DEPTH, D_MODEL, N_IN_SLICES * GROUP)) * D_MODEL ** -0.5,
        "conv_a_w": nrm(ks[7], (DEPTH, CONV_A_WIDTH, GROUP)) * CONV_A_WIDTH ** -0.5,
        "conv_a_b": 0.02 * nrm(ks[8], (DEPTH, GROUP)),
        "ln_a_g": 1.0 + 0.02 * nrm(ks[9], (DEPTH, GROUP)),
        "ln_a_b": 0.02 * nrm(ks[10], (DEPTH, GROUP)),
        "conv_b_w": nrm(ks[11], (DEPTH, CONV_B_WIDTH, GROUP)) * CONV_B_WIDTH ** -0.5,
        "ln_c_g": 1.0 + 0.02 * nrm(ks[12], (DEPTH, GROUP)),
        "ln_c_b": 0.02 * nrm(ks[13], (DEPTH, GROUP)),
        "spatial_w": nrm(ks[14], (DEPTH, HEADS_PER_MIXER, CHUNK, CHUNK)) * CHUNK ** -0.5,
        "spatial_b": 1.0 + 0.02 * nrm(ks[15], (DEPTH, HEADS_PER_MIXER, CHUNK)),
        "pool_w": nrm(ks[16], (DEPTH, HEADS_PER_MIXER, HEAD_DIM, HEAD_DIM)) * HEAD_DIM ** -0.5,
        "pool_scale": 1.0 + 0.02 * nrm(ks[17], (DEPTH, GROUP)),
        "w_out": nrm(ks[18], (DEPTH, D_MIX, D_MODEL)) * D_MIX ** -0.5,
        "post_norm_g": 1.0 + 0.02 * nrm(ks[19], (DEPTH, D_MODEL)),
    }


def reference(x_prompt, x_sample, state_conv_a, state_conv_b, state_pool, pre_norm_g, w_in, conv_a_w, conv_a_b,
              ln_a_g, ln_a_b, conv_b_w, ln_c_g, ln_c_b, spatial_w, spatial_b, pool_w, pool_scale, w_out,
              post_norm_g):
    bp = x_prompt.shape[0]
    dt = x_prompt.dtype
    zero_a = jnp.zeros((bp, CONV_A_WIDTH - 1, GROUP), dt)
    zero_b = jnp.zeros((bp, CONV_B_WIDTH - 1, GROUP), dt)
    zero_d = jnp.zeros((bp, POOL_BUF, GROUP), dt)
    yp, ys = x_prompt, x_sample
    ca_p, ca_s, cb_p, cb_s, pd_p, pd_s, cv_s = [], [], [], [], [], [], []
    for l in range(DEPTH):
        params = (pre_norm_g[l], w_in[l], conv_a_w[l], conv_a_b[l], ln_a_g[l], ln_a_b[l], conv_b_w[l],
                  ln_c_g[l], ln_c_b[l], spatial_w[l], spatial_b[l], pool_w[l], pool_scale[l], w_out[l],
                  post_norm_g[l])
        yp, a_p, b_p, d_p, _ = _layer(yp, zero_a, zero_b, zero_d, 0, *params)
        ys, a_s, b_s, d_s, v_s = _layer(ys, state_conv_a[l], state_conv_b[l], state_pool[l], PAST_LEN, *params)
        ca_p.append(a_p); ca_s.append(a_s); cb_p.append(b_p); cb_s.append(b_s)
        pd_p.append(d_p); pd_s.append(d_s); cv_s.append(v_s)
    return (yp, ys, jnp.stack(ca_p), jnp.stack(ca_s), jnp.stack(cb_p), jnp.stack(cb_s),
            jnp.stack(pd_p), jnp.stack(pd_s), jnp.stack(cv_s))
```

```python
import contextlib
import numpy as np
import concourse.bass as bass
import concourse.mybir as mybir
from concourse.bass_utils import run_bass_kernel_spmd

F32 = mybir.dt.float32
BF16 = mybir.dt.bfloat16
I32 = mybir.dt.int32
RSQ_MAGIC = 1597463007
AF = mybir.ActivationFunctionType
ALU = mybir.AluOpType

D = 1024
G = 256
SEQ = 2048
NSEQ_S = 16
TS = 8
DEPTH = 2
EPS = 1e-6
KA = 31
KB = 3
NBLK = 9
WINS = (2, 4, 8, 16)


class Prog:
    ENG = ("pe", "act", "dve", "pool", "sp")
    CH = 3000
    NDS = 24

    def __init__(self):
        self.ops = {e: [] for e in self.ENG}
        self.cnt = {e: 0 for e in self.ENG}
        self.lastw = {}
        self.readers = {}
        self.known = {e: {} for e in self.ENG}
        self.known_dma = {e: set() for e in self.ENG}
        self.ndma = {"sp": 0, "pool": 0, "act": 0}

    def _collect(self, eng, reads, writes, is_dma):
        raw = set()
        other = set()
        for r in reads:
            if r in self.lastw:
                raw.add(self.lastw[r])
        for w in writes:
            if w in self.lastw:
                other.add(self.lastw[w])
            for rd in self.readers.get(w, ()):
                other.add(rd)
        waits = []
        for tok in sorted(raw | other, key=str):
            if tok[0] == "dma":
                if tok in self.known_dma[eng]:
                    continue
                self.known_dma[eng].add(tok)
                waits.append(tok)
                continue
            if tok[0] == eng and not is_dma:
                if eng == "pe":
                    continue
            if self.known[eng].get(tok[0], 0) >= tok[1]:
                continue
            self.known[eng][tok[0]] = tok[1]
            waits.append(tok)
        return waits

    seq = 0
    last_touch = None

    def _update(self, tok, reads, writes):
        if self.last_touch is None:
            self.last_touch = {}
        self.seq += 1
        for r in reads:
            self.readers.setdefault(r, []).append(tok)
            self.last_touch[r] = self.seq
        for w in writes:
            self.lastw[w] = tok
            self.readers[w] = []
            self.last_touch[w] = self.seq

    @staticmethod
    def _excl(reads, writes):
        ps = tuple(r for r in reads if isinstance(r, tuple) and r and r[0] == "ps")
        if ps:
            writes = tuple(writes) + tuple(p for p in ps if p not in writes)
        return reads, writes

    mute_ops = False
    mute_dma = False

    def op(self, eng, fn, reads=(), writes=()):
        if self.mute_ops:
            return
        reads, writes = self._excl(reads, writes)
        idx = self.cnt[eng] + 1
        self.cnt[eng] = idx
        waits = self._collect(eng, reads, writes, False)
        self.ops[eng].append(("op", waits, fn, idx))
        self._update((eng, idx), reads, writes)

    def dma(self, queue, fn, reads=(), writes=()):
        if self.mute_dma:
            return
        j = self.ndma[queue]
        self.ndma[queue] += 1
        waits = self._collect(queue, reads, writes, True)
        prev = ("dma", queue, j - self.NDS)
        if j >= self.NDS and prev not in self.known_dma[queue]:
            self.known_dma[queue].add(prev)
            waits.append(prev)
        self.ops[queue].append(("dma", waits, fn, j))
        self._update(("dma", queue, j), reads, writes)

    def emit(self, nc, es):
        sem = {}
        for e in self.ENG:
            n = (self.cnt[e] + self.CH - 1) // self.CH + 1
            sem[e] = [es.enter_context(nc.semaphore("s_%s_%d" % (e, i))) for i in range(n)]
        dsem = {q: [es.enter_context(nc.semaphore("s_dma_%s_%d" % (q, i))) for i in range(self.NDS)]
                for q in ("sp", "pool", "act")}

        def do_wait(e, tok):
            if tok[0] == "dma":
                q, j = tok[1], tok[2]
                e.wait_ge(dsem[q][j % self.NDS], 16 * (j // self.NDS + 1))
            else:
                i = tok[1] - 1
                e.wait_ge(sem[tok[0]][i // self.CH], (i % self.CH) + 1)

        def run(name, e):
            for kind, waits, fn, idx in self.ops[name]:
                for w in waits:
                    do_wait(e, w)
                ins = fn(e)
                if kind == "op":
                    ins.then_inc(sem[name][(idx - 1) // self.CH], 1)
                else:
                    ins.then_inc(dsem[name][idx % self.NDS], 16)
            if name == "sp":
                for q in ("sp", "pool", "act"):
                    for j in range(max(0, self.ndma[q] - self.NDS), self.ndma[q]):
                        do_wait(e, ("dma", q, j))

        block = es.enter_context(nc.Block())

        @block.tensor
        def _(e):
            run("pe", e)

        @block.scalar
        def _(e):
            run("act", e)

        @block.vector
        def _(e):
            run("dve", e)

        @block.gpsimd
        def _(e):
            run("pool", e)

        @block.sync
        def _(e):
            run("sp", e)


def _consts():
    c = {}
    c["ident"] = np.eye(128, dtype=np.float32)
    t = np.arange(128)
    c["tril"] = (t[None, :] <= t[:, None]).astype(np.float32)
    q = t // TS
    c["bdmask"] = (q[:, None] == q[None, :]).astype(np.float32)
    e8 = np.zeros((8, 128), np.float32)
    e8[t % TS, t] = 1.0
    c["e8"] = e8
    cd = np.eye(128, dtype=np.float32) - 1.0 / 256
    co = np.full((128, 128), -1.0 / 256, np.float32)
    c["cmat"] = np.stack([cd, co], 1)
    c["ones256"] = np.full((128, 128), 1.0 / 256, np.float32)
    bcur = np.zeros((128, 4, 128), np.float64)
    bprev = np.zeros((128, 4, 128), np.float64)
    bcur0 = np.zeros((128, 4, 128), np.float64)
    bscur = np.zeros((128, 4, 128), np.float64)
    bsbuf = np.zeros((120, 2, 4, 128), np.float64)
    for g, w in enumerate(WINS):
        for tt in range(128):
            for d in range(w):
                s = tt - d
                if s >= 0:
                    bcur[s, g, tt] += 1.0 / w
                    bcur0[s, g, tt] += 1.0 / min(tt + 1, w)
                else:
                    bprev[128 + s, g, tt] += 1.0 / w
            bcur[tt, g, tt] -= 1.0
            bcur0[tt, g, tt] -= 1.0
            qq, tl = tt // TS, tt % TS
            for d in range(w):
                s = tl - d
                if s >= 0:
                    bscur[qq * TS + s, g, tt] += 1.0 / w
                else:
                    r = 15 + s
                    if r >= 0:
                        bsbuf[(qq % 8) * 15 + r, qq // 8, g, tt] += 1.0 / w
            bscur[tt, g, tt] -= 1.0
    c["bcur"] = bcur.astype(np.float32)
    c["bprev"] = bprev.astype(np.float32)
    hi = bcur0.astype(np.float32)
    u = hi.view(np.uint32).astype(np.uint64)
    u = ((u + 0x7FFF + ((u >> 16) & 1)) >> 16) << 16
    hi_b = u.astype(np.uint32).view(np.float32)
    c["bcur0h"] = hi_b
    c["bcur0l"] = (bcur0 - hi_b).astype(np.float32)
    sel = np.zeros((128, 4, 64), np.float32)
    for h in range(4):
        sel[h, h, :] = 1.0
        sel[32 + h, h, :] = 1.0
    c["sel4"] = sel
    c["bscur"] = bscur.astype(np.float32)
    c["bsbuf"] = bsbuf.astype(np.float32)
    return c


CONST_SHAPES = {
    "ident": [128, 128], "tril": [128, 128], "bdmask": [128, 128], "e8": [8, 128],
    "cmat": [128, 2, 128], "ones256": [128, 128], "bcur": [128, 4, 128], "bprev": [128, 4, 128],
    "bcur0h": [128, 4, 128], "bcur0l": [128, 4, 128], "bscur": [128, 4, 128],
    "bsbuf": [120, 2, 4, 128], "sel4": [128, 4, 64],
}

IN_SHAPES = {
    "xp": [SEQ, D], "xs": [128, D],
    "sa": [DEPTH, NSEQ_S * 30, G], "sb": [DEPTH, NSEQ_S * 2, G], "spl": [DEPTH, NSEQ_S * 15, G],
    "pre_g": [DEPTH, D], "w_in": [DEPTH, D, 12 * G], "conv_a_w": [DEPTH, KA, G],
    "conv_a_b": [DEPTH, G], "ln_a_g": [DEPTH, G], "ln_a_b": [DEPTH, G], "conv_b_w": [DEPTH, KB, G],
    "ln_c_g": [DEPTH, G], "ln_c_b": [DEPTH, G], "spatial_w": [DEPTH, 4, 128, 128],
    "spatial_b": [DEPTH, 4, 128], "pool_w": [DEPTH, 4, 64, 64], "pool_scale": [DEPTH, G],
    "w_out": [DEPTH, D, D], "post_g": [DEPTH, D],
}

OUT_SHAPES = {
    "yp": [SEQ, D], "ys": [128, D],
    "ca_p": [DEPTH, 30, G], "ca_s": [DEPTH, NSEQ_S, 30, G],
    "cb_p": [DEPTH, 2, G], "cb_s": [DEPTH, NSEQ_S, 2, G],
    "pd_p": [DEPTH, 15, G], "pd_s": [DEPTH, NSEQ_S, 15, G],
    "cv_s": [DEPTH, 128, G],
}


def build_nc():
    nc = bass.Bass("TRN2", target_bir_lowering=False)
    I = {k: nc.dram_tensor(k, s, F32, kind="ExternalInput").ap() for k, s in IN_SHAPES.items()}
    C = {k: nc.dram_tensor("c_" + k, s, F32, kind="ExternalInput").ap() for k, s in CONST_SHAPES.items()}
    O = {k: nc.dram_tensor(k, s, F32, kind="ExternalOutput").ap() for k, s in OUT_SHAPES.items()}
    y0 = nc.dram_tensor("y0_scratch", [SEQ + 128, D], F32, kind="Internal").ap()
    wbf_in = nc.dram_tensor("wbf_in", [D, 12 * G], BF16, kind="Internal").ap()
    wbf_out = nc.dram_tensor("wbf_out", [D, D], BF16, kind="Internal").ap()

    P = Prog()
    es = contextlib.ExitStack()
    with es:
        def sb(name, shape, dt=F32):
            return es.enter_context(nc.sbuf_tensor(name, shape, dt))

        banks = [es.enter_context(nc.psum_tensor("ps%d" % i, [128, 512], F32)) for i in range(8)]
        banks_bf = [b.bitcast(BF16) for b in banks]
        bank_rr = [0]

        def alloc_bank():
            if P.last_touch is None:
                P.last_touch = {}
            b = min(range(5), key=lambda i: (P.last_touch.get(("ps", i), -1), i))
            P.seq += 1
            P.last_touch[("ps", b)] = P.seq
            bank_rr[0] += 1
            return b

        win = sb("win", [128, 8, 12 * G], BF16)
        wout = sb("wout", [128, 8, D], BF16)
        diagA = sb("diagA", [128, 2, KA, 128], BF16)
        diagB = sb("diagB", [128, 2, KB, 128], BF16)
        xin = [sb("xin%d" % i, [128, D]) for i in range(2)]
        xres = [sb("xres%d" % i, [128, D]) for i in range(2)]
        htm = [sb("htm%d" % i, [128, D], BF16) for i in range(2)]
        hT = [sb("hT%d" % i, [128, 8, 256], BF16) for i in range(2)]
        mix = [sb("mix%d" % i, [128, 8, 256], BF16) for i in range(2)]
        glx = [sb("glx%d" % i, [128, 2, 30 + 256], BF16) for i in range(2)]
        gluext_s = sb("gluext_s", [128, 2, NSEQ_S, 38], BF16)
        hbx = [sb("hbx%d" % i, [128, 2, 2 + 256], BF16) for i in range(2)]
        hbext_s = sb("hbext_s", [128, 2, NSEQ_S, 10], BF16)
        th = sb("th", [128, 2, 256], BF16)
        sza = sb("sza", [128, 2, 256], BF16)
        ya = sb("ya", [128, 2, 256], BF16)
        ycsq = sb("ycsq", [128, 2, 256], BF16)
        varsb = sb("varsb", [128, 256])
        rstdA = sb("rstdA", [128, 256])
        yn = sb("yn", [128, 2, 256])
        s1 = sb("s1", [128, 2, 256], BF16)
        gluf = sb("gluf", [128, 2, 128])
        bx = sb("bx", [128, 2, 256], BF16)
        szb = sb("szb", [128, 2, 256], BF16)
        t1b = sb("t1b", [128, 2, 256], BF16)
        hbf = sb("hbf", [128, 2, 128])
        zc = [sb("zc%d" % i, [128, G]) for i in range(2)]
        vn1 = [sb("vn1_%d" % i, [128, G]) for i in range(2)]
        vnf = sb("vnf", [128, G])
        vnb = [sb("vnb%d" % i, [128, G], BF16) for i in range(2)]
        szc = sb("szc", [128, 2, 256], BF16)
        t1c = sb("t1c", [128, 2, 256], BF16)
        dxT = [sb("dxT%d" % i, [128, G], BF16) for i in range(3)]
        dxf = sb("dxf", [128, G])
        pooled = sb("pooled", [128, 2, 256], BF16)
        szd = sb("szd", [128, 2, 256], BF16)
        ptmp = [sb("ptmp%d" % i, [128, 512]) for i in range(2)]
        sttm = sb("sttm", [128, G])
        stat = sb("stat", [128, 512])
        bnst = [sb("bnst%d" % i, [128, 6]) for i in range(2)]
        ident_f = sb("ident_f", [128, 128])
        ident_b = sb("ident_b", [128, 128], BF16)
        tril = sb("tril", [128, 128])
        bdmask = sb("bdmask", [128, 128])
        e8 = sb("e8", [8, 128], BF16)
        cmat = sb("cmat", [128, 2, 128], BF16)
        ones256 = sb("ones256", [128, 128], BF16)
        ones1 = sb("ones1", [1, 64], BF16)
        bcur = sb("bcur", [128, 4, 128], BF16)
        bprev = sb("bprev", [128, 4, 128], BF16)
        bcur0h = sb("bcur0h", [128, 4, 128], BF16)
        bcur0l = sb("bcur0l", [128, 4, 128], BF16)
        bscur = sb("bscur", [128, 4, 128], BF16)
        bsbuf = sb("bsbuf", [120, 2, 4, 128], BF16)
        wstg_f = [sb("wstg_f%d" % i, [128, 1024]) for i in range(2)]
        wstg_b = sb("wstg_b", [128, 1024], BF16)
        gpre = sb("gpre", [128, D])
        gpost = sb("gpost", [128, D])
        lncg = [sb("lncg%d" % i, [128, G]) for i in range(DEPTH)]
        lncb = [sb("lncb%d" % i, [128, G]) for i in range(DEPTH)]
        pcol = sb("pcol", [128, 4, DEPTH, 2])
        cw_raw = sb("cw_raw", [KA + KB, G])
        cwT = [sb("cwT%d" % i, [128, 2, KA + KB]) for i in range(DEPTH)]
        spw = sb("spw", [128, 4, 128])
        wm = sb("wm", [128, 4, 128], BF16)
        WmT = [sb("WmT%d" % i, [128, 4, 128], BF16) for i in range(DEPTH)]
        WmTs = [sb("WmTs%d" % i, [128, 4, 128], BF16) for i in range(DEPTH)]
        p1sb = sb("p1sb", [8, 128], BF16)
        bsf = sb("bsf", [36, 128])
        bshf = sb("bshf", [36, 128])
        bsh2 = sb("bsh2", [36, 128], BF16)
        bs8 = [sb("bs8_%d" % i, [128, 128], BF16) for i in range(DEPTH)]
        sel4 = sb("sel4", [128, 4, 64], BF16)
        wpbd = [sb("wpbd%d" % i, [128, 2, 128], BF16) for i in range(DEPTH)]
        sa_raw = sb("sa_raw", [120, G])
        sa_bf = sb("sa_bf", [120, G], BF16)
        sb_raw = sb("sb_raw", [32, G])
        sb_bf = sb("sb_bf", [32, G], BF16)
        spool = sb("spool", [120, 2, G], BF16)

        def ld(queue, dst, src, wname, rname=None):
            P.dma(queue, lambda e, dst=dst, src=src: e.dma_start(out=dst, in_=src),
                  reads=(rname,) if rname else (), writes=(wname,))

        def load_consts(part):
            if part == "early":
                P.op("pool", lambda e: e.memset(ones1[:], 1.0), writes=("ones1",))
                P.op("act", lambda e: e.activation(out=bnst[0][0:1, 0:6], in_=ones1[0:1, 0:6], func=AF.Silu),
                     reads=("ones1",), writes=(("bnst", 0),))
                ld("pool", ident_b[:], C["ident"], "ident_b")
                ld("pool", e8[:], C["e8"], "e8")
                ld("pool", cmat[:], C["cmat"], "cmat")
                ld("pool", ones256[:], C["ones256"], "ones256")
                return
            if part == "mid":
                ld("pool", sel4[:], C["sel4"], "sel4")
                ld("pool", bcur0h[:], C["bcur0h"], "bcur0h")
                ld("pool", bcur0l[:], C["bcur0l"], "bcur0l")
                ld("pool", bcur[:], C["bcur"], "bcur")
                ld("pool", bprev[:], C["bprev"], "bprev")
                P.op("pool", lambda e: e.memset(wpbd[0][:], 0.0), writes=(("wpbd", 0),))
                for g in range(4):
                    r0 = (g % 2) * 64
                    ld("pool", wpbd[0][r0:r0 + 64, g // 2, r0:r0 + 64], I["pool_w"][0, g], ("wpbd", 0, g), ("wpbd", 0))
                return
            ld("sp", ident_f[:], C["ident"], "ident_f")
            ld("sp", tril[:], C["tril"], "tril")
            ld("sp", bdmask[:], C["bdmask"], "bdmask")
            for j, nm in enumerate(("conv_a_b", "ln_a_g", "ln_a_b", "pool_scale")):
                ld("sp", pcol[:, j, :, :], I[nm].rearrange("l (c p) -> p l c", p=128), ("pcol", j))
            ld("pool", bscur[:], C["bscur"], "bscur")
            ld("pool", bsbuf[:], C["bsbuf"], "bsbuf")

        stat_col = [0]

        def new_col(n=1):
            c0 = stat_col[0]
            stat_col[0] += n
            assert stat_col[0] <= 508
            return c0

        prep_sel = [None]
        wm_done = [False]

        def load_layer_params(l, part):
            if part == "win":
                wv = I["w_in"][l].rearrange("(k p) e -> p k e", p=128)
                for s in (1, 0, 2, 6, 5, 4, 3, 8, 10, 9, 7, 11):
                    ld("pool", win[:, :, s * G:(s + 1) * G], wv[:, :, s * G:(s + 1) * G], ("win", s))
                return
            if part == "wout":
                if l == 0:
                    wo = I["w_out"][l].rearrange("(k p) e -> p k e", p=128)
                    for k in range(0, 8, 2):
                        ld("pool", wout[:, k:k + 2, :], wo[:, k:k + 2, :], ("wout", k // 2))
                else:
                    wo = wbf_out.rearrange("(k p) e -> p k e", p=128)
                    for k in range(0, 8, 2):
                        P.dma("act", lambda e, k=k: e.dma_start(out=wout[:, k:k + 2, :], in_=wo[:, k:k + 2, :]),
                              reads=tuple(("wbf", pc) for pc in range(24, 32)), writes=(("wout", k // 2),))
                ld("sp", gpost[:], I["post_g"][l:l + 1, :].partition_broadcast(128).rearrange("p o d -> p (o d)"), "gpost")
                return
            if part == "gpre":
                ld("sp", gpre[:], I["pre_g"][l:l + 1, :].partition_broadcast(128).rearrange("p o d -> p (o d)"), "gpre")
                return
            if part == "prep":
                sel = prep_sel[0]
                lq = "sp"
                if sel != "rest":
                    ld(lq, lncg[l][:], I["ln_c_g"][l:l + 1, :].partition_broadcast(128).rearrange("p o d -> p (o d)"), ("lncg", l))
                    ld(lq, lncb[l][:], I["ln_c_b"][l:l + 1, :].partition_broadcast(128).rearrange("p o d -> p (o d)"), ("lncb", l))
                if sel in (None, "cw"):
                    ld(lq, cw_raw[0:KA, :], I["conv_a_w"][l], "cw_raw")
                    ld(lq, cw_raw[KA:KA + KB, :], I["conv_b_w"][l], "cw_raw", "cw_raw")
                    for c in range(2):
                        b = alloc_bank()
                        P.op("pe", lambda e, b=b, c=c: e.transpose(banks[b][:, 0:KA + KB], cw_raw[0:KA + KB, c * 128:(c + 1) * 128],
                                                                    ident_f[0:KA + KB, 0:KA + KB]),
                             reads=("cw_raw", "ident_f"), writes=(("ps", b),))
                        P.op("dve", lambda e, b=b, c=c: e.tensor_scalar(out=cwT[l][:, c, 0:KA], in0=banks[b][:, 0:KA], scalar1=0.5,
                                                                         scalar2=None, op0=ALU.mult),
                             reads=(("ps", b),), writes=(("cwT", l, c, 0),))
                        P.op("dve", lambda e, b=b, c=c: e.tensor_copy(out=cwT[l][:, c, KA:KA + KB], in_=banks[b][:, KA:KA + KB]),
                             reads=(("ps", b),), writes=(("cwT", l, c, 1),))
                if sel in (None, "rest"):
                    ld(lq, spw[:], I["spatial_w"][l].rearrange("h t s -> t h s"), "spw")
                    if not wm_done[0]:
                        P.op("dve", lambda e: e.tensor_tensor(out=wm[:], in0=spw[:], in1=tril[:, :].unsqueeze(1).to_broadcast([128, 4, 128]),
                                                              op=ALU.mult), reads=("spw", "tril"), writes=("wm",))
                    wm_done[0] = False
                    for h in range(4):
                        b = alloc_bank()
                        P.op("pe", lambda e, b=b, h=h: e.transpose(banks_bf[b][:, 0:128], wm[:, h, :], ident_b[:]),
                             reads=("wm", "ident_b"), writes=(("ps", b),))
                        P.op("dve", lambda e, b=b, h=h: e.tensor_copy(out=WmT[l][:, h, :], in_=banks_bf[b][:, 0:128]),
                             reads=(("ps", b),), writes=(("WmT", l),))
                        b1 = alloc_bank()
                        P.op("pe", lambda e, b1=b1, h=h: e.matmul(banks[b1][0:8, 0:128], lhsT=wm[0:8, h, 0:8], rhs=e8[:, :],
                                                                   start=True, stop=True),
                             reads=("wm", "e8"), writes=(("ps", b1),))
                        P.op("dve", lambda e, b1=b1: e.tensor_copy(out=p1sb[:], in_=banks[b1][0:8, 0:128]),
                             reads=(("ps", b1),), writes=("p1sb",))
                        b2 = alloc_bank()
                        P.op("pe", lambda e, b2=b2: e.matmul(banks[b2][:, 0:128], lhsT=e8[:, :], rhs=p1sb[:, :], start=True, stop=True),
                             reads=("p1sb", "e8"), writes=(("ps", b2),))
                        P.op("dve", lambda e, b2=b2, h=h: e.tensor_tensor(out=WmTs[l][:, h, :], in0=banks[b2][:, 0:128], in1=bdmask[:],
                                                                          op=ALU.mult),
                             reads=(("ps", b2), "bdmask"), writes=(("WmTs", l),))
                    ld(lq, bsf[0:4, :], I["spatial_b"][l], "bsf")
                    ld(lq, bsf[32:36, :], I["spatial_b"][l], "bsf", "bsf")
                    P.op("pool", lambda e: e.memset(bs8[l][:], 0.0), writes=(("bs8", l),))
                    P.op("dve", lambda e: e.tensor_copy(out=bs8[l][0:4, :], in_=bsf[0:4, :]), reads=("bsf",), writes=(("bs8", l),))
                    P.op("dve", lambda e: e.tensor_copy(out=bsh2[32:36, :], in_=bsf[32:36, :]), reads=("bsf",), writes=("bsh2",))
                    P.op("dve", lambda e: e.tensor_copy(out=bshf[32:36, :], in_=bsh2[32:36, :]), reads=("bsh2",), writes=("bshf",))
                    P.op("dve", lambda e: e.tensor_tensor(out=bs8[l][32:36, :], in0=bsf[32:36, :], in1=bshf[32:36, :], op=ALU.subtract),
                         reads=("bsf", "bshf"), writes=(("bs8", l),))
                    if not P.mute_ops and l > 0:
                        dm = P.mute_dma
                        P.mute_dma = False
                        P.op("pool", lambda e: e.memset(wpbd[l][:], 0.0), writes=(("wpbd", l),))
                        for g in range(4):
                            r0 = (g % 2) * 64
                            ld("pool", wpbd[l][r0:r0 + 64, g // 2, r0:r0 + 64], I["pool_w"][l, g], ("wpbd", l, g), ("wpbd", l))
                        P.mute_dma = dm
                    ld("sp", O["ca_s"][l, :, 0:22, :], I["sa"][l].rearrange("(q r) c -> q r c", r=30)[:, 8:30, :], ("o", "ca_s0", l))
                    ld("sp", O["pd_s"][l, :, 0:7, :], I["spl"][l].rearrange("(q r) c -> q r c", r=15)[:, 8:15, :], ("o", "pd_s0", l))
                return
            if part == "diag":
                for c in range(2):
                    P.op("dve", lambda e, c=c: e.tensor_tensor(
                        out=diagA[:, c, :, :], in0=ident_b[:, :].unsqueeze(1).to_broadcast([128, KA, 128]),
                        in1=cwT[l][:, c, 0:KA].unsqueeze(2).to_broadcast([128, KA, 128]), op=ALU.mult),
                         reads=("ident_b", ("cwT", l, c, 0)), writes=("diagA",))
                    P.op("pool", lambda e, c=c: e.tensor_tensor(
                        out=diagB[:, c, :, :], in0=ident_b[:, :].unsqueeze(1).to_broadcast([128, KB, 128]),
                        in1=cwT[l][:, c, KA:KA + KB].unsqueeze(2).to_broadcast([128, KB, 128]), op=ALU.mult),
                         reads=("ident_b", ("cwT", l, c, 1)), writes=("diagB",))
                return
            assert part[0] == "state"
            _, j, ph = part
            if j < 4:
                if ph == "a":
                    ld("act", sa_raw[:], I["sa"][l, j * 120:(j + 1) * 120, :], "sa_raw")
                    P.op("pool", lambda e: e.tensor_copy(out=sa_bf[:], in_=sa_raw[:]), reads=("sa_raw",), writes=("sa_bf",))
                else:
                    for c in range(2):
                        b = alloc_bank()
                        P.op("pe", lambda e, b=b, c=c: e.transpose(banks_bf[b][:, 0:120], sa_bf[:, c * 128:(c + 1) * 128],
                                                                    ident_b[0:120, 0:120]),
                             reads=("sa_bf", "ident_b"), writes=(("ps", b),))
                        P.op("act", lambda e, b=b, c=c, j=j: e.activation(
                            out=gluext_s[:, c, 4 * j:4 * j + 4, 0:30],
                            in_=banks_bf[b][:, 0:120].rearrange("p (q r) -> p q r", r=30), func=AF.Copy, scale=2.0),
                             reads=(("ps", b),), writes=("gluext_s_st",))
                return
            if ph == "a":
                ld("act", sb_raw[:], I["sb"][l], "sb_raw")
                P.op("pool", lambda e: e.tensor_copy(out=sb_bf[:], in_=sb_raw[:]), reads=("sb_raw",), writes=("sb_bf",))
                ld("pool", spool[:], I["spl"][l].rearrange("(h p) c -> p h c", p=120), "spool")
            else:
                for c in range(2):
                    b = alloc_bank()
                    P.op("pe", lambda e, b=b, c=c: e.transpose(banks_bf[b][:, 0:32], sb_bf[:, c * 128:(c + 1) * 128],
                                                                ident_b[0:32, 0:32]),
                         reads=("sb_bf", "ident_b"), writes=(("ps", b),))
                    P.op("act", lambda e, b=b, c=c: e.activation(
                        out=hbext_s[:, c, :, 0:2], in_=banks_bf[b][:, 0:32].rearrange("p (q r) -> p q r", r=2), func=AF.Copy),
                         reads=(("ps", b),), writes=("hbext_s_st",))

        def blk_info(b):
            if b < 8:
                return dict(tiles=[2 * b, 2 * b + 1], nb=256, t0=256 * b, sample=False)
            return dict(tiles=[16], nb=128, t0=0, sample=True)

        def src_rows(l, ti):
            if l == 0:
                return I["xp"][ti * 128:(ti + 1) * 128, :] if ti < 16 else I["xs"][:, :]
            return y0[ti * 128:(ti + 1) * 128, :]

        def dst_rows(l, ti):
            if l == 0:
                return y0[ti * 128:(ti + 1) * 128, :]
            return O["yp"][ti * 128:(ti + 1) * 128, :] if ti < 16 else O["ys"][:, :]

        def rsqrt_chain(v, y, t0, t1, rv, ry, rt, newton_eng="dve"):
            P.op("dve", lambda e: e.tensor_single_scalar(out=t0.bitcast(I32), in_=v.bitcast(I32), scalar=1,
                                                         op=ALU.logical_shift_right),
                 reads=(rv,), writes=(rt,))
            P.op("dve", lambda e: e.tensor_scalar(out=y.bitcast(I32), in0=t0.bitcast(I32), scalar1=-1, scalar2=RSQ_MAGIC,
                                                  op0=ALU.mult, op1=ALU.add),
                 reads=(rt,), writes=(ry,))
            for _ in range(2):
                if newton_eng == "dve":
                    P.op("dve", lambda e: e.tensor_tensor(out=t0, in0=y, in1=y, op=ALU.mult), reads=(ry,), writes=(rt,))
                    P.op("dve", lambda e: e.scalar_tensor_tensor(out=t1, in0=t0, scalar=-0.5, in1=v, op0=ALU.mult, op1=ALU.mult),
                         reads=(rt, rv), writes=(rt,))
                    P.op("dve", lambda e: e.scalar_tensor_tensor(out=y, in0=t1, scalar=1.5, in1=y, op0=ALU.add, op1=ALU.mult),
                         reads=(rt, ry), writes=(ry,))
                else:
                    P.op("pool", lambda e: e.tensor_tensor(out=t0, in0=y, in1=y, op=ALU.mult), reads=(ry,), writes=(rt,))
                    P.op("pool", lambda e: e.tensor_tensor(out=t1, in0=t0, in1=v, op=ALU.mult), reads=(rt, rv), writes=(rt,))
                    P.op("pool", lambda e: e.tensor_scalar(out=t1, in0=t1, scalar1=-0.5, scalar2=1.5, op0=ALU.mult, op1=ALU.add),
                         reads=(rt,), writes=(rt,))
                    P.op("pool", lambda e: e.tensor_tensor(out=y, in0=y, in1=t1, op=ALU.mult), reads=(rt, ry), writes=(ry,))

        def rstd_from(col_sum, col_tmp, col_out, scale):
            P.op("dve", lambda e: e.tensor_scalar(out=stat[:, col_tmp:col_tmp + 1], in0=stat[:, col_sum:col_sum + 1],
                                                  scalar1=scale, scalar2=EPS, op0=ALU.mult, op1=ALU.add),
                 reads=(("stat", col_sum),), writes=(("stat", col_tmp),))
            rsqrt_chain(stat[:, col_tmp:col_tmp + 1], stat[:, col_out:col_out + 1], stat[:, 508:509], stat[:, 509:510],
                        ("stat", col_tmp), ("stat", col_out), "stat_tmp")

        xin_rr = [0]
        xres_rr = [0]

        pre_slots = {}

        def pre_load(l, b):
            info = blk_info(b)
            for tl, ti in enumerate(info["tiles"]):
                slot = xin_rr[0] % 2
                xin_rr[0] += 1
                pre_slots[(l, b, tl)] = slot
                rd = (("y0", ti),) if l == 1 else ()
                P.dma("act", lambda e, slot=slot, ti=ti: e.dma_start(out=xin[slot][:], in_=src_rows(l, ti)),
                      reads=rd, writes=(("xin", slot),))

        def pre_a(l, b):
            info = blk_info(b)
            for tl, ti in enumerate(info["tiles"]):
                slot = pre_slots[(l, b, tl)]
                c = new_col(3)
                P.op("act", lambda e, slot=slot, c=c: e.activation(out=htm[slot][:], in_=xin[slot][:], func=AF.Square,
                                                                   accum_out=stat[:, c:c + 1]),
                     reads=(("xin", slot),), writes=(("htm", slot), ("stat", c)))
                rstd_from(c, c + 1, c + 2, 1.0 / D)
                P.op("dve", lambda e, slot=slot, c=c: e.scalar_tensor_tensor(
                    out=htm[slot][:], in0=xin[slot][:], scalar=stat[:, c + 2:c + 3], in1=gpre[:],
                    op0=ALU.mult, op1=ALU.mult),
                     reads=(("xin", slot), ("stat", c + 2), "gpre"), writes=(("htm", slot),))

        def pre_b(l, b):
            info = blk_info(b)
            par = (l * NBLK + b) % 2
            for tl, ti in enumerate(info["tiles"]):
                slot = pre_slots[(l, b, tl)]
                bk = alloc_bank()

                def tr(e, slot=slot, bk=bk):
                    ins = None
                    for j in range(8):
                        ins = e.transpose(banks_bf[bk][:, j * 128:(j + 1) * 128], htm[slot][:, j * 128:(j + 1) * 128], ident_b[:])
                    return ins
                P.op("pe", tr, reads=(("htm", slot), "ident_b"), writes=(("ps", bk),))
                P.op("act", lambda e, bk=bk, tl=tl, par=par: e.activation(
                    out=hT[par][:, :, tl * 128:(tl + 1) * 128],
                    in_=banks_bf[bk][:, 0:1024].rearrange("p (j t) -> p j t", j=8), func=AF.Copy),
                     reads=(("ps", bk),), writes=(("hT", par, tl),))

        def wc_src(l, pc):
            if pc < 24:
                k, j = pc // 3, pc % 3
                return I["w_in"][l, k * 128:(k + 1) * 128, j * 1024:(j + 1) * 1024]
            k = pc - 24
            return I["w_out"][l, k * 128:(k + 1) * 128, :]

        def wc_dst(pc):
            if pc < 24:
                k, j = pc // 3, pc % 3
                return wbf_in[k * 128:(k + 1) * 128, j * 1024:(j + 1) * 1024]
            k = pc - 24
            return wbf_out[k * 128:(k + 1) * 128, :]

        wc_state = {"tick": 0}

        def wc_tick(l):
            k = wc_state["tick"]
            wc_state["tick"] = k + 1
            pc = k - 2
            if 0 <= pc < 32:
                buf = pc % 2
                P.op("act", lambda e, buf=buf: e.activation(out=wstg_b[:], in_=wstg_f[buf][:], func=AF.Copy),
                     reads=(("wstg_f", buf),), writes=("wstg_b",))
                P.dma("sp", lambda e, pc=pc: e.dma_start(out=wc_dst(pc), in_=wstg_b[:]),
                      reads=("wstg_b",), writes=(("wbf", pc),))
            if k < 32:
                buf = k % 2
                P.dma("act", lambda e, k=k, buf=buf: e.dma_start(out=wstg_f[buf][:], in_=wc_src(l, k)),
                      writes=(("wstg_f", buf),))

        def reload_slab(l, s):
            wv = wbf_in.rearrange("(k p) e -> p k e", p=128)
            if s == 2:
                lo, hi = 0, 3
            elif s == 3:
                lo, hi = 3, 7
            elif s == 11:
                lo, hi = 7, 12
            else:
                return
            P.dma("act", lambda e: e.dma_start(out=win[:, :, lo * G:hi * G], in_=wv[:, :, lo * G:hi * G]),
                  reads=tuple(("wbf", pc) for pc in range(24)), writes=tuple(("win", j) for j in range(lo, hi)))

        def proj_cm(l, b, s):
            info = blk_info(b)
            par, nb = (l * NBLK + b) % 2, info["nb"]
            bk = alloc_bank()

            def f(e):
                ins = None
                for c in range(2):
                    for k in range(8):
                        ins = e.matmul(banks[bk][:, c * 256:c * 256 + nb], lhsT=win[:, k, s * G + c * 128:s * G + (c + 1) * 128],
                                       rhs=hT[par][:, k, 0:nb], start=(k == 0), stop=(k == 7))
                return ins
            P.op("pe", f, reads=(("win", s),) + tuple(("hT", par, tl) for tl in range(len(info["tiles"]))),
                 writes=(("ps", bk),))
            if b == NBLK - 1 and l + 1 < DEPTH:
                reload_slab(l + 1, s)
            return bk

        def psv(bk, nb):
            return banks[bk][:, :].rearrange("p (c n) -> p c n", c=2)[:, :, 0:nb]

        def stage_proj(l, b, hook_out=None, hook_pre_b=None, hook_early=None, hook_conv_done=None, reload_next=False):
            info = blk_info(b)
            par, nb, t0, smp = (l * NBLK + b) % 2, info["nb"], info["t0"], info["sample"]
            ntl = len(info["tiles"])
            hTr = tuple(("hT", par, tl) for tl in range(ntl))
            state_tile = smp or b == 7
            sl = slice(128, 256) if b == 7 else slice(0, 128)

            def gl_dst():
                if smp:
                    return gluext_s[:, :, :, 30:38]
                return glx[par][:, :, 30:30 + nb]

            def hb_dst():
                if smp:
                    return hbext_s[:, :, :, 2:10]
                return hbx[par][:, :, 2:2 + nb]

            def shp(ap):
                return ap.rearrange("p c (q t) -> p c q t", t=TS) if smp else ap

            if not smp:
                if b == 0:
                    P.op("pool", lambda e: e.memset(glx[par][:, :, 0:30], 0.0), writes=(("glx", par),))
                    P.op("pool", lambda e: e.memset(hbx[par][:, :, 0:2], 0.0), writes=(("hbx", par),))
                else:
                    P.op("pool", lambda e: e.tensor_copy(out=glx[par][:, :, 0:30], in_=glx[1 - par][:, :, 256:286]),
                         reads=(("glx", 1 - par),), writes=(("glx", par),))
                    P.op("pool", lambda e: e.tensor_copy(out=hbx[par][:, :, 0:2], in_=hbx[1 - par][:, :, 256:258]),
                         reads=(("hbx", 1 - par),), writes=(("hbx", par),))
            bk_gate = proj_cm(l, b, 1)
            P.op("act", lambda e: e.activation(out=th[:, :, 0:nb], in_=psv(bk_gate, nb), func=AF.Tanh, scale=0.5),
                 reads=(("ps", bk_gate),), writes=("th",))
            bk_val = proj_cm(l, b, 0)
            P.op("dve", lambda e: e.scalar_tensor_tensor(out=gl_dst(), in0=shp(th[:, :, 0:nb]), scalar=1.0,
                                                         in1=shp(psv(bk_val, nb)), op0=ALU.add, op1=ALU.mult),
                 reads=("th", ("ps", bk_val)), writes=(("glx_s",) if smp else ("glx", par),))
            if state_tile:
                P.op("dve", lambda e: e.scalar_tensor_tensor(out=gluf[:], in0=th[:, :, sl], scalar=1.0,
                                                             in1=psv(bk_val, nb)[:, :, sl], op0=ALU.add, op1=ALU.mult),
                     reads=("th", ("ps", bk_val)), writes=("gluf",))
            bk_za = proj_cm(l, b, 2)
            P.op("act", lambda e: e.activation(out=sza[:, :, 0:nb], in_=psv(bk_za, nb), func=AF.Silu),
                 reads=(("ps", bk_za),), writes=("sza",))
            bk_zb = proj_cm(l, b, 6)
            P.op("act", lambda e: e.activation(out=szb[:, :, 0:nb], in_=psv(bk_zb, nb), func=AF.Silu),
                 reads=(("ps", bk_zb),), writes=("szb",))
            bk_bx = proj_cm(l, b, 5)
            P.op("act", lambda e: e.activation(out=bx[:, :, 0:nb], in_=psv(bk_bx, nb), func=AF.Copy),
                 reads=(("ps", bk_bx),), writes=("bx",))
            bk_bc = proj_cm(l, b, 4)
            P.op("dve", lambda e: e.tensor_tensor(out=hb_dst(), in0=shp(psv(bk_bc, nb)), in1=shp(bx[:, :, 0:nb]), op=ALU.mult),
                 reads=("bx", ("ps", bk_bc)), writes=(("hbx_s",) if smp else ("hbx", par),))
            if state_tile:
                P.op("dve", lambda e: e.tensor_tensor(out=hbf[:], in0=psv(bk_bc, nb)[:, :, sl], in1=bx[:, :, sl], op=ALU.mult),
                     reads=("bx", ("ps", bk_bc)), writes=("hbf",))
            bk_bb = proj_cm(l, b, 3)
            P.op("dve", lambda e: e.tensor_tensor(out=t1b[:, :, 0:nb], in0=psv(bk_bb, nb), in1=szb[:, :, 0:nb], op=ALU.mult),
                 reads=("szb", ("ps", bk_bb)), writes=("t1b",))
            if hook_early is not None:
                hook_early()
            bk_ca = alloc_bank()

            def convA(e):
                ins = None
                for c in range(2):
                    for k in range(KA):
                        if smp:
                            rhs = gluext_s[:, c, :, k:k + TS]
                            out = banks[bk_ca][:, c * 256:c * 256 + nb].rearrange("p (q t) -> p q t", t=TS)
                        else:
                            rhs = glx[par][:, c, k:k + nb]
                            out = banks[bk_ca][:, c * 256:c * 256 + nb]
                        ins = e.matmul(out, lhsT=diagA[:, c, k, :], rhs=rhs, start=(k == 0), stop=(k == KA - 1))
                return ins
            glu_reads = (("glx_s",), "gluext_s_st", "diagA") if smp else (("glx", par), "diagA")
            P.op("pe", convA, reads=glu_reads, writes=(("ps", bk_ca),))

            def ya_evac(e):
                ins = None
                for c in range(2):
                    ins = e.activation(out=ya[:, c, 0:nb], in_=banks[bk_ca][:, c * 256:c * 256 + nb], func=AF.Identity,
                                       bias=pcol[:, 0, l, c:c + 1])
                return ins
            P.op("act", ya_evac, reads=(("ps", bk_ca), ("pcol", 0)), writes=("ya",))
            bk_cb = alloc_bank()

            def convB(e):
                ins = None
                for c in range(2):
                    for k in range(KB):
                        if smp:
                            rhs = hbext_s[:, c, :, k:k + TS]
                            out = banks[bk_cb][:, c * 256:c * 256 + nb].rearrange("p (q t) -> p q t", t=TS)
                        else:
                            rhs = hbx[par][:, c, k:k + nb]
                            out = banks[bk_cb][:, c * 256:c * 256 + nb]
                        ins = e.matmul(out, lhsT=diagB[:, c, k, :], rhs=rhs, start=(k == 0), stop=(k == KB - 1))
                return ins
            hb_reads = (("hbx_s",), "hbext_s_st", "diagB") if smp else (("hbx", par), "diagB")
            P.op("pe", convB, reads=hb_reads, writes=(("ps", bk_cb),))
            P.op("dve", lambda e: e.tensor_tensor(out=mix[par][:, 2:4, 0:nb], in0=psv(bk_cb, nb), in1=t1b[:, :, 0:nb], op=ALU.mult),
                 reads=("t1b", ("ps", bk_cb)), writes=(("mix", par, 1),))
            bk_tm = []
            for tl in range(ntl):
                bk = alloc_bank()

                def ptm(e, bk=bk, tl=tl):
                    ins = None
                    for k in range(8):
                        rhs = win[:, k, 8 * G:12 * G].rearrange("p (a s) -> p a s", a=2)[:, :, 0:G]
                        ins = e.matmul(banks[bk][:, :].rearrange("p (a s) -> p a s", a=2), lhsT=hT[par][:, k, tl * 128:(tl + 1) * 128],
                                       rhs=rhs, start=(k == 0), stop=(k == 7))
                    return ins
                P.op("pe", ptm, reads=(("win", 8), ("win", 10), ("hT", par, tl)), writes=(("ps", bk),))
                bk_tm.append(bk)
            if b == NBLK - 1 and l + 1 < DEPTH:
                reload_slab(l + 1, 8)
                reload_slab(l + 1, 10)
            late_pe = []
            bk_sp_box = []
            bk_pl_box = []
            for tl, ti in enumerate(info["tiles"]):
                bk = bk_tm[tl]
                cslot = tl
                c = new_col(4)
                dslot = ti % 3
                P.op("act", lambda e, bk=bk, dslot=dslot: e.activation(out=dxT[dslot][:], in_=banks[bk][:, G:2 * G], func=AF.Copy),
                     reads=(("ps", bk),), writes=(("dxT", dslot),))
                P.op("dve", lambda e, bk=bk, cslot=cslot: e.bn_stats(out=bnst[cslot][:], in_=banks[bk][:, 0:G]),
                     reads=(("ps", bk),), writes=(("bnst", cslot),))
                P.op("dve", lambda e, cslot=cslot, c=c: e.bn_aggr(out=stat[:, c:c + 2], in_=bnst[cslot][:]),
                     reads=(("bnst", cslot),), writes=(("stat", c),))
                P.op("dve", lambda e, c=c: e.tensor_scalar(out=stat[:, c + 2:c + 3], in0=stat[:, c + 1:c + 2], scalar1=EPS,
                                                           scalar2=None, op0=ALU.add),
                     reads=(("stat", c),), writes=(("stat", c + 2),))
                rsqrt_chain(stat[:, c + 2:c + 3], stat[:, c + 3:c + 4], stat[:, 510:511], stat[:, 511:512],
                            ("stat", c + 2), ("stat", c + 3), "stat_tmp2")
                P.op("dve", lambda e, bk=bk, cslot=cslot, c=c: e.tensor_scalar(
                    out=zc[cslot][:], in0=banks[bk][:, 0:G], scalar1=stat[:, c:c + 1], scalar2=stat[:, c + 3:c + 4],
                    op0=ALU.subtract, op1=ALU.mult),
                     reads=(("ps", bk), ("stat", c), ("stat", c + 3)), writes=(("zc", cslot),))
                veng = "dve" if smp else "pool"
                P.op(veng, lambda e, cslot=cslot: e.tensor_tensor(out=vn1[cslot][:], in0=zc[cslot][:], in1=lncg[l][:], op=ALU.mult),
                     reads=(("zc", cslot), ("lncg", l)), writes=(("vn1", cslot),))
                if smp:
                    P.op("dve", lambda e, cslot=cslot: e.tensor_tensor(out=vnf[:], in0=vn1[cslot][:], in1=lncb[l][:], op=ALU.add),
                         reads=(("vn1", cslot), ("lncb", l)), writes=("vnf",))
                    P.op("dve", lambda e, cslot=cslot: e.tensor_copy(out=vnb[cslot][:], in_=vnf[:]),
                         reads=("vnf",), writes=(("vnb", cslot),))
                    ld("sp", O["cv_s"][l], vnf[:], ("o", "cv_s", l), "vnf")
                else:
                    P.op("pool", lambda e, cslot=cslot: e.tensor_tensor(out=vnb[cslot][:], in0=vn1[cslot][:], in1=lncb[l][:], op=ALU.add),
                         reads=(("vn1", cslot), ("lncb", l)), writes=(("vnb", cslot),))

                def spat(e, cslot=cslot, tl=tl):
                    ins = None
                    for h in range(4):
                        r0 = (h % 2) * 64
                        out = banks[bk_sp_box[0]][r0:r0 + 64, (h // 2) * 256 + tl * 128:(h // 2) * 256 + (tl + 1) * 128]
                        wt = WmTs[l] if smp else WmT[l]
                        e.matmul(out, lhsT=vnb[cslot][:, h * 64:(h + 1) * 64], rhs=wt[:, h, :], start=True, stop=False)
                        if smp:
                            o3 = out.rearrange("p (q t) -> p q t", t=TS)
                            ins = e.matmul(o3, lhsT=sel4[:, h, :], rhs=bs8[l][:, 0:TS].unsqueeze(1).to_broadcast([128, NSEQ_S, TS]),
                                           start=False, stop=True)
                        else:
                            ins = e.matmul(out, lhsT=sel4[:, h, :], rhs=bs8[l][:, :], start=False, stop=True)
                    return ins
                late_pe.append((spat, (("vnb", cslot), ("WmTs", l) if smp else ("WmT", l), "sel4", ("bs8", l)), (("ps_sp",),)))
                if ti >= 15:
                    P.op("dve", lambda e, bk=bk: e.tensor_copy(out=dxf[:], in_=banks[bk][:, G:2 * G]),
                         reads=(("ps", bk),), writes=("dxf",))
                    if smp:
                        for q in range(NSEQ_S):
                            ld("sp", O["pd_s"][l, q, 7:15, :], dxf[q * TS:(q + 1) * TS, :], ("o", "pd_s1", l, q), "dxf")
                    else:
                        ld("sp", O["pd_p"][l], dxf[113:128, :], ("o", "pd_p", l), "dxf")

                def poolmm(e, dslot=dslot, tl=tl, ti=ti):
                    ins = None
                    pslot = (ti - 1) % 3
                    for g in range(4):
                        r0 = (g % 2) * 64
                        out = banks[bk_pl_box[0]][r0:r0 + 64, (g // 2) * 256 + tl * 128:(g // 2) * 256 + (tl + 1) * 128]
                        lh = dxT[dslot][:, g * 64:(g + 1) * 64]
                        if smp:
                            e.matmul(out, lhsT=lh, rhs=bscur[:, g, :], start=True, stop=False)
                            e.matmul(out, lhsT=spool[:, 0, g * 64:(g + 1) * 64], rhs=bsbuf[:, 0, g, :], start=False, stop=False)
                            ins = e.matmul(out, lhsT=spool[:, 1, g * 64:(g + 1) * 64], rhs=bsbuf[:, 1, g, :], start=False, stop=True)
                        elif ti == 0:
                            e.matmul(out, lhsT=lh, rhs=bcur0h[:, g, :], start=True, stop=False)
                            ins = e.matmul(out, lhsT=lh, rhs=bcur0l[:, g, :], start=False, stop=True)
                        else:
                            e.matmul(out, lhsT=lh, rhs=bcur[:, g, :], start=True, stop=False)
                            ins = e.matmul(out, lhsT=dxT[pslot][:, g * 64:(g + 1) * 64], rhs=bprev[:, g, :], start=False, stop=True)
                    return ins
                prd = (("dxT", dslot),) + ((("dxT", (ti - 1) % 3),) if (not smp and ti > 0) else ())
                late_pe.append((poolmm, prd + ("bcur", "bprev", "bcur0h", "bcur0l", "bscur", "bsbuf", "spool"), (("ps_pl",),)))
            if hook_conv_done is not None:
                hook_conv_done()
            bk_yc = 7

            def centre(e):
                ins = None
                for co in range(2):
                    for ci in range(2):
                        ins = e.matmul(banks[bk_yc][:, co * 256:co * 256 + nb], lhsT=cmat[:, 0 if ci == co else 1, :],
                                       rhs=ya[:, ci, 0:nb], start=(ci == 0), stop=(ci == 1))
                return ins
            P.op("pe", centre, reads=("ya", "cmat"), writes=(("ps", bk_yc),))
            P.op("act", lambda e: e.activation(out=ycsq[:, :, 0:nb], in_=psv(bk_yc, nb), func=AF.Square),
                 reads=(("ps", bk_yc),), writes=("ycsq",))
            if hook_pre_b is not None:
                hook_pre_b()
            if hook_out is not None:
                hook_out(0)
            bk_var = alloc_bank()

            def varmm(e):
                ins = None
                for ci in range(2):
                    ins = e.matmul(banks[bk_var][:, 0:nb], lhsT=ones256[:], rhs=ycsq[:, ci, 0:nb], start=(ci == 0), stop=(ci == 1))
                return ins
            P.op("pe", varmm, reads=("ycsq", "ones256"), writes=(("ps", bk_var),))
            P.op("dve", lambda e: e.tensor_scalar(out=varsb[:, 0:nb], in0=banks[bk_var][:, 0:nb], scalar1=EPS, scalar2=None,
                                                  op0=ALU.add),
                 reads=(("ps", bk_var),), writes=("varsb",))
            rsqrt_chain(varsb[:, 0:nb], rstdA[:, 0:nb], yn[:, 0, 0:nb], yn[:, 1, 0:nb], "varsb", "rstdA", "yn", newton_eng="dve")
            bk_pl = alloc_bank()
            bk_pl_box.append(bk_pl)

            def emit_late(tag):
                for fn, rds, wrs in late_pe:
                    if wrs[0] != (tag,):
                        continue
                    bkx = bk_sp_box[0] if tag == "ps_sp" else bk_pl
                    P.op("pe", fn, reads=rds, writes=(("ps", bkx),))
            emit_late("ps_pl")
            P.op("act", lambda e: e.activation(out=pooled[:, :, 0:nb], in_=psv(bk_pl, nb), func=AF.Copy),
                 reads=(("ps", bk_pl),), writes=("pooled",))
            bk_zc = proj_cm(l, b, 9)
            P.op("act", lambda e: e.activation(out=szc[:, :, 0:nb], in_=psv(bk_zc, nb), func=AF.Silu),
                 reads=(("ps", bk_zc),), writes=("szc",))
            bk_cu = proj_cm(l, b, 7)
            P.op("dve", lambda e: e.tensor_tensor(out=t1c[:, :, 0:nb], in0=psv(bk_cu, nb), in1=szc[:, :, 0:nb], op=ALU.mult),
                 reads=("szc", ("ps", bk_cu)), writes=("t1c",))
            bk_zd = proj_cm(l, b, 11)
            P.op("act", lambda e: e.activation(out=szd[:, :, 0:nb], in_=psv(bk_zd, nb), func=AF.Silu),
                 reads=(("ps", bk_zd),), writes=("szd",))
            if hook_out is not None:
                hook_out(1)
            very_last = (l == DEPTH - 1 and b == NBLK - 1)
            def emit_lnA_tail():
                P.op("dve", lambda e: e.tensor_tensor(out=yn[:, :, 0:nb], in0=psv(bk_yc, nb),
                                                      in1=rstdA[:, 0:nb].unsqueeze(1).to_broadcast([128, 2, nb]), op=ALU.mult),
                     reads=(("ps", bk_yc), "rstdA"), writes=("yn",))

                def s1f(e):
                    ins = None
                    for c in range(2):
                        ins = e.activation(out=s1[:, c, 0:nb], in_=yn[:, c, 0:nb], func=AF.Silu,
                                           scale=pcol[:, 1, l, c:c + 1], bias=pcol[:, 2, l, c:c + 1])
                    return ins
                P.op("act", s1f, reads=("yn", ("pcol", 1), ("pcol", 2)), writes=("s1",))
                P.op("pool", lambda e: e.tensor_tensor(out=mix[par][:, 0:2, 0:nb], in0=s1[:, :, 0:nb], in1=sza[:, :, 0:nb], op=ALU.mult),
                     reads=("s1", "sza"), writes=(("mix", par, 0),))
            if very_last:
                emit_lnA_tail()
            bk_sp = alloc_bank()
            bk_sp_box.append(bk_sp)
            emit_late("ps_sp")
            P.op("dve", lambda e: e.tensor_tensor(out=mix[par][:, 4:6, 0:nb], in0=psv(bk_sp, nb), in1=t1c[:, :, 0:nb], op=ALU.mult),
                 reads=("t1c", ("ps", bk_sp)), writes=(("mix", par, 2),))
            bk_wp = alloc_bank()

            def wpmm(e):
                ins = None
                for c in range(2):
                    ins = e.matmul(banks[bk_wp][:, c * 256:c * 256 + nb], lhsT=wpbd[l][:, c, :], rhs=pooled[:, c, 0:nb], start=True, stop=True)
                return ins
            P.op("pe", wpmm, reads=("pooled", ("wpbd", l)) + tuple(("wpbd", l, g) for g in range(4)), writes=(("ps", bk_wp),))

            def ydf(e):
                ins = None
                for c in range(2):
                    ins = e.scalar_tensor_tensor(out=mix[par][:, 6 + c, 0:nb], in0=banks[bk_wp][:, c * 256:c * 256 + nb],
                                                 scalar=pcol[:, 3, l, c:c + 1], in1=szd[:, c, 0:nb], op0=ALU.mult, op1=ALU.mult)
                return ins
            P.op("dve", ydf, reads=("szd", ("ps", bk_wp), ("pcol", 3)), writes=(("mix", par, 3),))
            if not very_last:
                emit_lnA_tail()
            if state_tile:
                for nm, src, scl, stg, rstg in (("ca", gluf, 0.5, sttm, "sttm"), ("cb", hbf, 1.0, zc[1], ("zc", 1))):
                    bk = alloc_bank()

                    def trs(e, bk=bk, src=src):
                        ins = None
                        for c in range(2):
                            ins = e.transpose(banks[bk][:, c * 128:(c + 1) * 128], src[:, c, :], ident_f[:])
                        return ins
                    P.op("pe", trs, reads=("gluf" if nm == "ca" else "hbf", "ident_f"), writes=(("ps", bk),))
                    P.op("dve", lambda e, bk=bk, scl=scl, stg=stg: e.tensor_scalar(out=stg[:], in0=banks[bk][:, 0:G], scalar1=scl,
                                                                                   scalar2=None, op0=ALU.mult),
                         reads=(("ps", bk),), writes=(rstg,))
                    if nm == "ca":
                        if smp:
                            for q in range(NSEQ_S):
                                ld("sp", O["ca_s"][l, q, 22:30, :], stg[q * TS:(q + 1) * TS, :], ("o", "ca_s1", l, q), rstg)
                        else:
                            ld("sp", O["ca_p"][l], stg[98:128, :], ("o", "ca_p", l), rstg)
                    else:
                        if smp:
                            for r in range(2):
                                ld("sp", O["cb_s"][l, :, r, :], stg[6 + r:128:8, :], ("o", "cb_s", l, r), rstg)
                        else:
                            ld("sp", O["cb_p"][l], stg[126:128, :], ("o", "cb_p", l), rstg)

        def stage_out(l, b, which=None):
            info = blk_info(b)
            par = (l * NBLK + b) % 2
            for tl, ti in enumerate(info["tiles"]):
                if which is not None and tl != which:
                    continue
                slot = xres_rr[0] % 2
                xres_rr[0] += 1
                rd = (("y0", ti),) if l == 1 else ()
                final_tile = (l == DEPTH - 1 and b == NBLK - 1)
                P.dma("act", lambda e, slot=slot, ti=ti: e.dma_start(out=xres[slot][:], in_=src_rows(l, ti)),
                      reads=rd, writes=(("xres", slot, 0), ("xres", slot, 1)))
                bks = []
                c = new_col(5)
                for hf in range(2):
                    bk = 5 + hf

                    def omm(e, bk=bk, hf=hf, tl=tl):
                        ins = None
                        for k in range(8):
                            ins = e.matmul(banks[bk][:, :], lhsT=mix[par][:, k, tl * 128:(tl + 1) * 128],
                                           rhs=wout[:, k, hf * 512:(hf + 1) * 512], start=(k == 0), stop=(k == 7))
                        return ins
                    P.op("pe", omm, reads=tuple(("mix", par, j) for j in range(4)) + tuple(("wout", j) for j in range(4)),
                         writes=(("ps", bk),))
                    P.op("act", lambda e, bk=bk, hf=hf, c=c: e.activation(out=ptmp[hf][:, :].bitcast(BF16)[:, 0:512], in_=banks[bk][:, :], func=AF.Square,
                                                                         accum_out=stat[:, c + hf:c + hf + 1]),
                         reads=(("ps", bk),), writes=(("ptmp", hf), ("stat", c + hf)))
                    bks.append(bk)
                P.op("dve", lambda e, c=c: e.tensor_scalar(out=stat[:, c + 2:c + 3], in0=stat[:, c:c + 1], scalar1=stat[:, c + 1:c + 2],
                                                           scalar2=None, op0=ALU.add),
                     reads=(("stat", c), ("stat", c + 1)), writes=(("stat", c + 2),))
                rstd_from(c + 2, c + 3, c + 4, 1.0 / D)
                for hf in range(2):
                    bk = bks[hf]
                    P.op("dve", lambda e, bk=bk, hf=hf, c=c: e.scalar_tensor_tensor(
                        out=ptmp[hf][:], in0=banks[bk][:, :], scalar=stat[:, c + 4:c + 5], in1=gpost[:, hf * 512:(hf + 1) * 512],
                        op0=ALU.mult, op1=ALU.mult),
                         reads=(("ps", bk), ("stat", c + 4), "gpost"), writes=(("ptmp", hf),))
                    aeng = "dve" if (final_tile and hf == 1) else "pool"
                    P.op(aeng, lambda e, hf=hf, slot=slot: e.tensor_tensor(
                        out=xres[slot][:, hf * 512:(hf + 1) * 512], in0=ptmp[hf][:], in1=xres[slot][:, hf * 512:(hf + 1) * 512],
                        op=ALU.add),
                         reads=(("ptmp", hf), ("xres", slot, hf)), writes=(("xres", slot, hf),))
                P.dma("sp", lambda e, slot=slot, ti=ti: e.dma_start(out=dst_rows(l, ti), in_=xres[slot][:]),
                      reads=(("xres", slot, 0), ("xres", slot, 1)), writes=((("y0", ti),) if l == 0 else (("o", "y", ti),)))

        pre_load(0, 0)
        load_layer_params(0, "gpre")
        load_consts("early")
        load_layer_params(0, "win")
        load_consts("mid")
        pre_a(0, 0)
        load_consts("rest")
        P.mute_ops = True
        bank_save0 = bank_rr[0]
        load_layer_params(0, "prep")
        bank_rr[0] = bank_save0
        P.mute_ops = False
        load_layer_params(0, "wout")
        pre_b(0, 0)
        P.mute_dma = True
        prep_sel[0] = "cw"
        load_layer_params(0, "prep")
        prep_sel[0] = None
        P.mute_dma = False
        load_layer_params(0, "diag")
        P.op("dve", lambda e: e.tensor_tensor(out=wm[:], in0=spw[:], in1=tril[:, :].unsqueeze(1).to_broadcast([128, 4, 128]),
                                              op=ALU.mult), reads=("spw", "tril"), writes=("wm",))
        wm_done[0] = True

        def nxt(l, b):
            if b + 1 < NBLK:
                return (l, b + 1)
            if l + 1 < DEPTH:
                return (l + 1, 0)
            return None

        def prv(l, b):
            if b > 0:
                return (l, b - 1)
            if l > 0:
                return (l - 1, NBLK - 1)
            return None

        for l in range(DEPTH):
            for b in range(NBLK):
                n1 = nxt(l, b)
                p1 = prv(l, b)
                if n1 is not None:
                    if n1[1] == 0:
                        load_layer_params(n1[0], "gpre")
                    pre_load(*n1)
                if 2 <= b <= 6:
                    P.mute_ops = True
                    load_layer_params(l, ("state", b - 2, "a"))
                    P.mute_ops = False
                last = (b == NBLK - 1 and l + 1 < DEPTH)
                bg = (l + 1 < DEPTH)

                def tick(l=l, bg=bg):
                    if bg:
                        wc_tick(l + 1)

                def early(l=l, b=b, n1=n1, tick=tick):
                    if l == 0 and b == 0:
                        P.mute_dma = True
                        prep_sel[0] = "rest"
                        load_layer_params(0, "prep")
                        prep_sel[0] = None
                        P.mute_dma = False
                    if b == 4 and l + 1 < DEPTH:
                        P.mute_dma = True
                        load_layer_params(l + 1, "prep")
                        P.mute_dma = False
                    if n1 is not None:
                        pre_a(*n1)
                    if 3 <= b <= 7:
                        load_layer_params(l, ("state", b - 3, "b"))
                    tick()

                def conv_done(l=l, b=b, last=last, tick=tick):
                    if 2 <= b <= 6:
                        P.mute_dma = True
                        load_layer_params(l, ("state", b - 2, "a"))
                        P.mute_dma = False
                    tick()

                def out_hook(tl, l=l, b=b, p1=p1, tick=tick):
                    if p1 is not None:
                        ntl = len(blk_info(p1[1])["tiles"])
                        if ntl == 1:
                            if tl == 1:
                                stage_out(p1[0], p1[1], 0)
                        elif tl < ntl:
                            stage_out(p1[0], p1[1], tl)
                        if tl == 1 and p1[0] != l:
                            load_layer_params(l, "wout")
                    tick()
                if b == 0 and l > 0:
                    load_layer_params(l, "diag")
                stage_proj(l, b,
                           hook_out=out_hook,
                           hook_pre_b=(lambda n1=n1: pre_b(*n1)) if n1 is not None else None,
                           hook_early=early,
                           hook_conv_done=conv_done)
                if b in (1, 2) and l + 1 < DEPTH:
                    P.mute_ops = True
                    bank_save = bank_rr[0]
                    prep_sel[0] = "cw" if b == 1 else "rest"
                    load_layer_params(l + 1, "prep")
                    prep_sel[0] = None
                    bank_rr[0] = bank_save
                    P.mute_ops = False
        stage_out(DEPTH - 1, NBLK - 1)

        with nc.allow_non_contiguous_dma(reason="tiny parameter columns / strided state rows"):
            P.emit(nc, es)
    return nc


_NC_CACHE = {}


def kernel(x_prompt, x_sample, state_conv_a, state_conv_b, state_pool, pre_norm_g, w_in, conv_a_w, conv_a_b,
           ln_a_g, ln_a_b, conv_b_w, ln_c_g, ln_c_b, spatial_w, spatial_b, pool_w, pool_scale, w_out, post_norm_g):
    n = 8
    f = lambda a: np.ascontiguousarray(np.asarray(a, dtype=np.float32))
    consts = _consts()
    shared = {
        "pre_g": f(pre_norm_g), "w_in": f(w_in), "conv_a_w": f(conv_a_w), "conv_a_b": f(conv_a_b),
        "ln_a_g": f(ln_a_g), "ln_a_b": f(ln_a_b), "conv_b_w": f(conv_b_w), "ln_c_g": f(ln_c_g), "ln_c_b": f(ln_c_b),
        "spatial_w": f(spatial_w), "spatial_b": f(spatial_b), "pool_w": f(pool_w), "pool_scale": f(pool_scale),
        "w_out": f(w_out), "post_g": f(post_norm_g),
    }
    for k, v in consts.items():
        shared["c_" + k] = f(v)
    x_prompt, x_sample = f(x_prompt), f(x_sample)
    sa, sbb, spl = f(state_conv_a), f(state_conv_b), f(state_pool)
    in_maps = []
    for i in range(n):
        m = dict(shared)
        s = slice(NSEQ_S * i, NSEQ_S * (i + 1))
        m["xp"] = x_prompt[i]
        m["xs"] = x_sample[s].reshape(128, D)
        m["sa"] = np.ascontiguousarray(sa[:, s].reshape(DEPTH, NSEQ_S * 30, G))
        m["sb"] = np.ascontiguousarray(sbb[:, s].reshape(DEPTH, NSEQ_S * 2, G))
        m["spl"] = np.ascontiguousarray(spl[:, s].reshape(DEPTH, NSEQ_S * 15, G))
        in_maps.append(m)
    if "nc" not in _NC_CACHE:
        _NC_CACHE["nc"] = build_nc()
    res = run_bass_kernel_spmd(_NC_CACHE["nc"], in_maps, core_ids=list(range(n)))
    R = res.results
    yp = np.stack([R[i]["yp"] for i in range(n)], 0)
    ys = np.concatenate([R[i]["ys"].reshape(NSEQ_S, TS, D) for i in range(n)], 0)
    ca_p = np.stack([R[i]["ca_p"] for i in range(n)], 1)
    ca_s = np.concatenate([R[i]["ca_s"] for i in range(n)], 1)
    cb_p = np.stack([R[i]["cb_p"] for i in range(n)], 1)
    cb_s = np.concatenate([R[i]["cb_s"] for i in range(n)], 1)
    pd_p = np.stack([R[i]["pd_p"] for i in range(n)], 1)
    pd_s = np.concatenate([R[i]["pd_s"] for i in range(n)], 1)
    cv_s = np.concatenate([R[i]["cv_s"].reshape(DEPTH, NSEQ_S, TS, G) for i in range(n)], 1)
    return tuple(np.ascontiguousarray(a.astype(np.float32)) for a in (yp, ys, ca_p, ca_s, cb_p, cb_s, pd_p, pd_s, cv_s))
```

```python
import contextlib
import numpy as np
import concourse.bass as bass
import concourse.mybir as mybir
from concourse.bass_utils import run_bass_kernel_spmd

F32 = mybir.dt.float32
BF16 = mybir.dt.bfloat16
I32 = mybir.dt.int32
RSQ_MAGIC = 1597463007
AF = mybir.ActivationFunctionType
ALU = mybir.AluOpType

D = 1024
G = 256
SEQ = 2048
NSEQ_S = 16
TS = 8
DEPTH = 2
EPS = 1e-6
KA = 31
KB = 3
NBLK = 9
WINS = (2, 4, 8, 16)


class Prog:
    ENG = ("pe", "act", "dve", "pool", "sp")
    CH = 3000
    NDS = 24

    def __init__(self):
        self.ops = {e: [] for e in self.ENG}
        self.cnt = {e: 0 for e in self.ENG}
        self.lastw = {}
        self.readers = {}
        self.known = {e: {} for e in self.ENG}
        self.known_dma = {e: set() for e in self.ENG}
        self.ndma = {"sp": 0, "pool": 0, "act": 0}

    def _collect(self, eng, reads, writes, is_dma):
        raw = set()
        other = set()
        for r in reads:
            if r in self.lastw:
                raw.add(self.lastw[r])
        for w in writes:
            if w in self.lastw:
                other.add(self.lastw[w])
            for rd in self.readers.get(w, ()):
                other.add(rd)
        waits = []
        for tok in sorted(raw | other, key=str):
            if tok[0] == "dma":
                if tok in self.known_dma[eng]:
                    continue
                self.known_dma[eng].add(tok)
                waits.append(tok)
                continue
            if tok[0] == eng and not is_dma:
                if eng == "pe":
                    continue
            if self.known[eng].get(tok[0], 0) >= tok[1]:
                continue
            self.known[eng][tok[0]] = tok[1]
            waits.append(tok)
        return waits

    seq = 0
    last_touch = None

    def _update(self, tok, reads, writes):
        if self.last_touch is None:
            self.last_touch = {}
        self.seq += 1
        for r in reads:
            self.readers.setdefault(r, []).append(tok)
            self.last_touch[r] = self.seq
        for w in writes:
            self.lastw[w] = tok
            self.readers[w] = []
            self.last_touch[w] = self.seq

    @staticmethod
    def _excl(reads, writes):
        ps = tuple(r for r in reads if isinstance(r, tuple) and r and r[0] == "ps")
        if ps:
            writes = tuple(writes) + tuple(p for p in ps if p not in writes)
        return reads, writes

    mute_ops = False
    mute_dma = False

    def op(self, eng, fn, reads=(), writes=()):
        if self.mute_ops:
            return
        reads, writes = self._excl(reads, writes)
        idx = self.cnt[eng] + 1
        self.cnt[eng] = idx
        waits = self._collect(eng, reads, writes, False)
        self.ops[eng].append(("op", waits, fn, idx))
        self._update((eng, idx), reads, writes)

    def dma(self, queue, fn, reads=(), writes=()):
        if self.mute_dma:
            return
        j = self.ndma[queue]
        self.ndma[queue] += 1
        waits = self._collect(queue, reads, writes, True)
        prev = ("dma", queue, j - self.NDS)
        if j >= self.NDS and prev not in self.known_dma[queue]:
            self.known_dma[queue].add(prev)
            waits.append(prev)
        self.ops[queue].append(("dma", waits, fn, j))
        self._update(("dma", queue, j), reads, writes)

    def emit(self, nc, es):
        sem = {}
        for e in self.ENG:
            n = (self.cnt[e] + self.CH - 1) // self.CH + 1
            sem[e] = [es.enter_context(nc.semaphore("s_%s_%d" % (e, i))) for i in range(n)]
        dsem = {q: [es.enter_context(nc.semaphore("s_dma_%s_%d" % (q, i))) for i in range(self.NDS)]
                for q in ("sp", "pool", "act")}

        def do_wait(e, tok):
            if tok[0] == "dma":
                q, j = tok[1], tok[2]
                e.wait_ge(dsem[q][j % self.NDS], 16 * (j // self.NDS + 1))
            else:
                i = tok[1] - 1
                e.wait_ge(sem[tok[0]][i // self.CH], (i % self.CH) + 1)

        def run(name, e):
            for kind, waits, fn, idx in self.ops[name]:
                for w in waits:
                    do_wait(e, w)
                ins = fn(e)
                if kind == "op":
                    ins.then_inc(sem[name][(idx - 1) // self.CH], 1)
                else:
                    ins.then_inc(dsem[name][idx % self.NDS], 16)
            if name == "sp":
                for q in ("sp", "pool", "act"):
                    for j in range(max(0, self.ndma[q] - self.NDS), self.ndma[q]):
                        do_wait(e, ("dma", q, j))

        block = es.enter_context(nc.Block())

        @block.tensor
        def _(e):
            run("pe", e)

        @block.scalar
        def _(e):
            run("act", e)

        @block.vector
        def _(e):
            run("dve", e)

        @block.gpsimd
        def _(e):
            run("pool", e)

        @block.sync
        def _(e):
            run("sp", e)


def _consts():
    c = {}
    c["ident"] = np.eye(128, dtype=np.float32)
    t = np.arange(128)
    c["tril"] = (t[None, :] <= t[:, None]).astype(np.float32)
    q = t // TS
    c["bdmask"] = (q[:, None] == q[None, :]).astype(np.float32)
    e8 = np.zeros((8, 128), np.float32)
    e8[t % TS, t] = 1.0
    c["e8"] = e8
    cd = np.eye(128, dtype=np.float32) - 1.0 / 256
    co = np.full((128, 128), -1.0 / 256, np.float32)
    c["cmat"] = np.stack([cd, co], 1)
    c["ones256"] = np.full((128, 128), 1.0 / 256, np.float32)
    bcur = np.zeros((128, 4, 128), np.float64)
    bprev = np.zeros((128, 4, 128), np.float64)
    bcur0 = np.zeros((128, 4, 128), np.float64)
    bscur = np.zeros((128, 4, 128), np.float64)
    bsbuf = np.zeros((120, 2, 4, 128), np.float64)
    for g, w in enumerate(WINS):
        for tt in range(128):
            for d in range(w):
                s = tt - d
                if s >= 0:
                    bcur[s, g, tt] += 1.0 / w
                    bcur0[s, g, tt] += 1.0 / min(tt + 1, w)
                else:
                    bprev[128 + s, g, tt] += 1.0 / w
            bcur[tt, g, tt] -= 1.0
            bcur0[tt, g, tt] -= 1.0
            qq, tl = tt // TS, tt % TS
            for d in range(w):
                s = tl - d
                if s >= 0:
                    bscur[qq * TS + s, g, tt] += 1.0 / w
                else:
                    r = 15 + s
                    if r >= 0:
                        bsbuf[(qq % 8) * 15 + r, qq // 8, g, tt] += 1.0 / w
            bscur[tt, g, tt] -= 1.0
    c["bcur"] = bcur.astype(np.float32)
    c["bprev"] = bprev.astype(np.float32)
    hi = bcur0.astype(np.float32)
    u = hi.view(np.uint32).astype(np.uint64)
    u = ((u + 0x7FFF + ((u >> 16) & 1)) >> 16) << 16
    hi_b = u.astype(np.uint32).view(np.float32)
    c["bcur0h"] = hi_b
    c["bcur0l"] = (bcur0 - hi_b).astype(np.float32)
    sel = np.zeros((128, 4, 64), np.float32)
    for h in range(4):
        sel[h, h, :] = 1.0
        sel[32 + h, h, :] = 1.0
    c["sel4"] = sel
    c["bscur"] = bscur.astype(np.float32)
    c["bsbuf"] = bsbuf.astype(np.float32)
    return c


CONST_SHAPES = {
    "ident": [128, 128], "tril": [128, 128], "bdmask": [128, 128], "e8": [8, 128],
    "cmat": [128, 2, 128], "ones256": [128, 128], "bcur": [128, 4, 128], "bprev": [128, 4, 128],
    "bcur0h": [128, 4, 128], "bcur0l": [128, 4, 128], "bscur": [128, 4, 128],
    "bsbuf": [120, 2, 4, 128], "sel4": [128, 4, 64],
}

IN_SHAPES = {
    "xp": [SEQ, D], "xs": [128, D],
    "sa": [DEPTH, NSEQ_S * 30, G], "sb": [DEPTH, NSEQ_S * 2, G], "spl": [DEPTH, NSEQ_S * 15, G],
    "pre_g": [DEPTH, D], "w_in": [DEPTH, D, 12 * G], "conv_a_w": [DEPTH, KA, G],
    "conv_a_b": [DEPTH, G], "ln_a_g": [DEPTH, G], "ln_a_b": [DEPTH, G], "conv_b_w": [DEPTH, KB, G],
    "ln_c_g": [DEPTH, G], "ln_c_b": [DEPTH, G], "spatial_w": [DEPTH, 4, 128, 128],
    "spatial_b": [DEPTH, 4, 128], "pool_w": [DEPTH, 4, 64, 64], "pool_scale": [DEPTH, G],
    "w_out": [DEPTH, D, D], "post_g": [DEPTH, D],
}

OUT_SHAPES = {
    "yp": [SEQ, D], "ys": [128, D],
    "ca_p": [DEPTH, 30, G], "ca_s": [DEPTH, NSEQ_S, 30, G],
    "cb_p": [DEPTH, 2, G], "cb_s": [DEPTH, NSEQ_S, 2, G],
    "pd_p": [DEPTH, 15, G], "pd_s": [DEPTH, NSEQ_S, 15, G],
    "cv_s": [DEPTH, 128, G],
}


def build_nc():
    nc = bass.Bass("TRN2", target_bir_lowering=False)
    I = {k: nc.dram_tensor(k, s, F32, kind="ExternalInput").ap() for k, s in IN_SHAPES.items()}
    C = {k: nc.dram_tensor("c_" + k, s, F32, kind="ExternalInput").ap() for k, s in CONST_SHAPES.items()}
    O = {k: nc.dram_tensor(k, s, F32, kind="ExternalOutput").ap() for k, s in OUT_SHAPES.items()}
    y0 = nc.dram_tensor("y0_scratch", [SEQ + 128, D], F32, kind="Internal").ap()
    wbf_in = nc.dram_tensor("wbf_in", [D, 12 * G], BF16, kind="Internal").ap()
    wbf_out = nc.dram_tensor("wbf_out", [D, D], BF16, kind="Internal").ap()

    P = Prog()
    es = contextlib.ExitStack()
    with es:
        def sb(name, shape, dt=F32):
            return es.enter_context(nc.sbuf_tensor(name, shape, dt))

        banks = [es.enter_context(nc.psum_tensor("ps%d" % i, [128, 512], F32)) for i in range(8)]
        banks_bf = [b.bitcast(BF16) for b in banks]
        bank_rr = [0]

        def alloc_bank():
            if P.last_touch is None:
                P.last_touch = {}
            b = min(range(5), key=lambda i: (P.last_touch.get(("ps", i), -1), i))
            P.seq += 1
            P.last_touch[("ps", b)] = P.seq
            bank_rr[0] += 1
            return b

        win = sb("win", [128, 8, 12 * G], BF16)
        wout = sb("wout", [128, 8, D], BF16)
        diagA = sb("diagA", [128, 2, KA, 128], BF16)
        diagB = sb("diagB", [128, 2, KB, 128], BF16)
        xin = [sb("xin%d" % i, [128, D]) for i in range(2)]
        xres = [sb("xres%d" % i, [128, D]) for i in range(2)]
        htm = [sb("htm%d" % i, [128, D], BF16) for i in range(2)]
        hT = [sb("hT%d" % i, [128, 8, 256], BF16) for i in range(2)]
        mix = [sb("mix%d" % i, [128, 8, 256], BF16) for i in range(2)]
        glx = [sb("glx%d" % i, [128, 2, 30 + 256], BF16) for i in range(2)]
        gluext_s = sb("gluext_s", [128, 2, NSEQ_S, 38], BF16)
        hbx = [sb("hbx%d" % i, [128, 2, 2 + 256], BF16) for i in range(2)]
        hbext_s = sb("hbext_s", [128, 2, NSEQ_S, 10], BF16)
        th = sb("th", [128, 2, 256], BF16)
        sza = sb("sza", [128, 2, 256], BF16)
        ya = sb("ya", [128, 2, 256], BF16)
        ycsq = sb("ycsq", [128, 2, 256], BF16)
        varsb = sb("varsb", [128, 256])
        rstdA = sb("rstdA", [128, 256])
        yn = sb("yn", [128, 2, 256])
        s1 = sb("s1", [128, 2, 256], BF16)
        gluf = sb("gluf", [128, 2, 128])
        bx = sb("bx", [128, 2, 256], BF16)
        szb = sb("szb", [128, 2, 256], BF16)
        t1b = sb("t1b", [128, 2, 256], BF16)
        hbf = sb("hbf", [128, 2, 128])
        zc = [sb("zc%d" % i, [128, G]) for i in range(2)]
        vn1 = [sb("vn1_%d" % i, [128, G]) for i in range(2)]
        vnf = sb("vnf", [128, G])
        vnb = [sb("vnb%d" % i, [128, G], BF16) for i in range(2)]
        szc = sb("szc", [128, 2, 256], BF16)
        t1c = sb("t1c", [128, 2, 256], BF16)
        dxT = [sb("dxT%d" % i, [128, G], BF16) for i in range(3)]
        dxf = sb("dxf", [128, G])
        pooled = sb("pooled", [128, 2, 256], BF16)
        szd = sb("szd", [128, 2, 256], BF16)
        ptmp = [sb("ptmp%d" % i, [128, 512]) for i in range(2)]
        sttm = sb("sttm", [128, G])
        stat = sb("stat", [128, 512])
        bnst = [sb("bnst%d" % i, [128, 6]) for i in range(2)]
        ident_f = sb("ident_f", [128, 128])
        ident_b = sb("ident_b", [128, 128], BF16)
        tril = sb("tril", [128, 128])
        bdmask = sb("bdmask", [128, 128])
        e8 = sb("e8", [8, 128], BF16)
        cmat = sb("cmat", [128, 2, 128], BF16)
        ones256 = sb("ones256", [128, 128], BF16)
        ones1 = sb("ones1", [1, 64], BF16)
        bcur = sb("bcur", [128, 4, 128], BF16)
        bprev = sb("bprev", [128, 4, 128], BF16)
        bcur0h = sb("bcur0h", [128, 4, 128], BF16)
        bcur0l = sb("bcur0l", [128, 4, 128], BF16)
        bscur = sb("bscur", [128, 4, 128], BF16)
        bsbuf = sb("bsbuf", [120, 2, 4, 128], BF16)
        wstg_f = [sb("wstg_f%d" % i, [128, 1024]) for i in range(2)]
        wstg_b = sb("wstg_b", [128, 1024], BF16)
        gpre = sb("gpre", [128, D])
        gpost = sb("gpost", [128, D])
        lncg = [sb("lncg%d" % i, [128, G]) for i in range(DEPTH)]
        lncb = [sb("lncb%d" % i, [128, G]) for i in range(DEPTH)]
        pcol = sb("pcol", [128, 4, DEPTH, 2])
        cw_raw = sb("cw_raw", [KA + KB, G])
        cwT = [sb("cwT%d" % i, [128, 2, KA + KB]) for i in range(DEPTH)]
        spw = sb("spw", [128, 4, 128])
        wm = sb("wm", [128, 4, 128], BF16)
        WmT = [sb("WmT%d" % i, [128, 4, 128], BF16) for i in range(DEPTH)]
        WmTs = [sb("WmTs%d" % i, [128, 4, 128], BF16) for i in range(DEPTH)]
        p1sb = sb("p1sb", [8, 128], BF16)
        bsf = sb("bsf", [36, 128])
        bshf = sb("bshf", [36, 128])
        bsh2 = sb("bsh2", [36, 128], BF16)
        bs8 = [sb("bs8_%d" % i, [128, 128], BF16) for i in range(DEPTH)]
        sel4 = sb("sel4", [128, 4, 64], BF16)
        wpbd = [sb("wpbd%d" % i, [128, 2, 128], BF16) for i in range(DEPTH)]
        sa_raw = sb("sa_raw", [120, G])
        sa_bf = sb("sa_bf", [120, G], BF16)
        sb_raw = sb("sb_raw", [32, G])
        sb_bf = sb("sb_bf", [32, G], BF16)
        spool = sb("spool", [120, 2, G], BF16)

        def ld(queue, dst, src, wname, rname=None):
            P.dma(queue, lambda e, dst=dst, src=src: e.dma_start(out=dst, in_=src),
                  reads=(rname,) if rname else (), writes=(wname,))

        def load_consts(part):
            if part == "early":
                P.op("pool", lambda e: e.memset(ones1[:], 1.0), writes=("ones1",))
                P.op("act", lambda e: e.activation(out=bnst[0][0:1, 0:6], in_=ones1[0:1, 0:6], func=AF.Silu),
                     reads=("ones1",), writes=(("bnst", 0),))
                ld("pool", ident_b[:], C["ident"], "ident_b")
                ld("pool", e8[:], C["e8"], "e8")
                ld("pool", cmat[:], C["cmat"], "cmat")
                ld("pool", ones256[:], C["ones256"], "ones256")
                return
            if part == "mid":
                ld("pool", sel4[:], C["sel4"], "sel4")
                ld("pool", bcur0h[:], C["bcur0h"], "bcur0h")
                ld("pool", bcur0l[:], C["bcur0l"], "bcur0l")
                ld("pool", bcur[:], C["bcur"], "bcur")
                ld("pool", bprev[:], C["bprev"], "bprev")
                P.op("pool", lambda e: e.memset(wpbd[0][:], 0.0), writes=(("wpbd", 0),))
                for g in range(4):
                    r0 = (g % 2) * 64
                    ld("pool", wpbd[0][r0:r0 + 64, g // 2, r0:r0 + 64], I["pool_w"][0, g], ("wpbd", 0, g), ("wpbd", 0))
                return
            ld("sp", ident_f[:], C["ident"], "ident_f")
            ld("sp", tril[:], C["tril"], "tril")
            ld("sp", bdmask[:], C["bdmask"], "bdmask")
            for j, nm in enumerate(("conv_a_b", "ln_a_g", "ln_a_b", "pool_scale")):
                ld("sp", pcol[:, j, :, :], I[nm].rearrange("l (c p) -> p l c", p=128), ("pcol", j))
            ld("pool", bscur[:], C["bscur"], "bscur")
            ld("pool", bsbuf[:], C["bsbuf"], "bsbuf")

        stat_col = [0]

        def new_col(n=1):
            c0 = stat_col[0]
            stat_col[0] += n
            assert stat_col[0] <= 508
            return c0

        prep_sel = [None]
        wm_done = [False]

        def load_layer_params(l, part):
            if part == "win":
                wv = I["w_in"][l].rearrange("(k p) e -> p k e", p=128)
                for s in (1, 0, 2, 6, 5, 4, 3, 8, 10, 9, 7, 11):
                    ld("pool", win[:, :, s * G:(s + 1) * G], wv[:, :, s * G:(s + 1) * G], ("win", s))
                return
            if part == "wout":
                if l == 0:
                    wo = I["w_out"][l].rearrange("(k p) e -> p k e", p=128)
                    for k in range(0, 8, 2):
                        ld("pool", wout[:, k:k + 2, :], wo[:, k:k + 2, :], ("wout", k // 2))
                else:
                    wo = wbf_out.rearrange("(k p) e -> p k e", p=128)
                    for k in range(0, 8, 2):
                        P.dma("act", lambda e, k=k: e.dma_start(out=wout[:, k:k + 2, :], in_=wo[:, k:k + 2, :]),
                              reads=tuple(("wbf", pc) for pc in range(24, 32)), writes=(("wout", k // 2),))
                ld("sp", gpost[:], I["post_g"][l:l + 1, :].partition_broadcast(128).rearrange("p o d -> p (o d)"), "gpost")
                return
            if part == "gpre":
                ld("sp", gpre[:], I["pre_g"][l:l + 1, :].partition_broadcast(128).rearrange("p o d -> p (o d)"), "gpre")
                return
            if part == "prep":
                sel = prep_sel[0]
                lq = "sp"
                ld(lq, lncg[l][:], I["ln_c_g"][l:l + 1, :].partition_broadcast(128).rearrange("p o d -> p (o d)"), ("lncg", l))
                ld(lq, lncb[l][:], I["ln_c_b"][l:l + 1, :].partition_broadcast(128).rearrange("p o d -> p (o d)"), ("lncb", l))
                if sel in (None, "cw"):
                    ld(lq, cw_raw[0:KA, :], I["conv_a_w"][l], "cw_raw")
                    ld(lq, cw_raw[KA:KA + KB, :], I["conv_b_w"][l], "cw_raw", "cw_raw")
                    for c in range(2):
                        b = alloc_bank()
                        P.op("pe", lambda e, b=b, c=c: e.transpose(banks[b][:, 0:KA + KB], cw_raw[0:KA + KB, c * 128:(c + 1) * 128],
                                                                    ident_f[0:KA + KB, 0:KA + KB]),
                             reads=("cw_raw", "ident_f"), writes=(("ps", b),))
                        P.op("dve", lambda e, b=b, c=c: e.tensor_scalar(out=cwT[l][:, c, 0:KA], in0=banks[b][:, 0:KA], scalar1=0.5,
                                                                         scalar2=None, op0=ALU.mult),
                             reads=(("ps", b),), writes=(("cwT", l, c, 0),))
                        P.op("dve", lambda e, b=b, c=c: e.tensor_copy(out=cwT[l][:, c, KA:KA + KB], in_=banks[b][:, KA:KA + KB]),
                             reads=(("ps", b),), writes=(("cwT", l, c, 1),))
                if sel in (None, "rest"):
                    ld(lq, spw[:], I["spatial_w"][l].rearrange("h t s -> t h s"), "spw")
                    if not wm_done[0]:
                        P.op("dve", lambda e: e.tensor_tensor(out=wm[:], in0=spw[:], in1=tril[:, :].unsqueeze(1).to_broadcast([128, 4, 128]),
                                                              op=ALU.mult), reads=("spw", "tril"), writes=("wm",))
                    wm_done[0] = False
                    for h in range(4):
                        b = alloc_bank()
                        P.op("pe", lambda e, b=b, h=h: e.transpose(banks_bf[b][:, 0:128], wm[:, h, :], ident_b[:]),
                             reads=("wm", "ident_b"), writes=(("ps", b),))
                        P.op("dve", lambda e, b=b, h=h: e.tensor_copy(out=WmT[l][:, h, :], in_=banks_bf[b][:, 0:128]),
                             reads=(("ps", b),), writes=(("WmT", l),))
                        b1 = alloc_bank()
                        P.op("pe", lambda e, b1=b1, h=h: e.matmul(banks[b1][0:8, 0:128], lhsT=wm[0:8, h, 0:8], rhs=e8[:, :],
                                                                   start=True, stop=True),
                             reads=("wm", "e8"), writes=(("ps", b1),))
                        P.op("dve", lambda e, b1=b1: e.tensor_copy(out=p1sb[:], in_=banks[b1][0:8, 0:128]),
                             reads=(("ps", b1),), writes=("p1sb",))
                        b2 = alloc_bank()
                        P.op("pe", lambda e, b2=b2: e.matmul(banks[b2][:, 0:128], lhsT=e8[:, :], rhs=p1sb[:, :], start=True, stop=True),
                             reads=("p1sb", "e8"), writes=(("ps", b2),))
                        P.op("dve", lambda e, b2=b2, h=h: e.tensor_tensor(out=WmTs[l][:, h, :], in0=banks[b2][:, 0:128], in1=bdmask[:],
                                                                          op=ALU.mult),
                             reads=(("ps", b2), "bdmask"), writes=(("WmTs", l),))
                    ld(lq, bsf[0:4, :], I["spatial_b"][l], "bsf")
                    ld(lq, bsf[32:36, :], I["spatial_b"][l], "bsf", "bsf")
                    P.op("pool", lambda e: e.memset(bs8[l][:], 0.0), writes=(("bs8", l),))
                    P.op("dve", lambda e: e.tensor_copy(out=bs8[l][0:4, :], in_=bsf[0:4, :]), reads=("bsf",), writes=(("bs8", l),))
                    P.op("dve", lambda e: e.tensor_copy(out=bsh2[32:36, :], in_=bsf[32:36, :]), reads=("bsf",), writes=("bsh2",))
                    P.op("dve", lambda e: e.tensor_copy(out=bshf[32:36, :], in_=bsh2[32:36, :]), reads=("bsh2",), writes=("bshf",))
                    P.op("dve", lambda e: e.tensor_tensor(out=bs8[l][32:36, :], in0=bsf[32:36, :], in1=bshf[32:36, :], op=ALU.subtract),
                         reads=("bsf", "bshf"), writes=(("bs8", l),))
                    if not P.mute_ops and l > 0:
                        dm = P.mute_dma
                        P.mute_dma = False
                        P.op("pool", lambda e: e.memset(wpbd[l][:], 0.0), writes=(("wpbd", l),))
                        for g in range(4):
                            r0 = (g % 2) * 64
                            ld("pool", wpbd[l][r0:r0 + 64, g // 2, r0:r0 + 64], I["pool_w"][l, g], ("wpbd", l, g), ("wpbd", l))
                        P.mute_dma = dm
                    ld("sp", O["ca_s"][l, :, 0:22, :], I["sa"][l].rearrange("(q r) c -> q r c", r=30)[:, 8:30, :], ("o", "ca_s0", l))
                    ld("sp", O["pd_s"][l, :, 0:7, :], I["spl"][l].rearrange("(q r) c -> q r c", r=15)[:, 8:15, :], ("o", "pd_s0", l))
                return
            if part == "diag":
                for c in range(2):
                    P.op("dve", lambda e, c=c: e.tensor_tensor(
                        out=diagA[:, c, :, :], in0=ident_b[:, :].unsqueeze(1).to_broadcast([128, KA, 128]),
                        in1=cwT[l][:, c, 0:KA].unsqueeze(2).to_broadcast([128, KA, 128]), op=ALU.mult),
                         reads=("ident_b", ("cwT", l, c, 0)), writes=("diagA",))
                    P.op("pool", lambda e, c=c: e.tensor_tensor(
                        out=diagB[:, c, :, :], in0=ident_b[:, :].unsqueeze(1).to_broadcast([128, KB, 128]),
                        in1=cwT[l][:, c, KA:KA + KB].unsqueeze(2).to_broadcast([128, KB, 128]), op=ALU.mult),
                         reads=("ident_b", ("cwT", l, c, 1)), writes=("diagB",))
                return
            assert part[0] == "state"
            _, j, ph = part
            if j < 4:
                if ph == "a":
                    ld("act", sa_raw[:], I["sa"][l, j * 120:(j + 1) * 120, :], "sa_raw")
                    P.op("pool", lambda e: e.tensor_copy(out=sa_bf[:], in_=sa_raw[:]), reads=("sa_raw",), writes=("sa_bf",))
                else:
                    for c in range(2):
                        b = alloc_bank()
                        P.op("pe", lambda e, b=b, c=c: e.transpose(banks_bf[b][:, 0:120], sa_bf[:, c * 128:(c + 1) * 128],
                                                                    ident_b[0:120, 0:120]),
                             reads=("sa_bf", "ident_b"), writes=(("ps", b),))
                        P.op("act", lambda e, b=b, c=c, j=j: e.activation(
                            out=gluext_s[:, c, 4 * j:4 * j + 4, 0:30],
                            in_=banks_bf[b][:, 0:120].rearrange("p (q r) -> p q r", r=30), func=AF.Copy, scale=2.0),
                             reads=(("ps", b),), writes=("gluext_s_st",))
                return
            if ph == "a":
                ld("act", sb_raw[:], I["sb"][l], "sb_raw")
                P.op("pool", lambda e: e.tensor_copy(out=sb_bf[:], in_=sb_raw[:]), reads=("sb_raw",), writes=("sb_bf",))
                ld("pool", spool[:], I["spl"][l].rearrange("(h p) c -> p h c", p=120), "spool")
            else:
                for c in range(2):
                    b = alloc_bank()
                    P.op("pe", lambda e, b=b, c=c: e.transpose(banks_bf[b][:, 0:32], sb_bf[:, c * 128:(c + 1) * 128],
                                                                ident_b[0:32, 0:32]),
                         reads=("sb_bf", "ident_b"), writes=(("ps", b),))
                    P.op("act", lambda e, b=b, c=c: e.activation(
                        out=hbext_s[:, c, :, 0:2], in_=banks_bf[b][:, 0:32].rearrange("p (q r) -> p q r", r=2), func=AF.Copy),
                         reads=(("ps", b),), writes=("hbext_s_st",))

        def blk_info(b):
            if b < 8:
                return dict(tiles=[2 * b, 2 * b + 1], nb=256, t0=256 * b, sample=False)
            return dict(tiles=[16], nb=128, t0=0, sample=True)

        def src_rows(l, ti):
            if l == 0:
                return I["xp"][ti * 128:(ti + 1) * 128, :] if ti < 16 else I["xs"][:, :]
            return y0[ti * 128:(ti + 1) * 128, :]

        def dst_rows(l, ti):
            if l == 0:
                return y0[ti * 128:(ti + 1) * 128, :]
            return O["yp"][ti * 128:(ti + 1) * 128, :] if ti < 16 else O["ys"][:, :]

        def rsqrt_chain(v, y, t0, t1, rv, ry, rt, newton_eng="dve"):
            P.op("dve", lambda e: e.tensor_single_scalar(out=t0.bitcast(I32), in_=v.bitcast(I32), scalar=1,
                                                         op=ALU.logical_shift_right),
                 reads=(rv,), writes=(rt,))
            P.op("dve", lambda e: e.tensor_scalar(out=y.bitcast(I32), in0=t0.bitcast(I32), scalar1=-1, scalar2=RSQ_MAGIC,
                                                  op0=ALU.mult, op1=ALU.add),
                 reads=(rt,), writes=(ry,))
            for _ in range(2):
                if newton_eng == "dve":
                    P.op("dve", lambda e: e.tensor_tensor(out=t0, in0=y, in1=y, op=ALU.mult), reads=(ry,), writes=(rt,))
                    P.op("dve", lambda e: e.scalar_tensor_tensor(out=t1, in0=t0, scalar=-0.5, in1=v, op0=ALU.mult, op1=ALU.mult),
                         reads=(rt, rv), writes=(rt,))
                    P.op("dve", lambda e: e.scalar_tensor_tensor(out=y, in0=t1, scalar=1.5, in1=y, op0=ALU.add, op1=ALU.mult),
                         reads=(rt, ry), writes=(ry,))
                else:
                    P.op("pool", lambda e: e.tensor_tensor(out=t0, in0=y, in1=y, op=ALU.mult), reads=(ry,), writes=(rt,))
                    P.op("pool", lambda e: e.tensor_tensor(out=t1, in0=t0, in1=v, op=ALU.mult), reads=(rt, rv), writes=(rt,))
                    P.op("pool", lambda e: e.tensor_scalar(out=t1, in0=t1, scalar1=-0.5, scalar2=1.5, op0=ALU.mult, op1=ALU.add),
                         reads=(rt,), writes=(rt,))
                    P.op("pool", lambda e: e.tensor_tensor(out=y, in0=y, in1=t1, op=ALU.mult), reads=(rt, ry), writes=(ry,))

        def rstd_from(col_sum, col_tmp, col_out, scale):
            P.op("dve", lambda e: e.tensor_scalar(out=stat[:, col_tmp:col_tmp + 1], in0=stat[:, col_sum:col_sum + 1],
                                                  scalar1=scale, scalar2=EPS, op0=ALU.mult, op1=ALU.add),
                 reads=(("stat", col_sum),), writes=(("stat", col_tmp),))
            rsqrt_chain(stat[:, col_tmp:col_tmp + 1], stat[:, col_out:col_out + 1], stat[:, 508:509], stat[:, 509:510],
                        ("stat", col_tmp), ("stat", col_out), "stat_tmp")

        xin_rr = [0]
        xres_rr = [0]

        pre_slots = {}

        def pre_load(l, b):
            info = blk_info(b)
            for tl, ti in enumerate(info["tiles"]):
                slot = xin_rr[0] % 2
                xin_rr[0] += 1
                pre_slots[(l, b, tl)] = slot
                rd = (("y0", ti),) if l == 1 else ()
                P.dma("act", lambda e, slot=slot, ti=ti: e.dma_start(out=xin[slot][:], in_=src_rows(l, ti)),
                      reads=rd, writes=(("xin", slot),))

        def pre_a(l, b):
            info = blk_info(b)
            for tl, ti in enumerate(info["tiles"]):
                slot = pre_slots[(l, b, tl)]
                c = new_col(3)
                P.op("act", lambda e, slot=slot, c=c: e.activation(out=htm[slot][:], in_=xin[slot][:], func=AF.Square,
                                                                   accum_out=stat[:, c:c + 1]),
                     reads=(("xin", slot),), writes=(("htm", slot), ("stat", c)))
                rstd_from(c, c + 1, c + 2, 1.0 / D)
                P.op("dve", lambda e, slot=slot, c=c: e.scalar_tensor_tensor(
                    out=htm[slot][:], in0=xin[slot][:], scalar=stat[:, c + 2:c + 3], in1=gpre[:],
                    op0=ALU.mult, op1=ALU.mult),
                     reads=(("xin", slot), ("stat", c + 2), "gpre"), writes=(("htm", slot),))

        def pre_b(l, b):
            info = blk_info(b)
            par = (l * NBLK + b) % 2
            for tl, ti in enumerate(info["tiles"]):
                slot = pre_slots[(l, b, tl)]
                bk = alloc_bank()

                def tr(e, slot=slot, bk=bk):
                    ins = None
                    for j in range(8):
                        ins = e.transpose(banks_bf[bk][:, j * 128:(j + 1) * 128], htm[slot][:, j * 128:(j + 1) * 128], ident_b[:])
                    return ins
                P.op("pe", tr, reads=(("htm", slot), "ident_b"), writes=(("ps", bk),))
                P.op("act", lambda e, bk=bk, tl=tl, par=par: e.activation(
                    out=hT[par][:, :, tl * 128:(tl + 1) * 128],
                    in_=banks_bf[bk][:, 0:1024].rearrange("p (j t) -> p j t", j=8), func=AF.Copy),
                     reads=(("ps", bk),), writes=(("hT", par, tl),))

        def wc_src(l, pc):
            if pc < 24:
                k, j = pc // 3, pc % 3
                return I["w_in"][l, k * 128:(k + 1) * 128, j * 1024:(j + 1) * 1024]
            k = pc - 24
            return I["w_out"][l, k * 128:(k + 1) * 128, :]

        def wc_dst(pc):
            if pc < 24:
                k, j = pc // 3, pc % 3
                return wbf_in[k * 128:(k + 1) * 128, j * 1024:(j + 1) * 1024]
            k = pc - 24
            return wbf_out[k * 128:(k + 1) * 128, :]

        wc_state = {"tick": 0}

        def wc_tick(l):
            k = wc_state["tick"]
            wc_state["tick"] = k + 1
            pc = k - 2
            if 0 <= pc < 32:
                buf = pc % 2
                P.op("act", lambda e, buf=buf: e.activation(out=wstg_b[:], in_=wstg_f[buf][:], func=AF.Copy),
                     reads=(("wstg_f", buf),), writes=("wstg_b",))
                P.dma("sp", lambda e, pc=pc: e.dma_start(out=wc_dst(pc), in_=wstg_b[:]),
                      reads=("wstg_b",), writes=(("wbf", pc),))
            if k < 32:
                buf = k % 2
                P.dma("act", lambda e, k=k, buf=buf: e.dma_start(out=wstg_f[buf][:], in_=wc_src(l, k)),
                      writes=(("wstg_f", buf),))

        def reload_slab(l, s):
            wv = wbf_in.rearrange("(k p) e -> p k e", p=128)
            if s == 2:
                lo, hi = 0, 3
            elif s == 3:
                lo, hi = 3, 7
            elif s == 11:
                lo, hi = 7, 12
            else:
                return
            P.dma("act", lambda e: e.dma_start(out=win[:, :, lo * G:hi * G], in_=wv[:, :, lo * G:hi * G]),
                  reads=tuple(("wbf", pc) for pc in range(24)), writes=tuple(("win", j) for j in range(lo, hi)))

        def proj_cm(l, b, s):
            info = blk_info(b)
            par, nb = (l * NBLK + b) % 2, info["nb"]
            bk = alloc_bank()

            def f(e):
                ins = None
                for c in range(2):
                    for k in range(8):
                        ins = e.matmul(banks[bk][:, c * 256:c * 256 + nb], lhsT=win[:, k, s * G + c * 128:s * G + (c + 1) * 128],
                                       rhs=hT[par][:, k, 0:nb], start=(k == 0), stop=(k == 7))
                return ins
            P.op("pe", f, reads=(("win", s),) + tuple(("hT", par, tl) for tl in range(len(info["tiles"]))),
                 writes=(("ps", bk),))
            if b == NBLK - 1 and l + 1 < DEPTH:
                reload_slab(l + 1, s)
            return bk

        def psv(bk, nb):
            return banks[bk][:, :].rearrange("p (c n) -> p c n", c=2)[:, :, 0:nb]

        def stage_proj(l, b, hook_out=None, hook_pre_b=None, hook_early=None, hook_conv_done=None, reload_next=False):
            info = blk_info(b)
            par, nb, t0, smp = (l * NBLK + b) % 2, info["nb"], info["t0"], info["sample"]
            ntl = len(info["tiles"])
            hTr = tuple(("hT", par, tl) for tl in range(ntl))
            state_tile = smp or b == 7
            sl = slice(128, 256) if b == 7 else slice(0, 128)

            def gl_dst():
                if smp:
                    return gluext_s[:, :, :, 30:38]
                return glx[par][:, :, 30:30 + nb]

            def hb_dst():
                if smp:
                    return hbext_s[:, :, :, 2:10]
                return hbx[par][:, :, 2:2 + nb]

            def shp(ap):
                return ap.rearrange("p c (q t) -> p c q t", t=TS) if smp else ap

            if not smp:
                if b == 0:
                    P.op("pool", lambda e: e.memset(glx[par][:, :, 0:30], 0.0), writes=(("glx", par),))
                    P.op("pool", lambda e: e.memset(hbx[par][:, :, 0:2], 0.0), writes=(("hbx", par),))
                else:
                    P.op("pool", lambda e: e.tensor_copy(out=glx[par][:, :, 0:30], in_=glx[1 - par][:, :, 256:286]),
                         reads=(("glx", 1 - par),), writes=(("glx", par),))
                    P.op("pool", lambda e: e.tensor_copy(out=hbx[par][:, :, 0:2], in_=hbx[1 - par][:, :, 256:258]),
                         reads=(("hbx", 1 - par),), writes=(("hbx", par),))
            bk_gate = proj_cm(l, b, 1)
            P.op("act", lambda e: e.activation(out=th[:, :, 0:nb], in_=psv(bk_gate, nb), func=AF.Tanh, scale=0.5),
                 reads=(("ps", bk_gate),), writes=("th",))
            bk_val = proj_cm(l, b, 0)
            P.op("dve", lambda e: e.scalar_tensor_tensor(out=gl_dst(), in0=shp(th[:, :, 0:nb]), scalar=1.0,
                                                         in1=shp(psv(bk_val, nb)), op0=ALU.add, op1=ALU.mult),
                 reads=("th", ("ps", bk_val)), writes=(("glx_s",) if smp else ("glx", par),))
            if state_tile:
                P.op("dve", lambda e: e.scalar_tensor_tensor(out=gluf[:], in0=th[:, :, sl], scalar=1.0,
                                                             in1=psv(bk_val, nb)[:, :, sl], op0=ALU.add, op1=ALU.mult),
                     reads=("th", ("ps", bk_val)), writes=("gluf",))
            bk_za = proj_cm(l, b, 2)
            P.op("act", lambda e: e.activation(out=sza[:, :, 0:nb], in_=psv(bk_za, nb), func=AF.Silu),
                 reads=(("ps", bk_za),), writes=("sza",))
            bk_zb = proj_cm(l, b, 6)
            P.op("act", lambda e: e.activation(out=szb[:, :, 0:nb], in_=psv(bk_zb, nb), func=AF.Silu),
                 reads=(("ps", bk_zb),), writes=("szb",))
            bk_bx = proj_cm(l, b, 5)
            P.op("act", lambda e: e.activation(out=bx[:, :, 0:nb], in_=psv(bk_bx, nb), func=AF.Copy),
                 reads=(("ps", bk_bx),), writes=("bx",))
            bk_bc = proj_cm(l, b, 4)
            P.op("dve", lambda e: e.tensor_tensor(out=hb_dst(), in0=shp(psv(bk_bc, nb)), in1=shp(bx[:, :, 0:nb]), op=ALU.mult),
                 reads=("bx", ("ps", bk_bc)), writes=(("hbx_s",) if smp else ("hbx", par),))
            if state_tile:
                P.op("dve", lambda e: e.tensor_tensor(out=hbf[:], in0=psv(bk_bc, nb)[:, :, sl], in1=bx[:, :, sl], op=ALU.mult),
                     reads=("bx", ("ps", bk_bc)), writes=("hbf",))
            bk_bb = proj_cm(l, b, 3)
            P.op("dve", lambda e: e.tensor_tensor(out=t1b[:, :, 0:nb], in0=psv(bk_bb, nb), in1=szb[:, :, 0:nb], op=ALU.mult),
                 reads=("szb", ("ps", bk_bb)), writes=("t1b",))
            if hook_early is not None:
                hook_early()
            bk_ca = alloc_bank()

            def convA(e):
                ins = None
                for c in range(2):
                    for k in range(KA):
                        if smp:
                            rhs = gluext_s[:, c, :, k:k + TS]
                            out = banks[bk_ca][:, c * 256:c * 256 + nb].rearrange("p (q t) -> p q t", t=TS)
                        else:
                            rhs = glx[par][:, c, k:k + nb]
                            out = banks[bk_ca][:, c * 256:c * 256 + nb]
                        ins = e.matmul(out, lhsT=diagA[:, c, k, :], rhs=rhs, start=(k == 0), stop=(k == KA - 1))
                return ins
            glu_reads = (("glx_s",), "gluext_s_st", "diagA") if smp else (("glx", par), "diagA")
            P.op("pe", convA, reads=glu_reads, writes=(("ps", bk_ca),))

            def ya_evac(e):
                ins = None
                for c in range(2):
                    ins = e.activation(out=ya[:, c, 0:nb], in_=banks[bk_ca][:, c * 256:c * 256 + nb], func=AF.Identity,
                                       bias=pcol[:, 0, l, c:c + 1])
                return ins
            P.op("act", ya_evac, reads=(("ps", bk_ca), ("pcol", 0)), writes=("ya",))
            bk_cb = alloc_bank()

            def convB(e):
                ins = None
                for c in range(2):
                    for k in range(KB):
                        if smp:
                            rhs = hbext_s[:, c, :, k:k + TS]
                            out = banks[bk_cb][:, c * 256:c * 256 + nb].rearrange("p (q t) -> p q t", t=TS)
                        else:
                            rhs = hbx[par][:, c, k:k + nb]
                            out = banks[bk_cb][:, c * 256:c * 256 + nb]
                        ins = e.matmul(out, lhsT=diagB[:, c, k, :], rhs=rhs, start=(k == 0), stop=(k == KB - 1))
                return ins
            hb_reads = (("hbx_s",), "hbext_s_st", "diagB") if smp else (("hbx", par), "diagB")
            P.op("pe", convB, reads=hb_reads, writes=(("ps", bk_cb),))
            P.op("dve", lambda e: e.tensor_tensor(out=mix[par][:, 2:4, 0:nb], in0=psv(bk_cb, nb), in1=t1b[:, :, 0:nb], op=ALU.mult),
                 reads=("t1b", ("ps", bk_cb)), writes=(("mix", par, 1),))
            bk_tm = []
            for tl in range(ntl):
                bk = alloc_bank()

                def ptm(e, bk=bk, tl=tl):
                    ins = None
                    for k in range(8):
                        rhs = win[:, k, 8 * G:12 * G].rearrange("p (a s) -> p a s", a=2)[:, :, 0:G]
                        ins = e.matmul(banks[bk][:, :].rearrange("p (a s) -> p a s", a=2), lhsT=hT[par][:, k, tl * 128:(tl + 1) * 128],
                                       rhs=rhs, start=(k == 0), stop=(k == 7))
                    return ins
                P.op("pe", ptm, reads=(("win", 8), ("win", 10), ("hT", par, tl)), writes=(("ps", bk),))
                bk_tm.append(bk)
            if b == NBLK - 1 and l + 1 < DEPTH:
                reload_slab(l + 1, 8)
                reload_slab(l + 1, 10)
            late_pe = []
            bk_sp_box = []
            bk_pl_box = []
            cb = new_col(4 * ntl)
            for tl, ti in enumerate(info["tiles"]):
                bk = bk_tm[tl]
                cslot = tl
                dslot = ti % 3
                P.op("act", lambda e, bk=bk, dslot=dslot: e.activation(out=dxT[dslot][:], in_=banks[bk][:, G:2 * G], func=AF.Copy),
                     reads=(("ps", bk),), writes=(("dxT", dslot),))
                P.op("dve", lambda e, bk=bk, cslot=cslot: e.bn_stats(out=bnst[cslot][:], in_=banks[bk][:, 0:G]),
                     reads=(("ps", bk),), writes=(("bnst", cslot),))
                P.op("dve", lambda e, cslot=cslot, cc=cb + 2 * tl: e.bn_aggr(out=stat[:, cc:cc + 2], in_=bnst[cslot][:]),
                     reads=(("bnst", cslot),), writes=(("stat", cb + 2 * tl),))
            P.op("dve", lambda e: e.tensor_scalar(out=stat[:, cb + 2 * ntl:cb + 3 * ntl], in0=stat[:, cb + 1:cb + 2 * ntl:2],
                                                  scalar1=EPS, scalar2=None, op0=ALU.add),
                 reads=tuple(("stat", cb + 2 * tl) for tl in range(ntl)), writes=(("statv", cb),))
            rsqrt_chain(stat[:, cb + 2 * ntl:cb + 3 * ntl], stat[:, cb + 3 * ntl:cb + 4 * ntl],
                        stat[:, 508:508 + ntl], stat[:, 510:510 + ntl], ("statv", cb), ("staty", cb), "stat_tmp")
            for tl, ti in enumerate(info["tiles"]):
                bk = bk_tm[tl]
                cslot = tl
                dslot = ti % 3
                P.op("dve", lambda e, bk=bk, cslot=cslot, cm=cb + 2 * tl, cr=cb + 3 * ntl + tl: e.tensor_scalar(
                    out=zc[cslot][:], in0=banks[bk][:, 0:G], scalar1=stat[:, cm:cm + 1], scalar2=stat[:, cr:cr + 1],
                    op0=ALU.subtract, op1=ALU.mult),
                     reads=(("ps", bk), ("stat", cb + 2 * tl), ("staty", cb)), writes=(("zc", cslot),))
                veng = "dve" if smp else "pool"
                P.op(veng, lambda e, cslot=cslot: e.tensor_tensor(out=vn1[cslot][:], in0=zc[cslot][:], in1=lncg[l][:], op=ALU.mult),
                     reads=(("zc", cslot), ("lncg", l)), writes=(("vn1", cslot),))
                if smp:
                    P.op("dve", lambda e, cslot=cslot: e.tensor_tensor(out=vnf[:], in0=vn1[cslot][:], in1=lncb[l][:], op=ALU.add),
                         reads=(("vn1", cslot), ("lncb", l)), writes=("vnf",))
                    P.op("dve", lambda e, cslot=cslot: e.tensor_copy(out=vnb[cslot][:], in_=vnf[:]),
                         reads=("vnf",), writes=(("vnb", cslot),))
                    ld("sp", O["cv_s"][l], vnf[:], ("o", "cv_s", l), "vnf")
                else:
                    P.op("pool", lambda e, cslot=cslot: e.tensor_tensor(out=vnb[cslot][:], in0=vn1[cslot][:], in1=lncb[l][:], op=ALU.add),
                         reads=(("vn1", cslot), ("lncb", l)), writes=(("vnb", cslot),))

                def spat(e, cslot=cslot, tl=tl):
                    ins = None
                    for h in range(4):
                        r0 = (h % 2) * 64
                        out = banks[bk_sp_box[0]][r0:r0 + 64, (h // 2) * 256 + tl * 128:(h // 2) * 256 + (tl + 1) * 128]
                        wt = WmTs[l] if smp else WmT[l]
                        e.matmul(out, lhsT=vnb[cslot][:, h * 64:(h + 1) * 64], rhs=wt[:, h, :], start=True, stop=False)
                        if smp:
                            o3 = out.rearrange("p (q t) -> p q t", t=TS)
                            ins = e.matmul(o3, lhsT=sel4[:, h, :], rhs=bs8[l][:, 0:TS].unsqueeze(1).to_broadcast([128, NSEQ_S, TS]),
                                           start=False, stop=True)
                        else:
                            ins = e.matmul(out, lhsT=sel4[:, h, :], rhs=bs8[l][:, :], start=False, stop=True)
                    return ins
                late_pe.append((spat, (("vnb", cslot), ("WmTs", l) if smp else ("WmT", l), "sel4", ("bs8", l)), (("ps_sp",),)))
                if ti >= 15:
                    P.op("dve", lambda e, bk=bk: e.tensor_copy(out=dxf[:], in_=banks[bk][:, G:2 * G]),
                         reads=(("ps", bk),), writes=("dxf",))
                    if smp:
                        for q in range(NSEQ_S):
                            ld("sp", O["pd_s"][l, q, 7:15, :], dxf[q * TS:(q + 1) * TS, :], ("o", "pd_s1", l, q), "dxf")
                    else:
                        ld("sp", O["pd_p"][l], dxf[113:128, :], ("o", "pd_p", l), "dxf")

                def poolmm(e, dslot=dslot, tl=tl, ti=ti):
                    ins = None
                    pslot = (ti - 1) % 3
                    for g in range(4):
                        r0 = (g % 2) * 64
                        out = banks[bk_pl_box[0]][r0:r0 + 64, (g // 2) * 256 + tl * 128:(g // 2) * 256 + (tl + 1) * 128]
                        lh = dxT[dslot][:, g * 64:(g + 1) * 64]
                        if smp:
                            e.matmul(out, lhsT=lh, rhs=bscur[:, g, :], start=True, stop=False)
                            e.matmul(out, lhsT=spool[:, 0, g * 64:(g + 1) * 64], rhs=bsbuf[:, 0, g, :], start=False, stop=False)
                            ins = e.matmul(out, lhsT=spool[:, 1, g * 64:(g + 1) * 64], rhs=bsbuf[:, 1, g, :], start=False, stop=True)
                        elif ti == 0:
                            e.matmul(out, lhsT=lh, rhs=bcur0h[:, g, :], start=True, stop=False)
                            ins = e.matmul(out, lhsT=lh, rhs=bcur0l[:, g, :], start=False, stop=True)
                        else:
                            e.matmul(out, lhsT=lh, rhs=bcur[:, g, :], start=True, stop=False)
                            ins = e.matmul(out, lhsT=dxT[pslot][:, g * 64:(g + 1) * 64], rhs=bprev[:, g, :], start=False, stop=True)
                    return ins
                prd = (("dxT", dslot),) + ((("dxT", (ti - 1) % 3),) if (not smp and ti > 0) else ())
                late_pe.append((poolmm, prd + ("bcur", "bprev", "bcur0h", "bcur0l", "bscur", "bsbuf", "spool"), (("ps_pl",),)))
            if hook_conv_done is not None:
                hook_conv_done()
            bk_yc = 7

            def centre(e):
                ins = None
                for co in range(2):
                    for ci in range(2):
                        ins = e.matmul(banks[bk_yc][:, co * 256:co * 256 + nb], lhsT=cmat[:, 0 if ci == co else 1, :],
                                       rhs=ya[:, ci, 0:nb], start=(ci == 0), stop=(ci == 1))
                return ins
            P.op("pe", centre, reads=("ya", "cmat"), writes=(("ps", bk_yc),))
            P.op("act", lambda e: e.activation(out=ycsq[:, :, 0:nb], in_=psv(bk_yc, nb), func=AF.Square),
                 reads=(("ps", bk_yc),), writes=("ycsq",))
            if hook_pre_b is not None:
                hook_pre_b()
            if hook_out is not None:
                hook_out(0)
            bk_var = alloc_bank()

            def varmm(e):
                ins = None
                for ci in range(2):
                    ins = e.matmul(banks[bk_var][:, 0:nb], lhsT=ones256[:], rhs=ycsq[:, ci, 0:nb], start=(ci == 0), stop=(ci == 1))
                return ins
            P.op("pe", varmm, reads=("ycsq", "ones256"), writes=(("ps", bk_var),))
            P.op("dve", lambda e: e.tensor_scalar(out=varsb[:, 0:nb], in0=banks[bk_var][:, 0:nb], scalar1=EPS, scalar2=None,
                                                  op0=ALU.add),
                 reads=(("ps", bk_var),), writes=("varsb",))
            rsqrt_chain(varsb[:, 0:nb], rstdA[:, 0:nb], yn[:, 0, 0:nb], yn[:, 1, 0:nb], "varsb", "rstdA", "yn", newton_eng="dve")
            bk_pl = alloc_bank()
            bk_pl_box.append(bk_pl)

            def emit_late(tag):
                for fn, rds, wrs in late_pe:
                    if wrs[0] != (tag,):
                        continue
                    bkx = bk_sp_box[0] if tag == "ps_sp" else bk_pl
                    P.op("pe", fn, reads=rds, writes=(("ps", bkx),))
            emit_late("ps_pl")
            P.op("act", lambda e: e.activation(out=pooled[:, :, 0:nb], in_=psv(bk_pl, nb), func=AF.Copy),
                 reads=(("ps", bk_pl),), writes=("pooled",))
            bk_zc = proj_cm(l, b, 9)
            P.op("act", lambda e: e.activation(out=szc[:, :, 0:nb], in_=psv(bk_zc, nb), func=AF.Silu),
                 reads=(("ps", bk_zc),), writes=("szc",))
            bk_cu = proj_cm(l, b, 7)
            P.op("dve", lambda e: e.tensor_tensor(out=t1c[:, :, 0:nb], in0=psv(bk_cu, nb), in1=szc[:, :, 0:nb], op=ALU.mult),
                 reads=("szc", ("ps", bk_cu)), writes=("t1c",))
            bk_zd = proj_cm(l, b, 11)
            P.op("act", lambda e: e.activation(out=szd[:, :, 0:nb], in_=psv(bk_zd, nb), func=AF.Silu),
                 reads=(("ps", bk_zd),), writes=("szd",))
            if hook_out is not None:
                hook_out(1)
            very_last = (l == DEPTH - 1 and b == NBLK - 1)
            def emit_lnA_tail():
                P.op("dve", lambda e: e.tensor_tensor(out=yn[:, :, 0:nb], in0=psv(bk_yc, nb),
                                                      in1=rstdA[:, 0:nb].unsqueeze(1).to_broadcast([128, 2, nb]), op=ALU.mult),
                     reads=(("ps", bk_yc), "rstdA"), writes=("yn",))

                def s1f(e):
                    ins = None
                    for c in range(2):
                        ins = e.activation(out=s1[:, c, 0:nb], in_=yn[:, c, 0:nb], func=AF.Silu,
                                           scale=pcol[:, 1, l, c:c + 1], bias=pcol[:, 2, l, c:c + 1])
                    return ins
                P.op("act", s1f, reads=("yn", ("pcol", 1), ("pcol", 2)), writes=("s1",))
                P.op("pool", lambda e: e.tensor_tensor(out=mix[par][:, 0:2, 0:nb], in0=s1[:, :, 0:nb], in1=sza[:, :, 0:nb], op=ALU.mult),
                     reads=("s1", "sza"), writes=(("mix", par, 0),))
            if very_last:
                emit_lnA_tail()
            bk_sp = alloc_bank()
            bk_sp_box.append(bk_sp)
            emit_late("ps_sp")
            P.op("dve", lambda e: e.tensor_tensor(out=mix[par][:, 4:6, 0:nb], in0=psv(bk_sp, nb), in1=t1c[:, :, 0:nb], op=ALU.mult),
                 reads=("t1c", ("ps", bk_sp)), writes=(("mix", par, 2),))
            bk_wp = alloc_bank()

            def wpmm(e):
                ins = None
                for c in range(2):
                    ins = e.matmul(banks[bk_wp][:, c * 256:c * 256 + nb], lhsT=wpbd[l][:, c, :], rhs=pooled[:, c, 0:nb], start=True, stop=True)
                return ins
            P.op("pe", wpmm, reads=("pooled", ("wpbd", l)) + tuple(("wpbd", l, g) for g in range(4)), writes=(("ps", bk_wp),))

            def ydf(e):
                ins = None
                for c in range(2):
                    ins = e.scalar_tensor_tensor(out=mix[par][:, 6 + c, 0:nb], in0=banks[bk_wp][:, c * 256:c * 256 + nb],
                                                 scalar=pcol[:, 3, l, c:c + 1], in1=szd[:, c, 0:nb], op0=ALU.mult, op1=ALU.mult)
                return ins
            P.op("dve", ydf, reads=("szd", ("ps", bk_wp), ("pcol", 3)), writes=(("mix", par, 3),))
            if not very_last:
                emit_lnA_tail()
            if state_tile:
                for nm, src, scl, stg, rstg in (("ca", gluf, 0.5, sttm, "sttm"), ("cb", hbf, 1.0, zc[1], ("zc", 1))):
                    bk = alloc_bank()

                    def trs(e, bk=bk, src=src):
                        ins = None
                        for c in range(2):
                            ins = e.transpose(banks[bk][:, c * 128:(c + 1) * 128], src[:, c, :], ident_f[:])
                        return ins
                    P.op("pe", trs, reads=("gluf" if nm == "ca" else "hbf", "ident_f"), writes=(("ps", bk),))
                    P.op("dve", lambda e, bk=bk, scl=scl, stg=stg: e.tensor_scalar(out=stg[:], in0=banks[bk][:, 0:G], scalar1=scl,
                                                                                   scalar2=None, op0=ALU.mult),
                         reads=(("ps", bk),), writes=(rstg,))
                    if nm == "ca":
                        if smp:
                            for q in range(NSEQ_S):
                                ld("sp", O["ca_s"][l, q, 22:30, :], stg[q * TS:(q + 1) * TS, :], ("o", "ca_s1", l, q), rstg)
                        else:
                            ld("sp", O["ca_p"][l], stg[98:128, :], ("o", "ca_p", l), rstg)
                    else:
                        if smp:
                            for r in range(2):
                                ld("sp", O["cb_s"][l, :, r, :], stg[6 + r:128:8, :], ("o", "cb_s", l, r), rstg)
                        else:
                            ld("sp", O["cb_p"][l], stg[126:128, :], ("o", "cb_p", l), rstg)

        def stage_out(l, b, which=None):
            info = blk_info(b)
            par = (l * NBLK + b) % 2
            for tl, ti in enumerate(info["tiles"]):
                if which is not None and tl != which:
                    continue
                slot = xres_rr[0] % 2
                xres_rr[0] += 1
                rd = (("y0", ti),) if l == 1 else ()
                final_tile = (l == DEPTH - 1 and b == NBLK - 1)
                P.dma("act", lambda e, slot=slot, ti=ti: e.dma_start(out=xres[slot][:], in_=src_rows(l, ti)),
                      reads=rd, writes=(("xres", slot, 0), ("xres", slot, 1)))
                bks = []
                c = new_col(5)
                for hf in range(2):
                    bk = 5 + hf

                    def omm(e, bk=bk, hf=hf, tl=tl):
                        ins = None
                        for k in range(8):
                            ins = e.matmul(banks[bk][:, :], lhsT=mix[par][:, k, tl * 128:(tl + 1) * 128],
                                           rhs=wout[:, k, hf * 512:(hf + 1) * 512], start=(k == 0), stop=(k == 7))
                        return ins
                    P.op("pe", omm, reads=tuple(("mix", par, j) for j in range(4)) + tuple(("wout", j) for j in range(4)),
                         writes=(("ps", bk),))
                    P.op("act", lambda e, bk=bk, hf=hf, c=c: e.activation(out=ptmp[hf][:, :].bitcast(BF16)[:, 0:512], in_=banks[bk][:, :], func=AF.Square,
                                                                         accum_out=stat[:, c + hf:c + hf + 1]),
                         reads=(("ps", bk),), writes=(("ptmp", hf), ("stat", c + hf)))
                    bks.append(bk)
                P.op("dve", lambda e, c=c: e.tensor_scalar(out=stat[:, c + 2:c + 3], in0=stat[:, c:c + 1], scalar1=stat[:, c + 1:c + 2],
                                                           scalar2=None, op0=ALU.add),
                     reads=(("stat", c), ("stat", c + 1)), writes=(("stat", c + 2),))
                rstd_from(c + 2, c + 3, c + 4, 1.0 / D)
                for hf in range(2):
                    bk = bks[hf]
                    P.op("dve", lambda e, bk=bk, hf=hf, c=c: e.scalar_tensor_tensor(
                        out=ptmp[hf][:], in0=banks[bk][:, :], scalar=stat[:, c + 4:c + 5], in1=gpost[:, hf * 512:(hf + 1) * 512],
                        op0=ALU.mult, op1=ALU.mult),
                         reads=(("ps", bk), ("stat", c + 4), "gpost"), writes=(("ptmp", hf),))
                    aeng = "dve" if (final_tile and hf == 1) else "pool"
                    P.op(aeng, lambda e, hf=hf, slot=slot: e.tensor_tensor(
                        out=xres[slot][:, hf * 512:(hf + 1) * 512], in0=ptmp[hf][:], in1=xres[slot][:, hf * 512:(hf + 1) * 512],
                        op=ALU.add),
                         reads=(("ptmp", hf), ("xres", slot, hf)), writes=(("xres", slot, hf),))
                P.dma("sp", lambda e, slot=slot, ti=ti: e.dma_start(out=dst_rows(l, ti), in_=xres[slot][:]),
                      reads=(("xres", slot, 0), ("xres", slot, 1)), writes=((("y0", ti),) if l == 0 else (("o", "y", ti),)))

        pre_load(0, 0)
        load_layer_params(0, "gpre")
        load_consts("early")
        load_layer_params(0, "win")
        load_consts("mid")
        pre_a(0, 0)
        load_consts("rest")
        P.mute_ops = True
        bank_save0 = bank_rr[0]
        load_layer_params(0, "prep")
        bank_rr[0] = bank_save0
        P.mute_ops = False
        load_layer_params(0, "wout")
        pre_b(0, 0)
        P.mute_dma = True
        prep_sel[0] = "cw"
        load_layer_params(0, "prep")
        prep_sel[0] = None
        P.mute_dma = False
        load_layer_params(0, "diag")
        P.op("dve", lambda e: e.tensor_tensor(out=wm[:], in0=spw[:], in1=tril[:, :].unsqueeze(1).to_broadcast([128, 4, 128]),
                                              op=ALU.mult), reads=("spw", "tril"), writes=("wm",))
        wm_done[0] = True

        def nxt(l, b):
            if b + 1 < NBLK:
                return (l, b + 1)
            if l + 1 < DEPTH:
                return (l + 1, 0)
            return None

        def prv(l, b):
            if b > 0:
                return (l, b - 1)
            if l > 0:
                return (l - 1, NBLK - 1)
            return None

        for l in range(DEPTH):
            for b in range(NBLK):
                n1 = nxt(l, b)
                p1 = prv(l, b)
                if n1 is not None:
                    if n1[1] == 0:
                        load_layer_params(n1[0], "gpre")
                    pre_load(*n1)
                if 2 <= b <= 6:
                    P.mute_ops = True
                    load_layer_params(l, ("state", b - 2, "a"))
                    P.mute_ops = False
                last = (b == NBLK - 1 and l + 1 < DEPTH)
                bg = (l + 1 < DEPTH)

                def tick(l=l, bg=bg):
                    if bg:
                        wc_tick(l + 1)

                def early(l=l, b=b, n1=n1, tick=tick):
                    if l == 0 and b == 0:
                        P.mute_dma = True
                        prep_sel[0] = "rest"
                        load_layer_params(0, "prep")
                        prep_sel[0] = None
                        P.mute_dma = False
                    if b == 3 and l + 1 < DEPTH:
                        P.mute_dma = True
                        load_layer_params(l + 1, "prep")
                        P.mute_dma = False
                    if n1 is not None:
                        pre_a(*n1)
                    if 3 <= b <= 7:
                        load_layer_params(l, ("state", b - 3, "b"))
                    tick()

                def conv_done(l=l, b=b, last=last, tick=tick):
                    if 2 <= b <= 6:
                        P.mute_dma = True
                        load_layer_params(l, ("state", b - 2, "a"))
                        P.mute_dma = False
                    tick()

                def out_hook(tl, l=l, b=b, p1=p1, tick=tick):
                    if p1 is not None:
                        ntl = len(blk_info(p1[1])["tiles"])
                        if ntl == 1:
                            if tl == 1:
                                stage_out(p1[0], p1[1], 0)
                        elif tl < ntl:
                            stage_out(p1[0], p1[1], tl)
                        if tl == 1 and p1[0] != l:
                            load_layer_params(l, "wout")
                    tick()
                if b == 0 and l > 0:
                    load_layer_params(l, "diag")
                stage_proj(l, b,
                           hook_out=out_hook,
                           hook_pre_b=(lambda n1=n1: pre_b(*n1)) if n1 is not None else None,
                           hook_early=early,
                           hook_conv_done=conv_done)
                if b == 1 and l + 1 < DEPTH:
                    P.mute_ops = True
                    bank_save = bank_rr[0]
                    load_layer_params(l + 1, "prep")
                    bank_rr[0] = bank_save
                    P.mute_ops = False
        stage_out(DEPTH - 1, NBLK - 1)

        with nc.allow_non_contiguous_dma(reason="tiny parameter columns / strided state rows"):
            P.emit(nc, es)
    return nc


_NC_CACHE = {}


def kernel(x_prompt, x_sample, state_conv_a, state_conv_b, state_pool, pre_norm_g, w_in, conv_a_w, conv_a_b,
           ln_a_g, ln_a_b, conv_b_w, ln_c_g, ln_c_b, spatial_w, spatial_b, pool_w, pool_scale, w_out, post_norm_g):
    n = 8
    f = lambda a: np.ascontiguousarray(np.asarray(a, dtype=np.float32))
    consts = _consts()
    shared = {
        "pre_g": f(pre_norm_g), "w_in": f(w_in), "conv_a_w": f(conv_a_w), "conv_a_b": f(conv_a_b),
        "ln_a_g": f(ln_a_g), "ln_a_b": f(ln_a_b), "conv_b_w": f(conv_b_w), "ln_c_g": f(ln_c_g), "ln_c_b": f(ln_c_b),
        "spatial_w": f(spatial_w), "spatial_b": f(spatial_b), "pool_w": f(pool_w), "pool_scale": f(pool_scale),
        "w_out": f(w_out), "post_g": f(post_norm_g),
    }
    for k, v in consts.items():
        shared["c_" + k] = f(v)
    x_prompt, x_sample = f(x_prompt), f(x_sample)
    sa, sbb, spl = f(state_conv_a), f(state_conv_b), f(state_pool)
    in_maps = []
    for i in range(n):
        m = dict(shared)
        s = slice(NSEQ_S * i, NSEQ_S * (i + 1))
        m["xp"] = x_prompt[i]
        m["xs"] = x_sample[s].reshape(128, D)
        m["sa"] = np.ascontiguousarray(sa[:, s].reshape(DEPTH, NSEQ_S * 30, G))
        m["sb"] = np.ascontiguousarray(sbb[:, s].reshape(DEPTH, NSEQ_S * 2, G))
        m["spl"] = np.ascontiguousarray(spl[:, s].reshape(DEPTH, NSEQ_S * 15, G))
        in_maps.append(m)
    if "nc" not in _NC_CACHE:
        _NC_CACHE["nc"] = build_nc()
    res = run_bass_kernel_spmd(_NC_CACHE["nc"], in_maps, core_ids=list(range(n)))
    R = res.results
    yp = np.stack([R[i]["yp"] for i in range(n)], 0)
    ys = np.concatenate([R[i]["ys"].reshape(NSEQ_S, TS, D) for i in range(n)], 0)
    ca_p = np.stack([R[i]["ca_p"] for i in range(n)], 1)
    ca_s = np.concatenate([R[i]["ca_s"] for i in range(n)], 1)
    cb_p = np.stack([R[i]["cb_p"] for i in range(n)], 1)
    cb_s = np.concatenate([R[i]["cb_s"] for i in range(n)], 1)
    pd_p = np.stack([R[i]["pd_p"] for i in range(n)], 1)
    pd_s = np.concatenate([R[i]["pd_s"] for i in range(n)], 1)
    cv_s = np.concatenate([R[i]["cv_s"].reshape(DEPTH, NSEQ_S, TS, G) for i in range(n)], 1)
    return tuple(np.ascontiguousarray(a.astype(np.float32)) for a in (yp, ys, ca_p, ca_s, cb_p, cb_s, pd_p, pd_s, cv_s))
```

```python
import contextlib
import numpy as np
import concourse.bass as bass
import concourse.mybir as mybir
from concourse.bass_utils import run_bass_kernel_spmd

F32 = mybir.dt.float32
BF16 = mybir.dt.bfloat16
I32 = mybir.dt.int32
RSQ_MAGIC = 1597463007
AF = mybir.ActivationFunctionType
ALU = mybir.AluOpType

D = 1024
G = 256
SEQ = 2048
NSEQ_S = 16
TS = 8
DEPTH = 2
EPS = 1e-6
KA = 31
KB = 3
NBLK = 9
WINS = (2, 4, 8, 16)


class Prog:
    ENG = ("pe", "act", "dve", "pool", "sp")
    CH = 3000
    NDS = 24

    def __init__(self):
        self.ops = {e: [] for e in self.ENG}
        self.cnt = {e: 0 for e in self.ENG}
        self.lastw = {}
        self.readers = {}
        self.known = {e: {} for e in self.ENG}
        self.known_dma = {e: set() for e in self.ENG}
        self.ndma = {"sp": 0, "pool": 0, "act": 0}

    def _collect(self, eng, reads, writes, is_dma):
        raw = set()
        other = set()
        for r in reads:
            if r in self.lastw:
                raw.add(self.lastw[r])
        for w in writes:
            if w in self.lastw:
                other.add(self.lastw[w])
            for rd in self.readers.get(w, ()):
                other.add(rd)
        waits = []
        for tok in sorted(raw | other, key=str):
            if tok[0] == "dma":
                if tok in self.known_dma[eng]:
                    continue
                self.known_dma[eng].add(tok)
                waits.append(tok)
                continue
            if tok[0] == eng and not is_dma:
                if eng == "pe":
                    continue
            if self.known[eng].get(tok[0], 0) >= tok[1]:
                continue
            self.known[eng][tok[0]] = tok[1]
            waits.append(tok)
        return waits

    seq = 0
    last_touch = None

    def _update(self, tok, reads, writes):
        if self.last_touch is None:
            self.last_touch = {}
        self.seq += 1
        for r in reads:
            self.readers.setdefault(r, []).append(tok)
            self.last_touch[r] = self.seq
        for w in writes:
            self.lastw[w] = tok
            self.readers[w] = []
            self.last_touch[w] = self.seq

    @staticmethod
    def _excl(reads, writes):
        ps = tuple(r for r in reads if isinstance(r, tuple) and r and r[0] == "ps")
        if ps:
            writes = tuple(writes) + tuple(p for p in ps if p not in writes)
        return reads, writes

    mute_ops = False
    mute_dma = False

    def op(self, eng, fn, reads=(), writes=()):
        if self.mute_ops:
            return
        reads, writes = self._excl(reads, writes)
        idx = self.cnt[eng] + 1
        self.cnt[eng] = idx
        waits = self._collect(eng, reads, writes, False)
        self.ops[eng].append(("op", waits, fn, idx))
        self._update((eng, idx), reads, writes)

    def dma(self, queue, fn, reads=(), writes=()):
        if self.mute_dma:
            return
        j = self.ndma[queue]
        self.ndma[queue] += 1
        waits = self._collect(queue, reads, writes, True)
        prev = ("dma", queue, j - self.NDS)
        if j >= self.NDS and prev not in self.known_dma[queue]:
            self.known_dma[queue].add(prev)
            waits.append(prev)
        self.ops[queue].append(("dma", waits, fn, j))
        self._update(("dma", queue, j), reads, writes)

    def emit(self, nc, es):
        sem = {}
        for e in self.ENG:
            n = (self.cnt[e] + self.CH - 1) // self.CH + 1
            sem[e] = [es.enter_context(nc.semaphore("s_%s_%d" % (e, i))) for i in range(n)]
        dsem = {q: [es.enter_context(nc.semaphore("s_dma_%s_%d" % (q, i))) for i in range(self.NDS)]
                for q in ("sp", "pool", "act")}

        def do_wait(e, tok):
            if tok[0] == "dma":
                q, j = tok[1], tok[2]
                e.wait_ge(dsem[q][j % self.NDS], 16 * (j // self.NDS + 1))
            else:
                i = tok[1] - 1
                e.wait_ge(sem[tok[0]][i // self.CH], (i % self.CH) + 1)

        def run(name, e):
            for kind, waits, fn, idx in self.ops[name]:
                for w in waits:
                    do_wait(e, w)
                ins = fn(e)
                if kind == "op":
                    ins.then_inc(sem[name][(idx - 1) // self.CH], 1)
                else:
                    ins.then_inc(dsem[name][idx % self.NDS], 16)
            if name == "sp":
                for q in ("sp", "pool", "act"):
                    for j in range(max(0, self.ndma[q] - self.NDS), self.ndma[q]):
                        do_wait(e, ("dma", q, j))

        block = es.enter_context(nc.Block())

        @block.tensor
        def _(e):
            run("pe", e)

        @block.scalar
        def _(e):
            run("act", e)

        @block.vector
        def _(e):
            run("dve", e)

        @block.gpsimd
        def _(e):
            run("pool", e)

        @block.sync
        def _(e):
            run("sp", e)


def _consts():
    c = {}
    c["ident"] = np.eye(128, dtype=np.float32)
    t = np.arange(128)
    c["tril"] = (t[None, :] <= t[:, None]).astype(np.float32)
    q = t // TS
    c["bdmask"] = (q[:, None] == q[None, :]).astype(np.float32)
    e8 = np.zeros((8, 128), np.float32)
    e8[t % TS, t] = 1.0
    c["e8"] = e8
    cd = np.eye(128, dtype=np.float32) - 1.0 / 256
    co = np.full((128, 128), -1.0 / 256, np.float32)
    c["cmat"] = np.stack([cd, co], 1)
    c["ones256"] = np.full((128, 128), 1.0 / 256, np.float32)
    bcur = np.zeros((128, 4, 128), np.float64)
    bprev = np.zeros((128, 4, 128), np.float64)
    bcur0 = np.zeros((128, 4, 128), np.float64)
    bscur = np.zeros((128, 4, 128), np.float64)
    bsbuf = np.zeros((120, 2, 4, 128), np.float64)
    for g, w in enumerate(WINS):
        for tt in range(128):
            for d in range(w):
                s = tt - d
                if s >= 0:
                    bcur[s, g, tt] += 1.0 / w
                    bcur0[s, g, tt] += 1.0 / min(tt + 1, w)
                else:
                    bprev[128 + s, g, tt] += 1.0 / w
            bcur[tt, g, tt] -= 1.0
            bcur0[tt, g, tt] -= 1.0
            qq, tl = tt // TS, tt % TS
            for d in range(w):
                s = tl - d
                if s >= 0:
                    bscur[qq * TS + s, g, tt] += 1.0 / w
                else:
                    r = 15 + s
                    if r >= 0:
                        bsbuf[(qq % 8) * 15 + r, qq // 8, g, tt] += 1.0 / w
            bscur[tt, g, tt] -= 1.0
    c["bcur"] = bcur.astype(np.float32)
    c["bprev"] = bprev.astype(np.float32)
    hi = bcur0.astype(np.float32)
    u = hi.view(np.uint32).astype(np.uint64)
    u = ((u + 0x7FFF + ((u >> 16) & 1)) >> 16) << 16
    hi_b = u.astype(np.uint32).view(np.float32)
    c["bcur0h"] = hi_b
    c["bcur0l"] = (bcur0 - hi_b).astype(np.float32)
    sel = np.zeros((128, 4, 64), np.float32)
    for h in range(4):
        sel[h, h, :] = 1.0
        sel[32 + h, h, :] = 1.0
    c["sel4"] = sel
    c["bscur"] = bscur.astype(np.float32)
    c["bsbuf"] = bsbuf.astype(np.float32)
    return c


CONST_SHAPES = {
    "ident": [128, 128], "tril": [128, 128], "bdmask": [128, 128], "e8": [8, 128],
    "cmat": [128, 2, 128], "ones256": [128, 128], "bcur": [128, 4, 128], "bprev": [128, 4, 128],
    "bcur0h": [128, 4, 128], "bcur0l": [128, 4, 128], "bscur": [128, 4, 128],
    "bsbuf": [120, 2, 4, 128], "sel4": [128, 4, 64],
}

IN_SHAPES = {
    "xp": [SEQ, D], "xs": [128, D],
    "sa": [DEPTH, NSEQ_S * 30, G], "sb": [DEPTH, NSEQ_S * 2, G], "spl": [DEPTH, NSEQ_S * 15, G],
    "pre_g": [DEPTH, D], "w_in": [DEPTH, D, 12 * G], "conv_a_w": [DEPTH, KA, G],
    "conv_a_b": [DEPTH, G], "ln_a_g": [DEPTH, G], "ln_a_b": [DEPTH, G], "conv_b_w": [DEPTH, KB, G],
    "ln_c_g": [DEPTH, G], "ln_c_b": [DEPTH, G], "spatial_w": [DEPTH, 4, 128, 128],
    "spatial_b": [DEPTH, 4, 128], "pool_w": [DEPTH, 4, 64, 64], "pool_scale": [DEPTH, G],
    "w_out": [DEPTH, D, D], "post_g": [DEPTH, D],
}

OUT_SHAPES = {
    "yp": [SEQ, D], "ys": [128, D],
    "ca_p": [DEPTH, 30, G], "ca_s": [DEPTH, NSEQ_S, 30, G],
    "cb_p": [DEPTH, 2, G], "cb_s": [DEPTH, NSEQ_S, 2, G],
    "pd_p": [DEPTH, 15, G], "pd_s": [DEPTH, NSEQ_S, 15, G],
    "cv_s": [DEPTH, 128, G],
}


def build_nc():
    nc = bass.Bass("TRN2", target_bir_lowering=False)
    I = {k: nc.dram_tensor(k, s, F32, kind="ExternalInput").ap() for k, s in IN_SHAPES.items()}
    C = {k: nc.dram_tensor("c_" + k, s, F32, kind="ExternalInput").ap() for k, s in CONST_SHAPES.items()}
    O = {k: nc.dram_tensor(k, s, F32, kind="ExternalOutput").ap() for k, s in OUT_SHAPES.items()}
    y0 = nc.dram_tensor("y0_scratch", [SEQ + 128, D], F32, kind="Internal").ap()
    wbf_in = nc.dram_tensor("wbf_in", [D, 12 * G], BF16, kind="Internal").ap()
    wbf_out = nc.dram_tensor("wbf_out", [D, D], BF16, kind="Internal").ap()

    P = Prog()
    es = contextlib.ExitStack()
    with es:
        def sb(name, shape, dt=F32):
            return es.enter_context(nc.sbuf_tensor(name, shape, dt))

        banks = [es.enter_context(nc.psum_tensor("ps%d" % i, [128, 512], F32)) for i in range(8)]
        banks_bf = [b.bitcast(BF16) for b in banks]
        bank_rr = [0]

        def alloc_bank():
            if P.last_touch is None:
                P.last_touch = {}
            b = min(range(5), key=lambda i: (P.last_touch.get(("ps", i), -1), i))
            P.seq += 1
            P.last_touch[("ps", b)] = P.seq
            bank_rr[0] += 1
            return b

        win = sb("win", [128, 8, 12 * G], BF16)
        wout = sb("wout", [128, 8, D], BF16)
        diagA = sb("diagA", [128, 2, KA, 128], BF16)
        diagB = sb("diagB", [128, 2, KB, 128], BF16)
        xin = [sb("xin%d" % i, [128, D]) for i in range(2)]
        xres = [sb("xres%d" % i, [128, D]) for i in range(2)]
        htm = [sb("htm%d" % i, [128, D], BF16) for i in range(2)]
        hT = [sb("hT%d" % i, [128, 8, 256], BF16) for i in range(2)]
        mix = [sb("mix%d" % i, [128, 8, 256], BF16) for i in range(2)]
        glx = [sb("glx%d" % i, [128, 2, 30 + 256], BF16) for i in range(2)]
        gluext_s = sb("gluext_s", [128, 2, NSEQ_S, 38], BF16)
        hbx = [sb("hbx%d" % i, [128, 2, 2 + 256], BF16) for i in range(2)]
        hbext_s = sb("hbext_s", [128, 2, NSEQ_S, 10], BF16)
        th = sb("th", [128, 2, 256], BF16)
        sza = sb("sza", [128, 2, 256], BF16)
        ya = sb("ya", [128, 2, 256], BF16)
        ycsq = sb("ycsq", [128, 2, 256], BF16)
        varsb = sb("varsb", [128, 256])
        rstdA = sb("rstdA", [128, 256])
        yn = sb("yn", [128, 2, 256])
        s1 = sb("s1", [128, 2, 256], BF16)
        gluf = sb("gluf", [128, 2, 128])
        bx = sb("bx", [128, 2, 256], BF16)
        szb = sb("szb", [128, 2, 256], BF16)
        t1b = sb("t1b", [128, 2, 256], BF16)
        hbf = sb("hbf", [128, 2, 128])
        zc = [sb("zc%d" % i, [128, G]) for i in range(2)]
        vn1 = [sb("vn1_%d" % i, [128, G]) for i in range(2)]
        vnf = sb("vnf", [128, G])
        vnb = [sb("vnb%d" % i, [128, G], BF16) for i in range(2)]
        szc = sb("szc", [128, 2, 256], BF16)
        t1c = sb("t1c", [128, 2, 256], BF16)
        dxT = [sb("dxT%d" % i, [128, G], BF16) for i in range(3)]
        dxf = sb("dxf", [128, G])
        pooled = sb("pooled", [128, 2, 256], BF16)
        szd = sb("szd", [128, 2, 256], BF16)
        ptmp = [sb("ptmp%d" % i, [128, 512]) for i in range(2)]
        sttm = sb("sttm", [128, G])
        stat = sb("stat", [128, 512])
        bnst = [sb("bnst%d" % i, [128, 6]) for i in range(2)]
        ident_f = sb("ident_f", [128, 128])
        ident_b = sb("ident_b", [128, 128], BF16)
        tril = sb("tril", [128, 128])
        bdmask = sb("bdmask", [128, 128])
        e8 = sb("e8", [8, 128], BF16)
        cmat = sb("cmat", [128, 2, 128], BF16)
        ones256 = sb("ones256", [128, 128], BF16)
        ones1 = sb("ones1", [1, 64], BF16)
        bcur = sb("bcur", [128, 4, 128], BF16)
        bprev = sb("bprev", [128, 4, 128], BF16)
        bcur0h = sb("bcur0h", [128, 4, 128], BF16)
        bcur0l = sb("bcur0l", [128, 4, 128], BF16)
        bscur = sb("bscur", [128, 4, 128], BF16)
        bsbuf = sb("bsbuf", [120, 2, 4, 128], BF16)
        wstg_f = [sb("wstg_f%d" % i, [128, 1024]) for i in range(2)]
        wstg_b = sb("wstg_b", [128, 1024], BF16)
        gpre = sb("gpre", [128, D])
        gpost = sb("gpost", [128, D])
        lncg = [sb("lncg%d" % i, [128, G]) for i in range(DEPTH)]
        lncb = [sb("lncb%d" % i, [128, G]) for i in range(DEPTH)]
        pcol = sb("pcol", [128, 4, DEPTH, 2])
        cw_raw = sb("cw_raw", [KA + KB, G])
        cwT = [sb("cwT%d" % i, [128, 2, KA + KB]) for i in range(DEPTH)]
        spw = sb("spw", [128, 4, 128])
        wm = sb("wm", [128, 4, 128], BF16)
        WmT = [sb("WmT%d" % i, [128, 4, 128], BF16) for i in range(DEPTH)]
        WmTs = [sb("WmTs%d" % i, [128, 4, 128], BF16) for i in range(DEPTH)]
        p1sb = sb("p1sb", [8, 128], BF16)
        bsf = sb("bsf", [36, 128])
        bshf = sb("bshf", [36, 128])
        bsh2 = sb("bsh2", [36, 128], BF16)
        bs8 = [sb("bs8_%d" % i, [128, 128], BF16) for i in range(DEPTH)]
        sel4 = sb("sel4", [128, 4, 64], BF16)
        wpbd = [sb("wpbd%d" % i, [128, 2, 128], BF16) for i in range(DEPTH)]
        sa_raw = sb("sa_raw", [120, G])
        sa_bf = sb("sa_bf", [120, G], BF16)
        sb_raw = sb("sb_raw", [32, G])
        sb_bf = sb("sb_bf", [32, G], BF16)
        spool = sb("spool", [120, 2, G], BF16)

        def ld(queue, dst, src, wname, rname=None):
            P.dma(queue, lambda e, dst=dst, src=src: e.dma_start(out=dst, in_=src),
                  reads=(rname,) if rname else (), writes=(wname,))

        def load_consts(part):
            if part == "early":
                P.op("pool", lambda e: e.memset(ones1[:], 1.0), writes=("ones1",))
                P.op("act", lambda e: e.activation(out=bnst[0][0:1, 0:6], in_=ones1[0:1, 0:6], func=AF.Silu),
                     reads=("ones1",), writes=(("bnst", 0),))
                ld("pool", ident_b[:], C["ident"], "ident_b")
                ld("pool", e8[:], C["e8"], "e8")
                ld("pool", cmat[:], C["cmat"], "cmat")
                ld("pool", ones256[:], C["ones256"], "ones256")
                return
            if part == "mid":
                ld("pool", sel4[:], C["sel4"], "sel4")
                ld("pool", bcur0h[:], C["bcur0h"], "bcur0h")
                ld("pool", bcur0l[:], C["bcur0l"], "bcur0l")
                ld("pool", bcur[:], C["bcur"], "bcur")
                ld("pool", bprev[:], C["bprev"], "bprev")
                P.op("pool", lambda e: e.memset(wpbd[0][:], 0.0), writes=(("wpbd", 0),))
                for g in range(4):
                    r0 = (g % 2) * 64
                    ld("pool", wpbd[0][r0:r0 + 64, g // 2, r0:r0 + 64], I["pool_w"][0, g], ("wpbd", 0, g), ("wpbd", 0))
                return
            ld("sp", ident_f[:], C["ident"], "ident_f")
            ld("sp", tril[:], C["tril"], "tril")
            ld("sp", bdmask[:], C["bdmask"], "bdmask")
            for j, nm in enumerate(("conv_a_b", "ln_a_g", "ln_a_b", "pool_scale")):
                ld("sp", pcol[:, j, :, :], I[nm].rearrange("l (c p) -> p l c", p=128), ("pcol", j))
            ld("pool", bscur[:], C["bscur"], "bscur")
            ld("pool", bsbuf[:], C["bsbuf"], "bsbuf")

        stat_col = [0]

        def new_col(n=1):
            c0 = stat_col[0]
            stat_col[0] += n
            assert stat_col[0] <= 508
            return c0

        prep_sel = [None]
        wm_done = [False]

        def load_layer_params(l, part):
            if part == "win":
                wv = I["w_in"][l].rearrange("(k p) e -> p k e", p=128)
                for s in (1, 0, 2, 6, 5, 4, 3, 8, 10, 9, 7, 11):
                    ld("pool", win[:, :, s * G:(s + 1) * G], wv[:, :, s * G:(s + 1) * G], ("win", s))
                return
            if part == "wout":
                if l == 0:
                    wo = I["w_out"][l].rearrange("(k p) e -> p k e", p=128)
                    for k in range(0, 8, 2):
                        ld("pool", wout[:, k:k + 2, :], wo[:, k:k + 2, :], ("wout", k // 2))
                else:
                    wo = wbf_out.rearrange("(k p) e -> p k e", p=128)
                    for k in range(0, 8, 2):
                        P.dma("act", lambda e, k=k: e.dma_start(out=wout[:, k:k + 2, :], in_=wo[:, k:k + 2, :]),
                              reads=tuple(("wbf", pc) for pc in range(24, 32)), writes=(("wout", k // 2),))
                ld("sp", gpost[:], I["post_g"][l:l + 1, :].partition_broadcast(128).rearrange("p o d -> p (o d)"), "gpost")
                return
            if part == "gpre":
                ld("sp", gpre[:], I["pre_g"][l:l + 1, :].partition_broadcast(128).rearrange("p o d -> p (o d)"), "gpre")
                return
            if part == "prep":
                sel = prep_sel[0]
                lq = "sp"
                ld(lq, lncg[l][:], I["ln_c_g"][l:l + 1, :].partition_broadcast(128).rearrange("p o d -> p (o d)"), ("lncg", l))
                ld(lq, lncb[l][:], I["ln_c_b"][l:l + 1, :].partition_broadcast(128).rearrange("p o d -> p (o d)"), ("lncb", l))
                if sel in (None, "cw"):
                    ld(lq, cw_raw[0:KA, :], I["conv_a_w"][l], "cw_raw")
                    ld(lq, cw_raw[KA:KA + KB, :], I["conv_b_w"][l], "cw_raw", "cw_raw")
                    for c in range(2):
                        b = alloc_bank()
                        P.op("pe", lambda e, b=b, c=c: e.transpose(banks[b][:, 0:KA + KB], cw_raw[0:KA + KB, c * 128:(c + 1) * 128],
                                                                    ident_f[0:KA + KB, 0:KA + KB]),
                             reads=("cw_raw", "ident_f"), writes=(("ps", b),))
                        P.op("dve", lambda e, b=b, c=c: e.tensor_scalar(out=cwT[l][:, c, 0:KA], in0=banks[b][:, 0:KA], scalar1=0.5,
                                                                         scalar2=None, op0=ALU.mult),
                             reads=(("ps", b),), writes=(("cwT", l, c, 0),))
                        P.op("dve", lambda e, b=b, c=c: e.tensor_copy(out=cwT[l][:, c, KA:KA + KB], in_=banks[b][:, KA:KA + KB]),
                             reads=(("ps", b),), writes=(("cwT", l, c, 1),))
                if sel in (None, "rest"):
                    ld(lq, spw[:], I["spatial_w"][l].rearrange("h t s -> t h s"), "spw")
                    if not wm_done[0]:
                        P.op("dve", lambda e: e.tensor_tensor(out=wm[:], in0=spw[:], in1=tril[:, :].unsqueeze(1).to_broadcast([128, 4, 128]),
                                                              op=ALU.mult), reads=("spw", "tril"), writes=("wm",))
                    wm_done[0] = False
                    for h in range(4):
                        b = alloc_bank()
                        P.op("pe", lambda e, b=b, h=h: e.transpose(banks_bf[b][:, 0:128], wm[:, h, :], ident_b[:]),
                             reads=("wm", "ident_b"), writes=(("ps", b),))
                        P.op("dve", lambda e, b=b, h=h: e.tensor_copy(out=WmT[l][:, h, :], in_=banks_bf[b][:, 0:128]),
                             reads=(("ps", b),), writes=(("WmT", l),))
                        b1 = alloc_bank()
                        P.op("pe", lambda e, b1=b1, h=h: e.matmul(banks[b1][0:8, 0:128], lhsT=wm[0:8, h, 0:8], rhs=e8[:, :],
                                                                   start=True, stop=True),
                             reads=("wm", "e8"), writes=(("ps", b1),))
                        P.op("dve", lambda e, b1=b1: e.tensor_copy(out=p1sb[:], in_=banks[b1][0:8, 0:128]),
                             reads=(("ps", b1),), writes=("p1sb",))
                        b2 = alloc_bank()
                        P.op("pe", lambda e, b2=b2: e.matmul(banks[b2][:, 0:128], lhsT=e8[:, :], rhs=p1sb[:, :], start=True, stop=True),
                             reads=("p1sb", "e8"), writes=(("ps", b2),))
                        P.op("dve", lambda e, b2=b2, h=h: e.tensor_tensor(out=WmTs[l][:, h, :], in0=banks[b2][:, 0:128], in1=bdmask[:],
                                                                          op=ALU.mult),
                             reads=(("ps", b2), "bdmask"), writes=(("WmTs", l),))
                    ld(lq, bsf[0:4, :], I["spatial_b"][l], "bsf")
                    ld(lq, bsf[32:36, :], I["spatial_b"][l], "bsf", "bsf")
                    P.op("pool", lambda e: e.memset(bs8[l][:], 0.0), writes=(("bs8", l),))
                    P.op("dve", lambda e: e.tensor_copy(out=bs8[l][0:4, :], in_=bsf[0:4, :]), reads=("bsf",), writes=(("bs8", l),))
                    P.op("dve", lambda e: e.tensor_copy(out=bsh2[32:36, :], in_=bsf[32:36, :]), reads=("bsf",), writes=("bsh2",))
                    P.op("dve", lambda e: e.tensor_copy(out=bshf[32:36, :], in_=bsh2[32:36, :]), reads=("bsh2",), writes=("bshf",))
                    P.op("dve", lambda e: e.tensor_tensor(out=bs8[l][32:36, :], in0=bsf[32:36, :], in1=bshf[32:36, :], op=ALU.subtract),
                         reads=("bsf", "bshf"), writes=(("bs8", l),))
                    if not P.mute_ops and l > 0:
                        dm = P.mute_dma
                        P.mute_dma = False
                        P.op("pool", lambda e: e.memset(wpbd[l][:], 0.0), writes=(("wpbd", l),))
                        for g in range(4):
                            r0 = (g % 2) * 64
                            ld("pool", wpbd[l][r0:r0 + 64, g // 2, r0:r0 + 64], I["pool_w"][l, g], ("wpbd", l, g), ("wpbd", l))
                        P.mute_dma = dm
                    ld("sp", O["ca_s"][l, :, 0:22, :], I["sa"][l].rearrange("(q r) c -> q r c", r=30)[:, 8:30, :], ("o", "ca_s0", l))
                    ld("sp", O["pd_s"][l, :, 0:7, :], I["spl"][l].rearrange("(q r) c -> q r c", r=15)[:, 8:15, :], ("o", "pd_s0", l))
                return
            if part == "diag":
                for c in range(2):
                    P.op("dve", lambda e, c=c: e.tensor_tensor(
                        out=diagA[:, c, :, :], in0=ident_b[:, :].unsqueeze(1).to_broadcast([128, KA, 128]),
                        in1=cwT[l][:, c, 0:KA].unsqueeze(2).to_broadcast([128, KA, 128]), op=ALU.mult),
                         reads=("ident_b", ("cwT", l, c, 0)), writes=("diagA",))
                    P.op("pool", lambda e, c=c: e.tensor_tensor(
                        out=diagB[:, c, :, :], in0=ident_b[:, :].unsqueeze(1).to_broadcast([128, KB, 128]),
                        in1=cwT[l][:, c, KA:KA + KB].unsqueeze(2).to_broadcast([128, KB, 128]), op=ALU.mult),
                         reads=("ident_b", ("cwT", l, c, 1)), writes=("diagB",))
                return
            assert part[0] == "state"
            _, j, ph = part
            if j < 4:
                if ph == "a":
                    ld("act", sa_raw[:], I["sa"][l, j * 120:(j + 1) * 120, :], "sa_raw")
                    P.op("pool", lambda e: e.tensor_copy(out=sa_bf[:], in_=sa_raw[:]), reads=("sa_raw",), writes=("sa_bf",))
                else:
                    for c in range(2):
                        b = alloc_bank()
                        P.op("pe", lambda e, b=b, c=c: e.transpose(banks_bf[b][:, 0:120], sa_bf[:, c * 128:(c + 1) * 128],
                                                                    ident_b[0:120, 0:120]),
                             reads=("sa_bf", "ident_b"), writes=(("ps", b),))
                        P.op("act", lambda e, b=b, c=c, j=j: e.activation(
                            out=gluext_s[:, c, 4 * j:4 * j + 4, 0:30],
                            in_=banks_bf[b][:, 0:120].rearrange("p (q r) -> p q r", r=30), func=AF.Copy, scale=2.0),
                             reads=(("ps", b),), writes=("gluext_s_st",))
                return
            if ph == "a":
                ld("act", sb_raw[:], I["sb"][l], "sb_raw")
                P.op("pool", lambda e: e.tensor_copy(out=sb_bf[:], in_=sb_raw[:]), reads=("sb_raw",), writes=("sb_bf",))
                ld("pool", spool[:], I["spl"][l].rearrange("(h p) c -> p h c", p=120), "spool")
            else:
                for c in range(2):
                    b = alloc_bank()
                    P.op("pe", lambda e, b=b, c=c: e.transpose(banks_bf[b][:, 0:32], sb_bf[:, c * 128:(c + 1) * 128],
                                                                ident_b[0:32, 0:32]),
                         reads=("sb_bf", "ident_b"), writes=(("ps", b),))
                    P.op("act", lambda e, b=b, c=c: e.activation(
                        out=hbext_s[:, c, :, 0:2], in_=banks_bf[b][:, 0:32].rearrange("p (q r) -> p q r", r=2), func=AF.Copy),
                         reads=(("ps", b),), writes=("hbext_s_st",))

        def blk_info(b):
            if b < 8:
                return dict(tiles=[2 * b, 2 * b + 1], nb=256, t0=256 * b, sample=False)
            return dict(tiles=[16], nb=128, t0=0, sample=True)

        def src_rows(l, ti):
            if l == 0:
                return I["xp"][ti * 128:(ti + 1) * 128, :] if ti < 16 else I["xs"][:, :]
            return y0[ti * 128:(ti + 1) * 128, :]

        def dst_rows(l, ti):
            if l == 0:
                return y0[ti * 128:(ti + 1) * 128, :]
            return O["yp"][ti * 128:(ti + 1) * 128, :] if ti < 16 else O["ys"][:, :]

        def rsqrt_chain(v, y, t0, t1, rv, ry, rt, newton_eng="dve"):
            P.op("dve", lambda e: e.tensor_single_scalar(out=t0.bitcast(I32), in_=v.bitcast(I32), scalar=1,
                                                         op=ALU.logical_shift_right),
                 reads=(rv,), writes=(rt,))
            P.op("dve", lambda e: e.tensor_scalar(out=y.bitcast(I32), in0=t0.bitcast(I32), scalar1=-1, scalar2=RSQ_MAGIC,
                                                  op0=ALU.mult, op1=ALU.add),
                 reads=(rt,), writes=(ry,))
            for _ in range(2):
                if newton_eng == "dve":
                    P.op("dve", lambda e: e.tensor_tensor(out=t0, in0=y, in1=y, op=ALU.mult), reads=(ry,), writes=(rt,))
                    P.op("dve", lambda e: e.scalar_tensor_tensor(out=t1, in0=t0, scalar=-0.5, in1=v, op0=ALU.mult, op1=ALU.mult),
                         reads=(rt, rv), writes=(rt,))
                    P.op("dve", lambda e: e.scalar_tensor_tensor(out=y, in0=t1, scalar=1.5, in1=y, op0=ALU.add, op1=ALU.mult),
                         reads=(rt, ry), writes=(ry,))
                else:
                    P.op("pool", lambda e: e.tensor_tensor(out=t0, in0=y, in1=y, op=ALU.mult), reads=(ry,), writes=(rt,))
                    P.op("pool", lambda e: e.tensor_tensor(out=t1, in0=t0, in1=v, op=ALU.mult), reads=(rt, rv), writes=(rt,))
                    P.op("pool", lambda e: e.tensor_scalar(out=t1, in0=t1, scalar1=-0.5, scalar2=1.5, op0=ALU.mult, op1=ALU.add),
                         reads=(rt,), writes=(rt,))
                    P.op("pool", lambda e: e.tensor_tensor(out=y, in0=y, in1=t1, op=ALU.mult), reads=(rt, ry), writes=(ry,))

        def rstd_from(col_sum, col_tmp, col_out, scale):
            P.op("dve", lambda e: e.tensor_scalar(out=stat[:, col_tmp:col_tmp + 1], in0=stat[:, col_sum:col_sum + 1],
                                                  scalar1=scale, scalar2=EPS, op0=ALU.mult, op1=ALU.add),
                 reads=(("stat", col_sum),), writes=(("stat", col_tmp),))
            rsqrt_chain(stat[:, col_tmp:col_tmp + 1], stat[:, col_out:col_out + 1], stat[:, 508:509], stat[:, 509:510],
                        ("stat", col_tmp), ("stat", col_out), "stat_tmp")

        xin_rr = [0]
        xres_rr = [0]

        pre_slots = {}

        def pre_load(l, b):
            info = blk_info(b)
            for tl, ti in enumerate(info["tiles"]):
                slot = xin_rr[0] % 2
                xin_rr[0] += 1
                pre_slots[(l, b, tl)] = slot
                rd = (("y0", ti),) if l == 1 else ()
                P.dma("act", lambda e, slot=slot, ti=ti: e.dma_start(out=xin[slot][:], in_=src_rows(l, ti)),
                      reads=rd, writes=(("xin", slot),))

        def pre_a(l, b):
            info = blk_info(b)
            nt = len(info["tiles"])
            c = new_col(3 * nt)
            for tl, ti in enumerate(info["tiles"]):
                slot = pre_slots[(l, b, tl)]
                P.op("act", lambda e, slot=slot, cc=c + tl: e.activation(out=htm[slot][:], in_=xin[slot][:], func=AF.Square,
                                                                         accum_out=stat[:, cc:cc + 1]),
                     reads=(("xin", slot),), writes=(("htm", slot), ("stat", c + tl)))
            P.op("dve", lambda e: e.tensor_scalar(out=stat[:, c + nt:c + 2 * nt], in0=stat[:, c:c + nt],
                                                  scalar1=1.0 / D, scalar2=EPS, op0=ALU.mult, op1=ALU.add),
                 reads=tuple(("stat", c + tl) for tl in range(nt)), writes=(("statv", c),))
            rsqrt_chain(stat[:, c + nt:c + 2 * nt], stat[:, c + 2 * nt:c + 3 * nt], stat[:, 504:504 + nt], stat[:, 506:506 + nt],
                        ("statv", c), ("staty", c), "stat_tmp3")
            for tl, ti in enumerate(info["tiles"]):
                slot = pre_slots[(l, b, tl)]
                cc = c + 2 * nt + tl
                P.op("dve", lambda e, slot=slot, cc=cc: e.scalar_tensor_tensor(
                    out=htm[slot][:], in0=xin[slot][:], scalar=stat[:, cc:cc + 1], in1=gpre[:],
                    op0=ALU.mult, op1=ALU.mult),
                     reads=(("xin", slot), ("staty", c), "gpre"), writes=(("htm", slot),))

        def pre_b(l, b):
            info = blk_info(b)
            par = (l * NBLK + b) % 2
            for tl, ti in enumerate(info["tiles"]):
                slot = pre_slots[(l, b, tl)]
                bk = alloc_bank()

                def tr(e, slot=slot, bk=bk):
                    ins = None
                    for j in range(8):
                        ins = e.transpose(banks_bf[bk][:, j * 128:(j + 1) * 128], htm[slot][:, j * 128:(j + 1) * 128], ident_b[:])
                    return ins
                P.op("pe", tr, reads=(("htm", slot), "ident_b"), writes=(("ps", bk),))
                P.op("act", lambda e, bk=bk, tl=tl, par=par: e.activation(
                    out=hT[par][:, :, tl * 128:(tl + 1) * 128],
                    in_=banks_bf[bk][:, 0:1024].rearrange("p (j t) -> p j t", j=8), func=AF.Copy),
                     reads=(("ps", bk),), writes=(("hT", par, tl),))

        def wc_src(l, pc):
            if pc < 24:
                k, j = pc // 3, pc % 3
                return I["w_in"][l, k * 128:(k + 1) * 128, j * 1024:(j + 1) * 1024]
            k = pc - 24
            return I["w_out"][l, k * 128:(k + 1) * 128, :]

        def wc_dst(pc):
            if pc < 24:
                k, j = pc // 3, pc % 3
                return wbf_in[k * 128:(k + 1) * 128, j * 1024:(j + 1) * 1024]
            k = pc - 24
            return wbf_out[k * 128:(k + 1) * 128, :]

        wc_state = {"tick": 0}

        def wc_tick(l):
            k = wc_state["tick"]
            wc_state["tick"] = k + 1
            pc = k - 2
            if 0 <= pc < 32:
                buf = pc % 2
                P.op("act", lambda e, buf=buf: e.activation(out=wstg_b[:], in_=wstg_f[buf][:], func=AF.Copy),
                     reads=(("wstg_f", buf),), writes=("wstg_b",))
                P.dma("sp", lambda e, pc=pc: e.dma_start(out=wc_dst(pc), in_=wstg_b[:]),
                      reads=("wstg_b",), writes=(("wbf", pc),))
            if k < 32:
                buf = k % 2
                P.dma("act", lambda e, k=k, buf=buf: e.dma_start(out=wstg_f[buf][:], in_=wc_src(l, k)),
                      writes=(("wstg_f", buf),))

        def reload_slab(l, s):
            wv = wbf_in.rearrange("(k p) e -> p k e", p=128)
            if s == 2:
                lo, hi = 0, 3
            elif s == 3:
                lo, hi = 3, 7
            elif s == 11:
                lo, hi = 7, 12
            else:
                return
            P.dma("act", lambda e: e.dma_start(out=win[:, :, lo * G:hi * G], in_=wv[:, :, lo * G:hi * G]),
                  reads=tuple(("wbf", pc) for pc in range(24)), writes=tuple(("win", j) for j in range(lo, hi)))

        def proj_cm(l, b, s):
            info = blk_info(b)
            par, nb = (l * NBLK + b) % 2, info["nb"]
            bk = alloc_bank()

            def f(e):
                ins = None
                for c in range(2):
                    for k in range(8):
                        ins = e.matmul(banks[bk][:, c * 256:c * 256 + nb], lhsT=win[:, k, s * G + c * 128:s * G + (c + 1) * 128],
                                       rhs=hT[par][:, k, 0:nb], start=(k == 0), stop=(k == 7))
                return ins
            P.op("pe", f, reads=(("win", s),) + tuple(("hT", par, tl) for tl in range(len(info["tiles"]))),
                 writes=(("ps", bk),))
            if b == NBLK - 1 and l + 1 < DEPTH:
                reload_slab(l + 1, s)
            return bk

        def psv(bk, nb):
            return banks[bk][:, :].rearrange("p (c n) -> p c n", c=2)[:, :, 0:nb]

        def stage_proj(l, b, hook_out=None, hook_pre_b=None, hook_early=None, hook_conv_done=None, reload_next=False):
            info = blk_info(b)
            par, nb, t0, smp = (l * NBLK + b) % 2, info["nb"], info["t0"], info["sample"]
            ntl = len(info["tiles"])
            hTr = tuple(("hT", par, tl) for tl in range(ntl))
            state_tile = smp or b == 7
            sl = slice(128, 256) if b == 7 else slice(0, 128)

            def gl_dst():
                if smp:
                    return gluext_s[:, :, :, 30:38]
                return glx[par][:, :, 30:30 + nb]

            def hb_dst():
                if smp:
                    return hbext_s[:, :, :, 2:10]
                return hbx[par][:, :, 2:2 + nb]

            def shp(ap):
                return ap.rearrange("p c (q t) -> p c q t", t=TS) if smp else ap

            if not smp:
                if b == 0:
                    P.op("pool", lambda e: e.memset(glx[par][:, :, 0:30], 0.0), writes=(("glx", par),))
                    P.op("pool", lambda e: e.memset(hbx[par][:, :, 0:2], 0.0), writes=(("hbx", par),))
                else:
                    P.op("pool", lambda e: e.tensor_copy(out=glx[par][:, :, 0:30], in_=glx[1 - par][:, :, 256:286]),
                         reads=(("glx", 1 - par),), writes=(("glx", par),))
                    P.op("pool", lambda e: e.tensor_copy(out=hbx[par][:, :, 0:2], in_=hbx[1 - par][:, :, 256:258]),
                         reads=(("hbx", 1 - par),), writes=(("hbx", par),))
            bk_gate = proj_cm(l, b, 1)
            P.op("act", lambda e: e.activation(out=th[:, :, 0:nb], in_=psv(bk_gate, nb), func=AF.Tanh, scale=0.5),
                 reads=(("ps", bk_gate),), writes=("th",))
            bk_val = proj_cm(l, b, 0)
            P.op("dve", lambda e: e.scalar_tensor_tensor(out=gl_dst(), in0=shp(th[:, :, 0:nb]), scalar=1.0,
                                                         in1=shp(psv(bk_val, nb)), op0=ALU.add, op1=ALU.mult),
                 reads=("th", ("ps", bk_val)), writes=(("glx_s",) if smp else ("glx", par),))
            if state_tile:
                P.op("dve", lambda e: e.scalar_tensor_tensor(out=gluf[:], in0=th[:, :, sl], scalar=1.0,
                                                             in1=psv(bk_val, nb)[:, :, sl], op0=ALU.add, op1=ALU.mult),
                     reads=("th", ("ps", bk_val)), writes=("gluf",))
            bk_za = proj_cm(l, b, 2)
            P.op("act", lambda e: e.activation(out=sza[:, :, 0:nb], in_=psv(bk_za, nb), func=AF.Silu),
                 reads=(("ps", bk_za),), writes=("sza",))
            bk_zb = proj_cm(l, b, 6)
            P.op("act", lambda e: e.activation(out=szb[:, :, 0:nb], in_=psv(bk_zb, nb), func=AF.Silu),
                 reads=(("ps", bk_zb),), writes=("szb",))
            bk_bx = proj_cm(l, b, 5)
            P.op("act", lambda e: e.activation(out=bx[:, :, 0:nb], in_=psv(bk_bx, nb), func=AF.Copy),
                 reads=(("ps", bk_bx),), writes=("bx",))
            bk_bc = proj_cm(l, b, 4)
            P.op("dve", lambda e: e.tensor_tensor(out=hb_dst(), in0=shp(psv(bk_bc, nb)), in1=shp(bx[:, :, 0:nb]), op=ALU.mult),
                 reads=("bx", ("ps", bk_bc)), writes=(("hbx_s",) if smp else ("hbx", par),))
            if state_tile:
                P.op("dve", lambda e: e.tensor_tensor(out=hbf[:], in0=psv(bk_bc, nb)[:, :, sl], in1=bx[:, :, sl], op=ALU.mult),
                     reads=("bx", ("ps", bk_bc)), writes=("hbf",))
            bk_bb = proj_cm(l, b, 3)
            P.op("dve", lambda e: e.tensor_tensor(out=t1b[:, :, 0:nb], in0=psv(bk_bb, nb), in1=szb[:, :, 0:nb], op=ALU.mult),
                 reads=("szb", ("ps", bk_bb)), writes=("t1b",))
            if hook_early is not None:
                hook_early()
            bk_ca = alloc_bank()

            def convA(e):
                ins = None
                for c in range(2):
                    for k in range(KA):
                        if smp:
                            rhs = gluext_s[:, c, :, k:k + TS]
                            out = banks[bk_ca][:, c * 256:c * 256 + nb].rearrange("p (q t) -> p q t", t=TS)
                        else:
                            rhs = glx[par][:, c, k:k + nb]
                            out = banks[bk_ca][:, c * 256:c * 256 + nb]
                        ins = e.matmul(out, lhsT=diagA[:, c, k, :], rhs=rhs, start=(k == 0), stop=(k == KA - 1))
                return ins
            glu_reads = (("glx_s",), "gluext_s_st", "diagA") if smp else (("glx", par), "diagA")
            P.op("pe", convA, reads=glu_reads, writes=(("ps", bk_ca),))

            def ya_evac(e):
                ins = None
                for c in range(2):
                    ins = e.activation(out=ya[:, c, 0:nb], in_=banks[bk_ca][:, c * 256:c * 256 + nb], func=AF.Identity,
                                       bias=pcol[:, 0, l, c:c + 1])
                return ins
            P.op("act", ya_evac, reads=(("ps", bk_ca), ("pcol", 0)), writes=("ya",))
            bk_cb = alloc_bank()

            def convB(e):
                ins = None
                for c in range(2):
                    for k in range(KB):
                        if smp:
                            rhs = hbext_s[:, c, :, k:k + TS]
                            out = banks[bk_cb][:, c * 256:c * 256 + nb].rearrange("p (q t) -> p q t", t=TS)
                        else:
                            rhs = hbx[par][:, c, k:k + nb]
                            out = banks[bk_cb][:, c * 256:c * 256 + nb]
                        ins = e.matmul(out, lhsT=diagB[:, c, k, :], rhs=rhs, start=(k == 0), stop=(k == KB - 1))
                return ins
            hb_reads = (("hbx_s",), "hbext_s_st", "diagB") if smp else (("hbx", par), "diagB")
            P.op("pe", convB, reads=hb_reads, writes=(("ps", bk_cb),))
            P.op("dve", lambda e: e.tensor_tensor(out=mix[par][:, 2:4, 0:nb], in0=psv(bk_cb, nb), in1=t1b[:, :, 0:nb], op=ALU.mult),
                 reads=("t1b", ("ps", bk_cb)), writes=(("mix", par, 1),))
            bk_tm = []
            for tl in range(ntl):
                bk = alloc_bank()

                def ptm(e, bk=bk, tl=tl):
                    ins = None
                    for k in range(8):
                        rhs = win[:, k, 8 * G:12 * G].rearrange("p (a s) -> p a s", a=2)[:, :, 0:G]
                        ins = e.matmul(banks[bk][:, :].rearrange("p (a s) -> p a s", a=2), lhsT=hT[par][:, k, tl * 128:(tl + 1) * 128],
                                       rhs=rhs, start=(k == 0), stop=(k == 7))
                    return ins
                P.op("pe", ptm, reads=(("win", 8), ("win", 10), ("hT", par, tl)), writes=(("ps", bk),))
                bk_tm.append(bk)
            if b == NBLK - 1 and l + 1 < DEPTH:
                reload_slab(l + 1, 8)
                reload_slab(l + 1, 10)
            late_pe = []
            bk_sp_box = []
            bk_pl_box = []
            cb = new_col(4 * ntl)
            for tl, ti in enumerate(info["tiles"]):
                bk = bk_tm[tl]
                cslot = tl
                dslot = ti % 3
                P.op("act", lambda e, bk=bk, dslot=dslot: e.activation(out=dxT[dslot][:], in_=banks[bk][:, G:2 * G], func=AF.Copy),
                     reads=(("ps", bk),), writes=(("dxT", dslot),))
                P.op("dve", lambda e, bk=bk, cslot=cslot: e.bn_stats(out=bnst[cslot][:], in_=banks[bk][:, 0:G]),
                     reads=(("ps", bk),), writes=(("bnst", cslot),))
                P.op("dve", lambda e, cslot=cslot, cc=cb + 2 * tl: e.bn_aggr(out=stat[:, cc:cc + 2], in_=bnst[cslot][:]),
                     reads=(("bnst", cslot),), writes=(("stat", cb + 2 * tl),))
            P.op("dve", lambda e: e.tensor_scalar(out=stat[:, cb + 2 * ntl:cb + 3 * ntl], in0=stat[:, cb + 1:cb + 2 * ntl:2],
                                                  scalar1=EPS, scalar2=None, op0=ALU.add),
                 reads=tuple(("stat", cb + 2 * tl) for tl in range(ntl)), writes=(("statv", cb),))
            rsqrt_chain(stat[:, cb + 2 * ntl:cb + 3 * ntl], stat[:, cb + 3 * ntl:cb + 4 * ntl],
                        stat[:, 508:508 + ntl], stat[:, 510:510 + ntl], ("statv", cb), ("staty", cb), "stat_tmp")
            for tl, ti in enumerate(info["tiles"]):
                bk = bk_tm[tl]
                cslot = tl
                dslot = ti % 3
                P.op("dve", lambda e, bk=bk, cslot=cslot, cm=cb + 2 * tl, cr=cb + 3 * ntl + tl: e.tensor_scalar(
                    out=zc[cslot][:], in0=banks[bk][:, 0:G], scalar1=stat[:, cm:cm + 1], scalar2=stat[:, cr:cr + 1],
                    op0=ALU.subtract, op1=ALU.mult),
                     reads=(("ps", bk), ("stat", cb + 2 * tl), ("staty", cb)), writes=(("zc", cslot),))
                veng = "dve" if smp else "pool"
                P.op(veng, lambda e, cslot=cslot: e.tensor_tensor(out=vn1[cslot][:], in0=zc[cslot][:], in1=lncg[l][:], op=ALU.mult),
                     reads=(("zc", cslot), ("lncg", l)), writes=(("vn1", cslot),))
                if smp:
                    P.op("dve", lambda e, cslot=cslot: e.tensor_tensor(out=vnf[:], in0=vn1[cslot][:], in1=lncb[l][:], op=ALU.add),
                         reads=(("vn1", cslot), ("lncb", l)), writes=("vnf",))
                    P.op("dve", lambda e, cslot=cslot: e.tensor_copy(out=vnb[cslot][:], in_=vnf[:]),
                         reads=("vnf",), writes=(("vnb", cslot),))
                    ld("sp", O["cv_s"][l], vnf[:], ("o", "cv_s", l), "vnf")
                else:
                    P.op("pool", lambda e, cslot=cslot: e.tensor_tensor(out=vnb[cslot][:], in0=vn1[cslot][:], in1=lncb[l][:], op=ALU.add),
                         reads=(("vn1", cslot), ("lncb", l)), writes=(("vnb", cslot),))

                def spat(e, cslot=cslot, tl=tl):
                    ins = None
                    for h in range(4):
                        r0 = (h % 2) * 64
                        out = banks[bk_sp_box[0]][r0:r0 + 64, (h // 2) * 256 + tl * 128:(h // 2) * 256 + (tl + 1) * 128]
                        wt = WmTs[l] if smp else WmT[l]
                        e.matmul(out, lhsT=vnb[cslot][:, h * 64:(h + 1) * 64], rhs=wt[:, h, :], start=True, stop=False)
                        if smp:
                            o3 = out.rearrange("p (q t) -> p q t", t=TS)
                            ins = e.matmul(o3, lhsT=sel4[:, h, :], rhs=bs8[l][:, 0:TS].unsqueeze(1).to_broadcast([128, NSEQ_S, TS]),
                                           start=False, stop=True)
                        else:
                            ins = e.matmul(out, lhsT=sel4[:, h, :], rhs=bs8[l][:, :], start=False, stop=True)
                    return ins
                late_pe.append((spat, (("vnb", cslot), ("WmTs", l) if smp else ("WmT", l), "sel4", ("bs8", l)), (("ps_sp",),)))
                if ti >= 15:
                    P.op("dve", lambda e, bk=bk: e.tensor_copy(out=dxf[:], in_=banks[bk][:, G:2 * G]),
                         reads=(("ps", bk),), writes=("dxf",))
                    if smp:
                        for q in range(NSEQ_S):
                            ld("sp", O["pd_s"][l, q, 7:15, :], dxf[q * TS:(q + 1) * TS, :], ("o", "pd_s1", l, q), "dxf")
                    else:
                        ld("sp", O["pd_p"][l], dxf[113:128, :], ("o", "pd_p", l), "dxf")

                def poolmm(e, dslot=dslot, tl=tl, ti=ti):
                    ins = None
                    pslot = (ti - 1) % 3
                    for g in range(4):
                        r0 = (g % 2) * 64
                        out = banks[bk_pl_box[0]][r0:r0 + 64, (g // 2) * 256 + tl * 128:(g // 2) * 256 + (tl + 1) * 128]
                        lh = dxT[dslot][:, g * 64:(g + 1) * 64]
                        if smp:
                            e.matmul(out, lhsT=lh, rhs=bscur[:, g, :], start=True, stop=False)
                            e.matmul(out, lhsT=spool[:, 0, g * 64:(g + 1) * 64], rhs=bsbuf[:, 0, g, :], start=False, stop=False)
                            ins = e.matmul(out, lhsT=spool[:, 1, g * 64:(g + 1) * 64], rhs=bsbuf[:, 1, g, :], start=False, stop=True)
                        elif ti == 0:
                            e.matmul(out, lhsT=lh, rhs=bcur0h[:, g, :], start=True, stop=False)
                            ins = e.matmul(out, lhsT=lh, rhs=bcur0l[:, g, :], start=False, stop=True)
                        else:
                            e.matmul(out, lhsT=lh, rhs=bcur[:, g, :], start=True, stop=False)
                            ins = e.matmul(out, lhsT=dxT[pslot][:, g * 64:(g + 1) * 64], rhs=bprev[:, g, :], start=False, stop=True)
                    return ins
                prd = (("dxT", dslot),) + ((("dxT", (ti - 1) % 3),) if (not smp and ti > 0) else ())
                late_pe.append((poolmm, prd + ("bcur", "bprev", "bcur0h", "bcur0l", "bscur", "bsbuf", "spool"), (("ps_pl",),)))
            if hook_conv_done is not None:
                hook_conv_done()
            bk_yc = 7

            def centre(e):
                ins = None
                for co in range(2):
                    for ci in range(2):
                        ins = e.matmul(banks[bk_yc][:, co * 256:co * 256 + nb], lhsT=cmat[:, 0 if ci == co else 1, :],
                                       rhs=ya[:, ci, 0:nb], start=(ci == 0), stop=(ci == 1))
                return ins
            P.op("pe", centre, reads=("ya", "cmat"), writes=(("ps", bk_yc),))
            P.op("act", lambda e: e.activation(out=ycsq[:, :, 0:nb], in_=psv(bk_yc, nb), func=AF.Square),
                 reads=(("ps", bk_yc),), writes=("ycsq",))
            if hook_pre_b is not None:
                hook_pre_b()
            if hook_out is not None:
                hook_out(0)
            bk_var = alloc_bank()

            def varmm(e):
                ins = None
                for ci in range(2):
                    ins = e.matmul(banks[bk_var][:, 0:nb], lhsT=ones256[:], rhs=ycsq[:, ci, 0:nb], start=(ci == 0), stop=(ci == 1))
                return ins
            P.op("pe", varmm, reads=("ycsq", "ones256"), writes=(("ps", bk_var),))
            P.op("dve", lambda e: e.tensor_scalar(out=varsb[:, 0:nb], in0=banks[bk_var][:, 0:nb], scalar1=EPS, scalar2=None,
                                                  op0=ALU.add),
                 reads=(("ps", bk_var),), writes=("varsb",))
            rsqrt_chain(varsb[:, 0:nb], rstdA[:, 0:nb], yn[:, 0, 0:nb], yn[:, 1, 0:nb], "varsb", "rstdA", "yn", newton_eng="dve")
            bk_pl = alloc_bank()
            bk_pl_box.append(bk_pl)

            def emit_late(tag):
                for fn, rds, wrs in late_pe:
                    if wrs[0] != (tag,):
                        continue
                    bkx = bk_sp_box[0] if tag == "ps_sp" else bk_pl
                    P.op("pe", fn, reads=rds, writes=(("ps", bkx),))
            emit_late("ps_pl")
            P.op("act", lambda e: e.activation(out=pooled[:, :, 0:nb], in_=psv(bk_pl, nb), func=AF.Copy),
                 reads=(("ps", bk_pl),), writes=("pooled",))
            bk_zc = proj_cm(l, b, 9)
            P.op("act", lambda e: e.activation(out=szc[:, :, 0:nb], in_=psv(bk_zc, nb), func=AF.Silu),
                 reads=(("ps", bk_zc),), writes=("szc",))
            bk_cu = proj_cm(l, b, 7)
            P.op("dve", lambda e: e.tensor_tensor(out=t1c[:, :, 0:nb], in0=psv(bk_cu, nb), in1=szc[:, :, 0:nb], op=ALU.mult),
                 reads=("szc", ("ps", bk_cu)), writes=("t1c",))
            bk_zd = proj_cm(l, b, 11)
            P.op("act", lambda e: e.activation(out=szd[:, :, 0:nb], in_=psv(bk_zd, nb), func=AF.Silu),
                 reads=(("ps", bk_zd),), writes=("szd",))
            if hook_out is not None:
                hook_out(1)
            very_last = (l == DEPTH - 1 and b == NBLK - 1)
            def emit_lnA_tail():
                P.op("dve", lambda e: e.tensor_tensor(out=yn[:, :, 0:nb], in0=psv(bk_yc, nb),
                                                      in1=rstdA[:, 0:nb].unsqueeze(1).to_broadcast([128, 2, nb]), op=ALU.mult),
                     reads=(("ps", bk_yc), "rstdA"), writes=("yn",))

                def s1f(e):
                    ins = None
                    for c in range(2):
                        ins = e.activation(out=s1[:, c, 0:nb], in_=yn[:, c, 0:nb], func=AF.Silu,
                                           scale=pcol[:, 1, l, c:c + 1], bias=pcol[:, 2, l, c:c + 1])
                    return ins
                P.op("act", s1f, reads=("yn", ("pcol", 1), ("pcol", 2)), writes=("s1",))
                P.op("pool", lambda e: e.tensor_tensor(out=mix[par][:, 0:2, 0:nb], in0=s1[:, :, 0:nb], in1=sza[:, :, 0:nb], op=ALU.mult),
                     reads=("s1", "sza"), writes=(("mix", par, 0),))
            if very_last:
                emit_lnA_tail()
            bk_sp = alloc_bank()
            bk_sp_box.append(bk_sp)
            emit_late("ps_sp")
            P.op("dve", lambda e: e.tensor_tensor(out=mix[par][:, 4:6, 0:nb], in0=psv(bk_sp, nb), in1=t1c[:, :, 0:nb], op=ALU.mult),
                 reads=("t1c", ("ps", bk_sp)), writes=(("mix", par, 2),))
            bk_wp = alloc_bank()

            def wpmm(e):
                ins = None
                for c in range(2):
                    ins = e.matmul(banks[bk_wp][:, c * 256:c * 256 + nb], lhsT=wpbd[l][:, c, :], rhs=pooled[:, c, 0:nb], start=True, stop=True)
                return ins
            P.op("pe", wpmm, reads=("pooled", ("wpbd", l)) + tuple(("wpbd", l, g) for g in range(4)), writes=(("ps", bk_wp),))

            def ydf(e):
                ins = None
                for c in range(2):
                    ins = e.scalar_tensor_tensor(out=mix[par][:, 6 + c, 0:nb], in0=banks[bk_wp][:, c * 256:c * 256 + nb],
                                                 scalar=pcol[:, 3, l, c:c + 1], in1=szd[:, c, 0:nb], op0=ALU.mult, op1=ALU.mult)
                return ins
            P.op("dve", ydf, reads=("szd", ("ps", bk_wp), ("pcol", 3)), writes=(("mix", par, 3),))
            if not very_last:
                emit_lnA_tail()
            if state_tile:
                for nm, src, scl, stg, rstg in (("ca", gluf, 0.5, sttm, "sttm"), ("cb", hbf, 1.0, zc[1], ("zc", 1))):
                    bk = alloc_bank()

                    def trs(e, bk=bk, src=src):
                        ins = None
                        for c in range(2):
                            ins = e.transpose(banks[bk][:, c * 128:(c + 1) * 128], src[:, c, :], ident_f[:])
                        return ins
                    P.op("pe", trs, reads=("gluf" if nm == "ca" else "hbf", "ident_f"), writes=(("ps", bk),))
                    P.op("dve", lambda e, bk=bk, scl=scl, stg=stg: e.tensor_scalar(out=stg[:], in0=banks[bk][:, 0:G], scalar1=scl,
                                                                                   scalar2=None, op0=ALU.mult),
                         reads=(("ps", bk),), writes=(rstg,))
                    if nm == "ca":
                        if smp:
                            for q in range(NSEQ_S):
                                ld("sp", O["ca_s"][l, q, 22:30, :], stg[q * TS:(q + 1) * TS, :], ("o", "ca_s1", l, q), rstg)
                        else:
                            ld("sp", O["ca_p"][l], stg[98:128, :], ("o", "ca_p", l), rstg)
                    else:
                        if smp:
                            for r in range(2):
                                ld("sp", O["cb_s"][l, :, r, :], stg[6 + r:128:8, :], ("o", "cb_s", l, r), rstg)
                        else:
                            ld("sp", O["cb_p"][l], stg[126:128, :], ("o", "cb_p", l), rstg)

        def stage_out(l, b, which=None):
            info = blk_info(b)
            par = (l * NBLK + b) % 2
            for tl, ti in enumerate(info["tiles"]):
                if which is not None and tl != which:
                    continue
                slot = xres_rr[0] % 2
                xres_rr[0] += 1
                rd = (("y0", ti),) if l == 1 else ()
                final_tile = (l == DEPTH - 1 and b == NBLK - 1)
                P.dma("act", lambda e, slot=slot, ti=ti: e.dma_start(out=xres[slot][:], in_=src_rows(l, ti)),
                      reads=rd, writes=(("xres", slot, 0), ("xres", slot, 1)))
                bks = []
                c = new_col(5)
                for hf in range(2):
                    bk = 5 + hf

                    def omm(e, bk=bk, hf=hf, tl=tl):
                        ins = None
                        for k in range(8):
                            ins = e.matmul(banks[bk][:, :], lhsT=mix[par][:, k, tl * 128:(tl + 1) * 128],
                                           rhs=wout[:, k, hf * 512:(hf + 1) * 512], start=(k == 0), stop=(k == 7))
                        return ins
                    P.op("pe", omm, reads=tuple(("mix", par, j) for j in range(4)) + tuple(("wout", j) for j in range(4)),
                         writes=(("ps", bk),))
                    P.op("act", lambda e, bk=bk, hf=hf, c=c: e.activation(out=ptmp[hf][:, :].bitcast(BF16)[:, 0:512], in_=banks[bk][:, :], func=AF.Square,
                                                                         scale=1.0 / 32.0,
                                                                         accum_out=stat[:, c + hf:c + hf + 1]),
                         reads=(("ps", bk),), writes=(("ptmp", hf), ("stat", c + hf)))
                    bks.append(bk)
                P.op("dve", lambda e, c=c: e.tensor_scalar(out=stat[:, c + 3:c + 4], in0=stat[:, c:c + 1], scalar1=stat[:, c + 1:c + 2],
                                                           scalar2=EPS, op0=ALU.add, op1=ALU.add),
                     reads=(("stat", c), ("stat", c + 1)), writes=(("stat", c + 3),))
                rsqrt_chain(stat[:, c + 3:c + 4], stat[:, c + 4:c + 5], stat[:, 508:509], stat[:, 509:510],
                            ("stat", c + 3), ("stat", c + 4), "stat_tmp")
                for hf in range(2):
                    bk = bks[hf]
                    P.op("dve", lambda e, bk=bk, hf=hf, c=c: e.scalar_tensor_tensor(
                        out=ptmp[hf][:], in0=banks[bk][:, :], scalar=stat[:, c + 4:c + 5], in1=gpost[:, hf * 512:(hf + 1) * 512],
                        op0=ALU.mult, op1=ALU.mult),
                         reads=(("ps", bk), ("stat", c + 4), "gpost"), writes=(("ptmp", hf),))
                    aeng = "dve" if (final_tile and hf == 1) else "pool"
                    P.op(aeng, lambda e, hf=hf, slot=slot: e.tensor_tensor(
                        out=xres[slot][:, hf * 512:(hf + 1) * 512], in0=ptmp[hf][:], in1=xres[slot][:, hf * 512:(hf + 1) * 512],
                        op=ALU.add),
                         reads=(("ptmp", hf), ("xres", slot, hf)), writes=(("xres", slot, hf),))
                P.dma("sp", lambda e, slot=slot, ti=ti: e.dma_start(out=dst_rows(l, ti), in_=xres[slot][:]),
                      reads=(("xres", slot, 0), ("xres", slot, 1)), writes=((("y0", ti),) if l == 0 else (("o", "y", ti),)))

        pre_load(0, 0)
        load_layer_params(0, "gpre")
        load_consts("early")
        load_layer_params(0, "win")
        load_consts("mid")
        pre_a(0, 0)
        load_consts("rest")
        P.mute_ops = True
        bank_save0 = bank_rr[0]
        load_layer_params(0, "prep")
        bank_rr[0] = bank_save0
        P.mute_ops = False
        load_layer_params(0, "wout")
        pre_b(0, 0)
        P.mute_dma = True
        prep_sel[0] = "cw"
        load_layer_params(0, "prep")
        prep_sel[0] = None
        P.mute_dma = False
        load_layer_params(0, "diag")
        P.op("dve", lambda e: e.tensor_tensor(out=wm[:], in0=spw[:], in1=tril[:, :].unsqueeze(1).to_broadcast([128, 4, 128]),
                                              op=ALU.mult), reads=("spw", "tril"), writes=("wm",))
        wm_done[0] = True

        def nxt(l, b):
            if b + 1 < NBLK:
                return (l, b + 1)
            if l + 1 < DEPTH:
                return (l + 1, 0)
            return None

        def prv(l, b):
            if b > 0:
                return (l, b - 1)
            if l > 0:
                return (l - 1, NBLK - 1)
            return None

        for l in range(DEPTH):
            for b in range(NBLK):
                n1 = nxt(l, b)
                p1 = prv(l, b)
                if n1 is not None:
                    if n1[1] == 0:
                        load_layer_params(n1[0], "gpre")
                    pre_load(*n1)
                if 2 <= b <= 6:
                    P.mute_ops = True
                    load_layer_params(l, ("state", b - 2, "a"))
                    P.mute_ops = False
                last = (b == NBLK - 1 and l + 1 < DEPTH)
                bg = (l + 1 < DEPTH)

                def tick(l=l, bg=bg):
                    if bg:
                        wc_tick(l + 1)

                def early(l=l, b=b, n1=n1, tick=tick):
                    if l == 0 and b == 0:
                        P.mute_dma = True
                        prep_sel[0] = "rest"
                        load_layer_params(0, "prep")
                        prep_sel[0] = None
                        P.mute_dma = False
                    if b == 3 and l + 1 < DEPTH:
                        P.mute_dma = True
                        load_layer_params(l + 1, "prep")
                        P.mute_dma = False
                    if n1 is not None:
                        pre_a(*n1)
                    if 3 <= b <= 7:
                        load_layer_params(l, ("state", b - 3, "b"))
                    tick()

                def conv_done(l=l, b=b, last=last, tick=tick):
                    if 2 <= b <= 6:
                        P.mute_dma = True
                        load_layer_params(l, ("state", b - 2, "a"))
                        P.mute_dma = False
                    tick()

                def out_hook(tl, l=l, b=b, p1=p1, tick=tick):
                    if p1 is not None:
                        ntl = len(blk_info(p1[1])["tiles"])
                        if ntl == 1:
                            if tl == 1:
                                stage_out(p1[0], p1[1], 0)
                        elif tl < ntl:
                            stage_out(p1[0], p1[1], tl)
                        if tl == 1 and p1[0] != l:
                            load_layer_params(l, "wout")
                    tick()
                if b == 0 and l > 0:
                    load_layer_params(l, "diag")
                stage_proj(l, b,
                           hook_out=out_hook,
                           hook_pre_b=(lambda n1=n1: pre_b(*n1)) if n1 is not None else None,
                           hook_early=early,
                           hook_conv_done=conv_done)
                if b == 1 and l + 1 < DEPTH:
                    P.mute_ops = True
                    bank_save = bank_rr[0]
                    load_layer_params(l + 1, "prep")
                    bank_rr[0] = bank_save
                    P.mute_ops = False
        stage_out(DEPTH - 1, NBLK - 1)

        with nc.allow_non_contiguous_dma(reason="tiny parameter columns / strided state rows"):
            P.emit(nc, es)
    return nc


_NC_CACHE = {}


def kernel(x_prompt, x_sample, state_conv_a, state_conv_b, state_pool, pre_norm_g, w_in, conv_a_w, conv_a_b,
           ln_a_g, ln_a_b, conv_b_w, ln_c_g, ln_c_b, spatial_w, spatial_b, pool_w, pool_scale, w_out, post_norm_g):
    n = 8
    f = lambda a: np.ascontiguousarray(np.asarray(a, dtype=np.float32))
    consts = _consts()
    shared = {
        "pre_g": f(pre_norm_g), "w_in": f(w_in), "conv_a_w": f(conv_a_w), "conv_a_b": f(conv_a_b),
        "ln_a_g": f(ln_a_g), "ln_a_b": f(ln_a_b), "conv_b_w": f(conv_b_w), "ln_c_g": f(ln_c_g), "ln_c_b": f(ln_c_b),
        "spatial_w": f(spatial_w), "spatial_b": f(spatial_b), "pool_w": f(pool_w), "pool_scale": f(pool_scale),
        "w_out": f(w_out), "post_g": f(post_norm_g),
    }
    for k, v in consts.items():
        shared["c_" + k] = f(v)
    x_prompt, x_sample = f(x_prompt), f(x_sample)
    sa, sbb, spl = f(state_conv_a), f(state_conv_b), f(state_pool)
    in_maps = []
    for i in range(n):
        m = dict(shared)
        s = slice(NSEQ_S * i, NSEQ_S * (i + 1))
        m["xp"] = x_prompt[i]
        m["xs"] = x_sample[s].reshape(128, D)
        m["sa"] = np.ascontiguousarray(sa[:, s].reshape(DEPTH, NSEQ_S * 30, G))
        m["sb"] = np.ascontiguousarray(sbb[:, s].reshape(DEPTH, NSEQ_S * 2, G))
        m["spl"] = np.ascontiguousarray(spl[:, s].reshape(DEPTH, NSEQ_S * 15, G))
        in_maps.append(m)
    if "nc" not in _NC_CACHE:
        _NC_CACHE["nc"] = build_nc()
    res = run_bass_kernel_spmd(_NC_CACHE["nc"], in_maps, core_ids=list(range(n)))
    R = res.results
    yp = np.stack([R[i]["yp"] for i in range(n)], 0)
    ys = np.concatenate([R[i]["ys"].reshape(NSEQ_S, TS, D) for i in range(n)], 0)
    ca_p = np.stack([R[i]["ca_p"] for i in range(n)], 1)
    ca_s = np.concatenate([R[i]["ca_s"] for i in range(n)], 1)
    cb_p = np.stack([R[i]["cb_p"] for i in range(n)], 1)
    cb_s = np.concatenate([R[i]["cb_s"] for i in range(n)], 1)
    pd_p = np.stack([R[i]["pd_p"] for i in range(n)], 1)
    pd_s = np.concatenate([R[i]["pd_s"] for i in range(n)], 1)
    cv_s = np.concatenate([R[i]["cv_s"].reshape(DEPTH, NSEQ_S, TS, G) for i in range(n)], 1)
    return tuple(np.ascontiguousarray(a.astype(np.float32)) for a in (yp, ys, ca_p, ca_s, cb_p, cb_s, pd_p, pd_s, cv_s))
```
